# Optimizing a Trainium2 kernel written in Bass

```python
import math
import jax, jax.numpy as jnp
from jax import lax
import numpy as np

D_MODEL = 4096
BATCH = 4
SEQ = 2048
DEPTH = 2

HEAD_DIM = 128
ROPE_THETA = 10000.0
RMS_EPS = 1e-6
D_FF = 11008
Q_BLOCK = 128
NEG = -1e30
BIG = 1e9

FOX_HEADS = 16
DIFF_HEADS = 8
DIFF_V_DIM = 2 * HEAD_DIM
IN0_WIDTH = 3 * FOX_HEADS * HEAD_DIM + FOX_HEADS + 2 * DIFF_HEADS * 2 * HEAD_DIM + DIFF_HEADS * DIFF_V_DIM
MIX0_OUT = FOX_HEADS * HEAD_DIM + DIFF_HEADS * DIFF_V_DIM

NSA_HEADS = 32
NSA_KV_HEADS = 4
NSA_GROUP = NSA_HEADS // NSA_KV_HEADS
CMP_BLOCK = 32
CMP_STRIDE = 16
CMP_HIDDEN = 256
SEL_BLOCK = 64
N_SELECT = 16
WINDOW = 512
SEL_Q_CHUNK = 64
N_BRANCH = 3
IN1_WIDTH = NSA_HEADS * HEAD_DIM + 6 * NSA_KV_HEADS * HEAD_DIM + N_BRANCH * NSA_HEADS
MIX1_OUT = NSA_HEADS * HEAD_DIM

N_EVEN = (DEPTH + 1) // 2
N_ODD = DEPTH // 2

kernel_name = "hybrid_fox_diff_nsa_macaron"


def rms_norm(x, g):
    xf = x.astype(jnp.float32)
    y = xf * lax.rsqrt(jnp.mean(xf * xf, axis=-1, keepdims=True) + RMS_EPS)
    return (y * g.astype(jnp.float32)).astype(x.dtype)


def swiglu(x, w_gate, w_up, w_down):
    return (jax.nn.silu(x @ w_gate) * (x @ w_up)) @ w_down


def rope(x, pos):
    half = x.shape[-1] // 2
    inv = ROPE_THETA ** (-jnp.arange(half, dtype=jnp.float32) / half)
    ang = pos.astype(jnp.float32)[:, None] * inv[None, :]
    shape = (1, x.shape[1]) + (1,) * (x.ndim - 3) + (half,)
    cos = jnp.cos(ang).reshape(shape)
    sin = jnp.sin(ang).reshape(shape)
    x1 = x[..., :half].astype(jnp.float32)
    x2 = x[..., half:].astype(jnp.float32)
    return jnp.concatenate([x1 * cos - x2 * sin, x2 * cos + x1 * sin], axis=-1).astype(x.dtype)


def split_cols(proj, sizes):
    return jnp.split(proj, np.cumsum(sizes)[:-1].tolist(), axis=-1)


def fox_diff_mixer(h, w_in, b_forget, lq1, lk1, lq2, lk2, subln_g, w_out, lam_init):
    B, T, _ = h.shape
    proj = h @ w_in
    fq, fk, fv, fgate, dq, dk, dv = split_cols(
        proj, [FOX_HEADS * HEAD_DIM] * 3 + [FOX_HEADS] + [DIFF_HEADS * 2 * HEAD_DIM] * 2 + [DIFF_HEADS * DIFF_V_DIM])
    pos = jnp.arange(T)
    fq = fq.reshape(B, T, FOX_HEADS, HEAD_DIM)
    fk = fk.reshape(B, T, FOX_HEADS, HEAD_DIM)
    fv = fv.reshape(B, T, FOX_HEADS, HEAD_DIM)
    log_f = jax.nn.log_sigmoid(fgate.astype(jnp.float32) + b_forget.astype(jnp.float32))
    cum = jnp.transpose(jnp.cumsum(log_f, axis=1), (0, 2, 1))
    dq = rope(dq.reshape(B, T, DIFF_HEADS, 2, HEAD_DIM), pos)
    dk = rope(dk.reshape(B, T, DIFF_HEADS, 2, HEAD_DIM), pos)
    dv = dv.reshape(B, T, DIFF_HEADS, DIFF_V_DIM)
    lam = (jnp.exp(jnp.sum(lq1.astype(jnp.float32) * lk1.astype(jnp.float32)))
           - jnp.exp(jnp.sum(lq2.astype(jnp.float32) * lk2.astype(jnp.float32))) + lam_init)
    scale = HEAD_DIM ** -0.5

    def block(i):
        start = i * Q_BLOCK
        tq = start + jnp.arange(Q_BLOCK)
        causal = tq[:, None] >= pos[None, :]
        q = lax.dynamic_slice_in_dim(fq, start, Q_BLOCK, axis=1)
        cq = lax.dynamic_slice_in_dim(cum, start, Q_BLOCK, axis=2)
        s = jnp.einsum('bqhd,bshd->bhqs', q, fk).astype(jnp.float32) * scale
        s = s + (cq[..., :, None] - cum[..., None, :])
        p = jax.nn.softmax(jnp.where(causal, s, NEG), axis=-1)
        o_fox = jnp.einsum('bhqs,bshd->bqhd', p.astype(fv.dtype), fv)
        q2 = lax.dynamic_slice_in_dim(dq, start, Q_BLOCK, axis=1)
        s2 = jnp.einsum('bqhcd,bshcd->bhcqs', q2, dk).astype(jnp.float32) * scale
        p2 = jax.nn.softmax(jnp.where(causal, s2, NEG), axis=-1)
        a = p2[:, :, 0] - lam * p2[:, :, 1]
        o_diff = jnp.einsum('bhqs,bshe->bqhe', a.astype(dv.dtype), dv)
        return o_fox, o_diff

    o_fox, o_diff = lax.map(block, jnp.arange(T // Q_BLOCK))
    o_fox = jnp.moveaxis(o_fox, 0, 1).reshape(B, T, FOX_HEADS * HEAD_DIM)
    o_diff = jnp.moveaxis(o_diff, 0, 1).reshape(B, T, DIFF_HEADS, DIFF_V_DIM)
    o_diff = rms_norm(o_diff, subln_g) * (1.0 - lam_init)
    out = jnp.concatenate([o_fox, o_diff.reshape(B, T, DIFF_HEADS * DIFF_V_DIM)], axis=-1)
    return out @ w_out


def nsa_mixer(h, w_in, pe_k, k_w1, k_b1, k_w2, pe_v, v_w1, v_b1, v_w2, w_out):
    B, T, _ = h.shape
    G, R, d = NSA_KV_HEADS, NSA_GROUP, HEAD_DIM
    proj = h @ w_in
    q, kc, vc, ks, vs, kw, vw, gates = split_cols(proj, [NSA_HEADS * d] + [G * d] * 6 + [N_BRANCH * NSA_HEADS])
    q = q.reshape(B, T, G, R, d)
    kc, vc, ks, vs, kw, vw = [a.reshape(B, T, G, d) for a in (kc, vc, ks, vs, kw, vw)]
    gates = jax.nn.sigmoid(gates.astype(jnp.float32)).reshape(B, T, G, R, N_BRANCH).astype(h.dtype)
    pos = jnp.arange(T)
    q_rot = rope(q, pos)
    ks = rope(ks, pos)
    kw = rope(kw, pos)
    scale = d ** -0.5

    n_cmp = (T - CMP_BLOCK) // CMP_STRIDE + 1
    cmp_idx = np.arange(n_cmp)[:, None] * CMP_STRIDE + np.arange(CMP_BLOCK)[None, :]

    def compress(a, pe, w1, b1, w2):
        blk = a[:, cmp_idx] + pe[:, None, :]
        blk = jnp.moveaxis(blk, 3, 2).reshape(B, n_cmp, G, CMP_BLOCK * d)
        return jax.nn.silu(blk @ w1 + b1) @ w2

    k_cmp = compress(kc, pe_k, k_w1, k_b1, k_w2)
    v_cmp = compress(vc, pe_v, v_w1, v_b1, v_w2)
    cmp_mask = pos[:, None] >= jnp.asarray(cmp_idx[:, -1])[None, :]
    s = jnp.einsum('btgrd,bngd->bgrtn', q, k_cmp).astype(jnp.float32) * scale
    p_cmp = jax.nn.softmax(jnp.where(cmp_mask, s, NEG), axis=-1) * cmp_mask
    o_cmp = jnp.einsum('bgrtn,bngd->btgrd', p_cmp.astype(v_cmp.dtype), v_cmp)

    n_sel = T // SEL_BLOCK
    n_top = min(N_SELECT, n_sel)
    cmp_start = cmp_idx[:, 0]
    sel_start = np.arange(n_sel) * SEL_BLOCK
    overlap = ((cmp_start[:, None] < sel_start[None, :] + SEL_BLOCK)
               & (cmp_start[:, None] + CMP_BLOCK > sel_start[None, :])).astype(np.float32)
    imp = jnp.einsum('bgrtn,nj->bgtj', p_cmp, jnp.asarray(overlap))
    blk_t = pos // SEL_BLOCK
    j = jnp.arange(n_sel)
    valid = j[None, :] <= blk_t[:, None]
    forced = (j[None, :] == 0) | (j[None, :] == blk_t[:, None]) | (j[None, :] == blk_t[:, None] - 1)
    score = jnp.where(valid, jnp.where(forced, BIG, imp), -BIG)
    top_val, top_idx = lax.top_k(score, n_top)
    top_ok = top_val > -BIG / 2
    ks_blk = jnp.transpose(ks.reshape(B, n_sel, SEL_BLOCK, G, d), (0, 3, 1, 2, 4))
    vs_blk = jnp.transpose(vs.reshape(B, n_sel, SEL_BLOCK, G, d), (0, 3, 1, 2, 4))
    bi = jnp.arange(B)[:, None, None, None]
    gi = jnp.arange(G)[None, :, None, None]

    def sel_chunk(i):
        start = i * SEL_Q_CHUNK
        tq = start + jnp.arange(SEL_Q_CHUNK)
        qc = lax.dynamic_slice_in_dim(q_rot, start, SEL_Q_CHUNK, axis=1)
        idx = lax.dynamic_slice_in_dim(top_idx, start, SEL_Q_CHUNK, axis=2)
        ok = lax.dynamic_slice_in_dim(top_ok, start, SEL_Q_CHUNK, axis=2)
        kg = ks_blk[bi, gi, idx]
        vg = vs_blk[bi, gi, idx]
        key_pos = idx[..., None] * SEL_BLOCK + jnp.arange(SEL_BLOCK)
        mask = ok[..., None] & (key_pos <= tq[None, None, :, None, None])
        s = jnp.einsum('bqgrd,bgqkld->bgrqkl', qc, kg).astype(jnp.float32) * scale
        s = jnp.where(mask[:, :, None], s, NEG).reshape(B, G, R, SEL_Q_CHUNK, n_top * SEL_BLOCK)
        p = jax.nn.softmax(s, axis=-1).reshape(B, G, R, SEL_Q_CHUNK, n_top, SEL_BLOCK)
        return jnp.einsum('bgrqkl,bgqkld->bqgrd', p.astype(vg.dtype), vg)

    o_sel = lax.map(sel_chunk, jnp.arange(T // SEL_Q_CHUNK))
    o_sel = jnp.moveaxis(o_sel, 0, 1).reshape(B, T, G, R, d)

    kw_pad = jnp.pad(kw, ((0, 0), (WINDOW, 0), (0, 0), (0, 0)))
    vw_pad = jnp.pad(vw, ((0, 0), (WINDOW, 0), (0, 0), (0, 0)))
    span = WINDOW + Q_BLOCK

    def win_block(i):
        start = i * Q_BLOCK
        tq = start + jnp.arange(Q_BLOCK)
        kpos = start - WINDOW + jnp.arange(span)
        qb = lax.dynamic_slice_in_dim(q_rot, start, Q_BLOCK, axis=1)
        kb = lax.dynamic_slice_in_dim(kw_pad, start, span, axis=1)
        vb = lax.dynamic_slice_in_dim(vw_pad, start, span, axis=1)
        dist = tq[:, None] - kpos[None, :]
        mask = (kpos[None, :] >= 0) & (dist >= 0) & (dist < WINDOW)
        s = jnp.einsum('bqgrd,bsgd->bgrqs', qb, kb).astype(jnp.float32) * scale
        p = jax.nn.softmax(jnp.where(mask, s, NEG), axis=-1)
        return jnp.einsum('bgrqs,bsgd->bqgrd', p.astype(vb.dtype), vb)

    o_win = lax.map(win_block, jnp.arange(T // Q_BLOCK))
    o_win = jnp.moveaxis(o_win, 0, 1).reshape(B, T, G, R, d)

    out = gates[..., 0:1] * o_cmp + gates[..., 1:2] * o_sel + gates[..., 2:3] * o_win
    return out.reshape(B, T, MIX1_OUT) @ w_out


def setup_inputs(seed: int = 0) -> dict:
    key = jax.random.key(seed)
    keys = iter(jax.random.split(key, 32))

    def nrm(shape, scale):
        return jax.random.normal(next(keys), shape, jnp.float32) * scale

    def gain(shape):
        return 1.0 + 0.02 * jax.random.normal(next(keys), shape, jnp.float32)

    D = D_MODEL
    return {
        "x": nrm((BATCH, SEQ, D), 1.0),
        "ffn1_norm": gain((DEPTH, D)),
        "ffn1_w_gate": nrm((DEPTH, D, D_FF), D ** -0.5),
        "ffn1_w_up": nrm((DEPTH, D, D_FF), D ** -0.5),
        "ffn1_w_down": nrm((DEPTH, D_FF, D), D_FF ** -0.5),
        "mix_norm": gain((DEPTH, D)),
        "ffn2_norm": gain((DEPTH, D)),
        "ffn2_w_gate": nrm((DEPTH, D, D_FF), D ** -0.5),
        "ffn2_w_up": nrm((DEPTH, D, D_FF), D ** -0.5),
        "ffn2_w_down": nrm((DEPTH, D_FF, D), D_FF ** -0.5),
        "even_w_in": nrm((N_EVEN, D, IN0_WIDTH), D ** -0.5),
        "even_b_forget": nrm((N_EVEN, FOX_HEADS), 0.1),
        "even_lambda_q1": nrm((N_EVEN, HEAD_DIM), 0.1),
        "even_lambda_k1": nrm((N_EVEN, HEAD_DIM), 0.1),
        "even_lambda_q2": nrm((N_EVEN, HEAD_DIM), 0.1),
        "even_lambda_k2": nrm((N_EVEN, HEAD_DIM), 0.1),
        "even_subln": gain((N_EVEN, DIFF_V_DIM)),
        "even_w_out": nrm((N_EVEN, MIX0_OUT, D), MIX0_OUT ** -0.5),
        "odd_w_in": nrm((N_ODD, D, IN1_WIDTH), D ** -0.5),
        "odd_cmp_pe_k": nrm((N_ODD, CMP_BLOCK, HEAD_DIM), 0.02),
        "odd_cmp_k_w1": nrm((N_ODD, CMP_BLOCK * HEAD_DIM, CMP_HIDDEN), (CMP_BLOCK * HEAD_DIM) ** -0.5),
        "odd_cmp_k_b1": nrm((N_ODD, CMP_HIDDEN), 0.01),
        "odd_cmp_k_w2": nrm((N_ODD, CMP_HIDDEN, HEAD_DIM), CMP_HIDDEN ** -0.5),
        "odd_cmp_pe_v": nrm((N_ODD, CMP_BLOCK, HEAD_DIM), 0.02),
        "odd_cmp_v_w1": nrm((N_ODD, CMP_BLOCK * HEAD_DIM, CMP_HIDDEN), (CMP_BLOCK * HEAD_DIM) ** -0.5),
        "odd_cmp_v_b1": nrm((N_ODD, CMP_HIDDEN), 0.01),
        "odd_cmp_v_w2": nrm((N_ODD, CMP_HIDDEN, HEAD_DIM), CMP_HIDDEN ** -0.5),
        "odd_w_out": nrm((N_ODD, MIX1_OUT, D), MIX1_OUT ** -0.5),
        "final_norm": gain((D,)),
    }


def reference(x, ffn1_norm, ffn1_w_gate, ffn1_w_up, ffn1_w_down, mix_norm,
              ffn2_norm, ffn2_w_gate, ffn2_w_up, ffn2_w_down,
              even_w_in, even_b_forget, even_lambda_q1, even_lambda_k1, even_lambda_q2, even_lambda_k2,
              even_subln, even_w_out,
              odd_w_in, odd_cmp_pe_k, odd_cmp_k_w1, odd_cmp_k_b1, odd_cmp_k_w2,
              odd_cmp_pe_v, odd_cmp_v_w1, odd_cmp_v_b1, odd_cmp_v_w2, odd_w_out,
              final_norm):
    h = x
    for layer in range(DEPTH):
        h = h + 0.5 * swiglu(rms_norm(h, ffn1_norm[layer]), ffn1_w_gate[layer], ffn1_w_up[layer], ffn1_w_down[layer])
        hn = rms_norm(h, mix_norm[layer])
        i = layer // 2
        if layer % 2 == 0:
            lam_init = 0.8 - 0.6 * math.exp(-0.3 * layer)
            h = h + fox_diff_mixer(hn, even_w_in[i], even_b_forget[i], even_lambda_q1[i], even_lambda_k1[i],
                                   even_lambda_q2[i], even_lambda_k2[i], even_subln[i], even_w_out[i], lam_init)
        else:
            h = h + nsa_mixer(hn, odd_w_in[i], odd_cmp_pe_k[i], odd_cmp_k_w1[i], odd_cmp_k_b1[i], odd_cmp_k_w2[i],
                              odd_cmp_pe_v[i], odd_cmp_v_w1[i], odd_cmp_v_b1[i], odd_cmp_v_w2[i], odd_w_out[i])
        h = h + 0.5 * swiglu(rms_norm(h, ffn2_norm[layer]), ffn2_w_gate[layer], ffn2_w_up[layer], ffn2_w_down[layer])
    return rms_norm(h, final_norm)
```

```python
import math
import numpy as np
import concourse.bass as bass
import concourse.mybir as mybir
from concourse.bass_utils import run_bass_kernel_spmd
from contextlib import ExitStack

F32 = mybir.dt.float32
BF16 = mybir.dt.bfloat16
I32 = mybir.dt.int32
AF = mybir.ActivationFunctionType
ALU = mybir.AluOpType
AX = mybir.AxisListType

D = 4096
DC = D // 128
DFF = 11008
NFF = DFF // 128
NCORES = 8
RMS_EPS = 1e-6


class Buf:
    __slots__ = ("name", "w", "r", "dsem", "dcnt")

    def __init__(self, name):
        self.name = name
        self.w = {}
        self.r = {}
        self.dsem = None
        self.dcnt = 0


class K:
    def __init__(self, nc, es):
        self.nc = nc
        self.es = es
        self.engs = {"pe": nc.tensor, "dve": nc.vector, "act": nc.scalar,
                     "pool": nc.gpsimd, "sp": nc.sync}
        self.sem = {k: es.enter_context(nc.semaphore("s_" + k))
                    for k in ["pe", "dve", "act", "pool"]}
        self.cnt = {k: 0 for k in self.sem}
        self.waited = {k: {} for k in self.engs}
        self.nsem = 0
        self.nwait = 0
        self.nins = 0
        self.pend = {k: [] for k in self.engs}
        self.es_outer = es
        self.pfx = ""
        self.dsems = []

    def sb(self, name, shape, dt):
        return self.es.enter_context(self.nc.sbuf_tensor(self.pfx + name, shape, dt))

    def ps(self, name, shape, dt=F32):
        return self.es.enter_context(self.nc.psum_tensor(self.pfx + name, shape, dt))

    def newsem(self, name):
        self.nsem += 1
        return self.es_outer.enter_context(self.nc.semaphore(self.pfx + name))

    def rotate(self):
        self.barrier()
        self.nrot = getattr(self, "nrot", 0) + 1
        for e in list(self.sem):
            self.sem[e] = self.es_outer.enter_context(self.nc.semaphore(f"{self.pfx}s_{e}_r{self.nrot}"))
            self.cnt[e] = 0

    def barrier(self):
        evs = [(self.sem[e], self.cnt[e]) for e in self.sem if self.cnt[e] > 0]
        evs += [(b.dsem, b.dcnt) for b in self.dsems if b.dcnt > 0]
        for eng in self.engs:
            assert not self.pend[eng]
            wd = self.waited[eng]
            for (sm, v) in evs:
                if wd.get(sm.num, -1) >= v:
                    continue
                self.engs[eng].wait_ge(sm, v)
                self.nwait += 1
                wd[sm.num] = v

    def _deps(self, eng, reads, writes):
        need = {}

        def add(ev, kind):
            s, v = ev
            key = s.num
            if eng in self.sem and s.num == self.sem[eng].num:
                if eng == "pe" or kind != "raw":
                    return
            if key not in need or need[key][1] < v:
                need[key] = (s, v)

        for b in reads:
            for ev in b.w.values():
                add(ev, "raw")
        for b in writes:
            for ev in b.w.values():
                add(ev, "waw")
            for ev in b.r.values():
                add(ev, "war")
        e = self.engs[eng]
        wd = self.waited[eng]
        for key, (s, v) in need.items():
            if wd.get(key, -1) >= v:
                continue
            e.wait_ge(s, v)
            self.nwait += 1
            wd[key] = v

    def _record(self, ev, reads, writes, pwrites=()):
        key = ev[0].num
        for b in reads:
            b.r[key] = ev
        for b in writes:
            b.w = {key: ev}
            b.r = {}
        for b in pwrites:
            b.w[key] = ev

    def op(self, eng, fn, reads=(), writes=(), pwrites=(), inc=True):
        self._deps(eng, reads, list(writes) + list(pwrites))
        ins = fn(self.engs[eng])
        self.nins += 1
        if not inc:
            self.pend[eng].append((list(reads), list(writes), list(pwrites)))
            return ins
        self.cnt[eng] += 1
        ins.then_inc(self.sem[eng], 1)
        ev = (self.sem[eng], self.cnt[eng])
        for (r_, w_, pw_) in self.pend[eng]:
            self._record(ev, r_, w_, pw_)
        self.pend[eng] = []
        self._record(ev, reads, writes, pwrites)
        return ins

    def dma(self, q, out, in_, reads=(), writes=(), pwrites=(), owner=None, **kw):
        allw = list(writes) + list(pwrites)
        self._deps(q, reads, allw)
        if owner is None:
            owner = allw[0] if allw else reads[0]
        if owner.dsem is None:
            owner.dsem = self.newsem("d_" + owner.name)
            self.dsems.append(owner)
        ins = self.engs[q].dma_start(out=out, in_=in_, **kw)
        self.nins += 1
        owner.dcnt += 16
        ins.then_inc(owner.dsem, 16)
        ev = (owner.dsem, owner.dcnt)
        self._record(ev, reads, writes, pwrites)
        return ins

    def finish(self, bufs, eng="sp"):
        self._deps(eng, bufs, bufs)


def _cast_dma(k, dst, src, ncols, **kw):
    k.dma("pool", dst, src, max_dma_last_dim=2048, **kw)


def build_ffn(NT=1024, nff=NFF):
    nc = bass.Bass("TRN2", target_bir_lowering=False)
    x = nc.dram_tensor("x", [NT, D], F32, kind="ExternalInput").ap()
    gT = nc.dram_tensor("gT", [128, DC], F32, kind="ExternalInput").ap()
    ident = nc.dram_tensor("ident", [128, 128], F32, kind="ExternalInput").ap()
    wg = nc.dram_tensor("wg", [nff, 128, D], F32, kind="ExternalInput").ap()
    wu = nc.dram_tensor("wu", [nff, 128, D], F32, kind="ExternalInput").ap()
    wd = nc.dram_tensor("wd", [nff * 128, D], F32, kind="ExternalInput").ap()
    y = nc.dram_tensor("y", [NT, D], F32, kind="ExternalOutput").ap()
    with ExitStack() as es:
        k = K(nc, es)
        emit_ffn(k, x, gT, ident, wg, wu, wd, y, NT, nff)
    return nc


def emit_ffn(k, x, gT, ident, wg, wu, wd, y, NT=1024, nff=NFF):
    TB = 512
    NB = NT // TB
    with ExitStack() as ph:
        k.es = ph
        gs = k.sb("gs", [128, DC], F32); Bg = Buf("gs")
        ids = k.sb("ids", [128, 128], F32); Bid = Buf("ids")
        xs = [k.sb("xs0", [128, D], F32)] * 2
        Bxs = [Buf("xs0")] * 2
        st = [k.sb(f"st{i}", [128, 4], F32) for i in range(2)]
        Bst = [Buf(f"st{i}") for i in range(2)]
        junk = k.sb("junk", [128, D], BF16); Bjunk = Buf("junk")
        xnT = k.sb("xnT", [128, DC, TB], BF16); BxnT = Buf("xnT")
        hT = k.sb("hT", [128, nff, TB], BF16); BhT = Buf("hT")
        wgb = [k.sb(f"wgb{i}", [128, D], BF16) for i in range(2)]
        wub = [k.sb(f"wub{i}", [128, D], BF16) for i in range(2)]
        Bwg = [Buf(f"wgb{i}") for i in range(2)]
        Bwu = [Buf(f"wub{i}") for i in range(2)]
        wdb = [k.sb(f"wdb{i}", [128, 8, 512], BF16) for i in range(2)]
        Bwd = [Buf(f"wdb{i}") for i in range(2)]
        sg = [k.sb(f"sg{i}", [128, TB], F32) for i in range(2)]
        Bsg = [Buf(f"sg{i}") for i in range(2)]
        xr = [k.sb(f"xr{i}", [128, 512], F32) for i in range(2)]
        Bxr = [Buf(f"xr{i}") for i in range(2)]
        yo = [k.sb(f"yo{i}", [128, 512], F32) for i in range(2)]
        Byo = [Buf(f"yo{i}") for i in range(2)]
        pt = [k.ps(f"pt{i}", [128, 512]) for i in range(8)]
        Bp = [Buf(f"pt{i}") for i in range(8)]

        k.dma("sp", gs[:], gT, writes=[Bg])
        k.dma("sp", ids[:], ident, writes=[Bid])

        nxt = 0
        ev2 = 0
        for tb in range(NB):
            t0 = tb * TB
            for tt in range(4):
                s = tt % 2
                r0 = t0 + tt * 128
                k.dma("sp", xs[s][:], x[r0:r0 + 128, :], writes=[Bxs[s]])
                k.op("act", lambda e: e.activation(out=junk[:], in_=xs[s][:], func=AF.Square,
                                                   accum_out=st[s][:, 0:1]),
                     reads=[Bxs[s]], writes=[Bjunk, Bst[s]])
                k.op("dve", lambda e: e.tensor_scalar(out=st[s][:, 1:2], in0=st[s][:, 0:1],
                                                      scalar1=1.0 / D, scalar2=RMS_EPS,
                                                      op0=ALU.mult, op1=ALU.add),
                     reads=[Bst[s]], pwrites=[Bst[s]])
                k.op("act", lambda e: e.activation(out=st[s][:, 2:3], in_=st[s][:, 1:2], func=AF.Sqrt),
                     reads=[Bst[s]], pwrites=[Bst[s]])
                k.op("dve", lambda e: e.reciprocal(out=st[s][:, 3:4], in_=st[s][:, 2:3]),
                     reads=[Bst[s]], pwrites=[Bst[s]])
                k.op("dve", lambda e: e.tensor_scalar(out=xs[s][:], in0=xs[s][:], scalar1=st[s][:, 3:4],
                                                      scalar2=None, op0=ALU.mult),
                     reads=[Bst[s], Bxs[s]], pwrites=[Bxs[s]])
                for c4 in range(DC // 4):
                    p = nxt % 8; nxt += 1
                    for j in range(4):
                        c = c4 * 4 + j
                        k.op("pe", lambda e: e.transpose(out=pt[p][:, j * 128:(j + 1) * 128],
                                                         in_=xs[s][:, c * 128:(c + 1) * 128],
                                                         identity=ids[:]),
                             reads=[Bxs[s], Bid], writes=[Bp[p]] if j == 0 else [],
                             pwrites=[] if j == 0 else [Bp[p]], inc=(j == 3))
                    for j in range(4):
                        c = c4 * 4 + j
                        eng = "dve" if j % 2 == 0 else "act"
                        if eng == "dve":
                            k.op("dve", lambda e: e.tensor_scalar(
                                out=xnT[:, c, tt * 128:(tt + 1) * 128], in0=pt[p][:, j * 128:(j + 1) * 128],
                                scalar1=gs[:, c:c + 1], scalar2=None, op0=ALU.mult),
                                reads=[Bp[p], Bg], pwrites=[BxnT])
                        else:
                            k.op("act", lambda e: e.activation(
                                out=xnT[:, c, tt * 128:(tt + 1) * 128], in_=pt[p][:, j * 128:(j + 1) * 128],
                                func=AF.Copy, scale=gs[:, c:c + 1]),
                                reads=[Bp[p], Bg], pwrites=[BxnT])
            for f in range(nff):
                s = f % 2
                _cast_dma(k, wgb[s][:], wg[f], D, writes=[Bwg[s]])
                _cast_dma(k, wub[s][:], wu[f], D, writes=[Bwu[s]])
                pg, pu = pt[2 * s], pt[2 * s + 1]
                for c in range(DC):
                    k.op("pe", lambda e: e.matmul(pg[:], lhsT=wgb[s][:, c * 128:(c + 1) * 128],
                                                  rhs=xnT[:, c, :], start=(c == 0), stop=(c == DC - 1)),
                         reads=[Bwg[s], BxnT], writes=[Bp[2 * s]] if c == 0 else [],
                         pwrites=[] if c == 0 else [Bp[2 * s]], inc=(c == DC - 1))
                for c in range(DC):
                    k.op("pe", lambda e: e.matmul(pu[:], lhsT=wub[s][:, c * 128:(c + 1) * 128],
                                                  rhs=xnT[:, c, :], start=(c == 0), stop=(c == DC - 1)),
                         reads=[Bwu[s], BxnT], writes=[Bp[2 * s + 1]] if c == 0 else [],
                         pwrites=[] if c == 0 else [Bp[2 * s + 1]], inc=(c == DC - 1))
                k.op("act", lambda e: e.activation(out=sg[s][:], in_=pg[:], func=AF.Silu),
                     reads=[Bp[2 * s]], writes=[Bsg[s]])
                k.op("dve", lambda e: e.tensor_tensor(out=hT[:, f, :], in0=sg[s][:], in1=pu[:], op=ALU.mult),
                     reads=[Bsg[s], Bp[2 * s + 1]], pwrites=[BhT])
            ngrp = (nff + 7) // 8
            li = 0
            for db in range(D // 512):
                ps0 = 4 * (db % 2)
                for g in range(ngrp):
                    s = li % 2; li += 1
                    nk = min(8, nff - g * 8)
                    _cast_dma(k, wdb[s][:, 0:nk, :],
                              wd[g * 1024:g * 1024 + nk * 128, db * 512:(db + 1) * 512]
                              .rearrange("(k p) n -> p k n", p=128),
                              512, writes=[Bwd[s]])
                    for kk in range(nk):
                        f = g * 8 + kk
                        for tt in range(4):
                            first = (f == 0)
                            k.op("pe", lambda e: e.matmul(pt[ps0 + tt][:], lhsT=hT[:, f, tt * 128:(tt + 1) * 128],
                                                          rhs=wdb[s][:, kk, :], start=first, stop=(f == nff - 1)),
                                 reads=[BhT, Bwd[s]], writes=[Bp[ps0 + tt]] if first else [],
                                 pwrites=[] if first else [Bp[ps0 + tt]],
                                 inc=(f == nff - 1) or (kk == nk - 1 and tt == 3))
                for tt in range(4):
                    s2 = ev2 % 2; ev2 += 1
                    r0 = t0 + tt * 128
                    k.dma("sp", xr[s2][:], x[r0:r0 + 128, db * 512:(db + 1) * 512], writes=[Bxr[s2]])
                    k.op("dve", lambda e: e.scalar_tensor_tensor(
                        out=yo[s2][:], in0=pt[ps0 + tt][:], scalar=0.5, in1=xr[s2][:],
                        op0=ALU.mult, op1=ALU.add),
                        reads=[Bp[ps0 + tt], Bxr[s2]], writes=[Byo[s2]])
                    k.dma("sp", y[r0:r0 + 128, db * 512:(db + 1) * 512], yo[s2][:], reads=[Byo[s2]], owner=Byo[s2])
        k.barrier()
        print(f"[ffn] ins={k.nins} waits={k.nwait} sems={k.nsem} cnt={k.cnt}")


def _ffn_weight_layout(w):
    nff = w.shape[1] // 128
    return np.ascontiguousarray(w.reshape(DC, 128, nff, 128).transpose(2, 1, 0, 3)).reshape(nff, 128, D)


def _gT(g):
    return np.ascontiguousarray(g.reshape(DC, 128).T)


def build_linear(ncols, norm, residual, NT=1024):
    nc = bass.Bass("TRN2", target_bir_lowering=False)
    x = nc.dram_tensor("x", [NT, D], F32, kind="ExternalInput").ap()
    gT = nc.dram_tensor("gT", [128, DC], F32, kind="ExternalInput").ap()
    ident = nc.dram_tensor("ident", [128, 128], F32, kind="ExternalInput").ap()
    w = nc.dram_tensor("w", [D, ncols], F32, kind="ExternalInput").ap()
    res = nc.dram_tensor("res", [NT, ncols], F32, kind="ExternalInput").ap() if residual else None
    y = nc.dram_tensor("y", [NT, ncols], F32, kind="ExternalOutput").ap()
    with ExitStack() as es:
        k = K(nc, es)
        emit_linear(k, x, gT, ident, w, res, y, ncols, norm, NT)
    return nc


def emit_linear(k, x, gT, ident, w, res, y, ncols, norm, NT=1024):
    NTT = NT // 128
    residual = res is not None
    with ExitStack() as ph:
        k.es = ph
        gs = k.sb("gs", [128, DC], F32); Bg = Buf("gs")
        ids = k.sb("ids", [128, 128], F32); Bid = Buf("ids")
        xs = [k.sb(f"xs{i}", [128, D], F32) for i in range(2)]
        Bxs = [Buf(f"xs{i}") for i in range(2)]
        st = [k.sb(f"st{i}", [128, 4], F32) for i in range(2)]
        Bst = [Buf(f"st{i}") for i in range(2)]
        junk = k.sb("junk", [128, D], BF16); Bjunk = Buf("junk")
        xT = k.sb("xT", [128, DC, NT], BF16); BxT = Buf("xT")
        wb = [k.sb(f"wb{i}", [128, DC, 512], BF16) for i in range(2)]
        Bw = [Buf(f"wb{i}") for i in range(2)]
        rs = [k.sb(f"rs{i}", [128, 512], F32) for i in range(2)]
        Brs = [Buf(f"rs{i}") for i in range(2)]
        yo = [k.sb(f"yo{i}", [128, 512], F32) for i in range(2)]
        Byo = [Buf(f"yo{i}") for i in range(2)]
        pt = [k.ps(f"pt{i}", [128, 512]) for i in range(8)]
        Bp = [Buf(f"pt{i}") for i in range(8)]
        k.dma("sp", gs[:], gT, writes=[Bg])
        k.dma("sp", ids[:], ident, writes=[Bid])
        nxt = 0
        for tt in range(NTT):
            s = tt % 2
            k.dma("sp", xs[s][:], x[tt * 128:(tt + 1) * 128, :], writes=[Bxs[s]])
            if norm:
                k.op("act", lambda e: e.activation(out=junk[:], in_=xs[s][:], func=AF.Square,
                                                   accum_out=st[s][:, 0:1]),
                     reads=[Bxs[s]], writes=[Bjunk, Bst[s]])
                k.op("dve", lambda e: e.tensor_scalar(out=st[s][:, 1:2], in0=st[s][:, 0:1],
                                                      scalar1=1.0 / D, scalar2=RMS_EPS,
                                                      op0=ALU.mult, op1=ALU.add),
                     reads=[Bst[s]], pwrites=[Bst[s]])
                k.op("act", lambda e: e.activation(out=st[s][:, 2:3], in_=st[s][:, 1:2], func=AF.Sqrt),
                     reads=[Bst[s]], pwrites=[Bst[s]])
                k.op("dve", lambda e: e.reciprocal(out=st[s][:, 3:4], in_=st[s][:, 2:3]),
                     reads=[Bst[s]], pwrites=[Bst[s]])
                k.op("dve", lambda e: e.tensor_scalar(out=xs[s][:], in0=xs[s][:], scalar1=st[s][:, 3:4],
                                                      scalar2=None, op0=ALU.mult),
                     reads=[Bst[s], Bxs[s]], pwrites=[Bxs[s]])
            for c4 in range(DC // 4):
                p = nxt % 8; nxt += 1
                for j in range(4):
                    c = c4 * 4 + j
                    k.op("pe", lambda e: e.transpose(out=pt[p][:, j * 128:(j + 1) * 128],
                                                     in_=xs[s][:, c * 128:(c + 1) * 128], identity=ids[:]),
                         reads=[Bxs[s], Bid], writes=[Bp[p]] if j == 0 else [],
                         pwrites=[] if j == 0 else [Bp[p]], inc=(j == 3))
                for j in range(4):
                    c = c4 * 4 + j
                    if j % 2 == 0:
                        k.op("dve", lambda e: e.tensor_scalar(
                            out=xT[:, c, tt * 128:(tt + 1) * 128], in0=pt[p][:, j * 128:(j + 1) * 128],
                            scalar1=gs[:, c:c + 1], scalar2=None, op0=ALU.mult),
                            reads=[Bp[p], Bg], pwrites=[BxT])
                    else:
                        k.op("act", lambda e: e.activation(
                            out=xT[:, c, tt * 128:(tt + 1) * 128], in_=pt[p][:, j * 128:(j + 1) * 128],
                            func=AF.Copy, scale=gs[:, c:c + 1]),
                            reads=[Bp[p], Bg], pwrites=[BxT])
        ncb = (ncols + 511) // 512
        ev = 0
        for cb in range(ncb):
            c0 = cb * 512
            cw = min(512, ncols - c0)
            s = cb % 2
            for hf in range(2):
                _cast_dma(k, wb[s][:, hf * 16:(hf + 1) * 16, 0:cw],
                          w[hf * 2048:(hf + 1) * 2048, c0:c0 + cw].rearrange("(c p) n -> p c n", p=128),
                          cw, writes=[Bw[s]] if hf == 0 else [], pwrites=[] if hf == 0 else [Bw[s]])
            for tt in range(NTT):
                p = nxt % 8; nxt += 1
                for c in range(DC):
                    k.op("pe", lambda e: e.matmul(pt[p][:, 0:cw], lhsT=xT[:, c, tt * 128:(tt + 1) * 128],
                                                  rhs=wb[s][:, c, 0:cw], start=(c == 0), stop=(c == DC - 1)),
                         reads=[BxT, Bw[s]], writes=[Bp[p]] if c == 0 else [],
                         pwrites=[] if c == 0 else [Bp[p]], inc=(c == DC - 1))
                s2 = ev % 2; ev += 1
                if residual:
                    k.dma("sp", rs[s2][:, 0:cw], res[tt * 128:(tt + 1) * 128, c0:c0 + cw], writes=[Brs[s2]])
                    k.op("dve", lambda e: e.tensor_tensor(out=yo[s2][:, 0:cw], in0=pt[p][:, 0:cw],
                                                          in1=rs[s2][:, 0:cw], op=ALU.add),
                         reads=[Bp[p], Brs[s2]], writes=[Byo[s2]])
                else:
                    if ev % 2 == 0:
                        k.op("dve", lambda e: e.tensor_copy(out=yo[s2][:, 0:cw], in_=pt[p][:, 0:cw]),
                             reads=[Bp[p]], writes=[Byo[s2]])
                    else:
                        k.op("act", lambda e: e.copy(out=yo[s2][:, 0:cw], in_=pt[p][:, 0:cw]),
                             reads=[Bp[p]], writes=[Byo[s2]])
                k.dma("sp", y[tt * 128:(tt + 1) * 128, c0:c0 + cw], yo[s2][:, 0:cw],
                      reads=[Byo[s2]], owner=Byo[s2])
        k.barrier()
        print(f"[linear {ncols} n={norm} r={residual}] ins={k.nins} waits={k.nwait} sems={k.nsem} cnt={k.cnt}")


def build_fnorm(NT=1024):
    nc = bass.Bass("TRN2", target_bir_lowering=False)
    x = nc.dram_tensor("x", [NT, D], F32, kind="ExternalInput").ap()
    gR = nc.dram_tensor("gR", [128, D], F32, kind="ExternalInput").ap()
    y = nc.dram_tensor("y", [NT, D], F32, kind="ExternalOutput").ap()
    with ExitStack() as es:
        k = K(nc, es)
        emit_fnorm(k, x, gR, y, NT)
    return nc


def emit_fnorm(k, x, gR, y, NT=1024):
    with ExitStack() as ph:
        k.es = ph
        gs = k.sb("gs", [128, D], F32); Bg = Buf("gs")
        xs = [k.sb(f"xs{i}", [128, D], F32) for i in range(2)]
        Bxs = [Buf(f"xs{i}") for i in range(2)]
        ys = [k.sb(f"ys{i}", [128, D], F32) for i in range(2)]
        Bys = [Buf(f"ys{i}") for i in range(2)]
        st = [k.sb(f"st{i}", [128, 4], F32) for i in range(2)]
        Bst = [Buf(f"st{i}") for i in range(2)]
        junk = k.sb("junk", [128, D], BF16); Bjunk = Buf("junk")
        k.dma("sp", gs[:], gR, writes=[Bg])
        for tt in range(NT // 128):
            s = tt % 2
            k.dma("sp", xs[s][:], x[tt * 128:(tt + 1) * 128, :], writes=[Bxs[s]])
            k.op("act", lambda e: e.activation(out=junk[:], in_=xs[s][:], func=AF.Square,
                                               accum_out=st[s][:, 0:1]),
                 reads=[Bxs[s]], writes=[Bjunk, Bst[s]])
            k.op("dve", lambda e: e.tensor_scalar(out=st[s][:, 1:2], in0=st[s][:, 0:1],
                                                  scalar1=1.0 / D, scalar2=RMS_EPS, op0=ALU.mult, op1=ALU.add),
                 reads=[Bst[s]], pwrites=[Bst[s]])
            k.op("act", lambda e: e.activation(out=st[s][:, 2:3], in_=st[s][:, 1:2], func=AF.Sqrt),
                 reads=[Bst[s]], pwrites=[Bst[s]])
            k.op("dve", lambda e: e.reciprocal(out=st[s][:, 3:4], in_=st[s][:, 2:3]),
                 reads=[Bst[s]], pwrites=[Bst[s]])
            k.op("dve", lambda e: e.scalar_tensor_tensor(out=ys[s][:], in0=xs[s][:], scalar=st[s][:, 3:4],
                                                         in1=gs[:], op0=ALU.mult, op1=ALU.mult),
                 reads=[Bst[s], Bxs[s], Bg], writes=[Bys[s]])
            k.dma("sp", y[tt * 128:(tt + 1) * 128, :], ys[s][:], reads=[Bys[s]], owner=Bys[s])
        k.barrier()


def build_chain(stages, NT=1024, nff=NFF):
    nc = bass.Bass("TRN2", target_bir_lowering=False)
    ein = lambda n, sh: nc.dram_tensor(n, sh, F32, kind="ExternalInput").ap()
    eout = lambda n, sh: nc.dram_tensor(n, sh, F32, kind="ExternalOutput").ap()
    ident = ein("ident", [128, 128])
    cur = ein("x0", [NT, D])
    t = []
    for i, st in enumerate(stages):
        if st[0] == "ffn":
            t.append(dict(gT=ein(f"s{i}_gT", [128, DC]), wg=ein(f"s{i}_wg", [nff, 128, D]),
                          wu=ein(f"s{i}_wu", [nff, 128, D]), wd=ein(f"s{i}_wd", [nff * 128, D]),
                          y=eout(f"s{i}_y", [NT, D])))
        elif st[0] == "lin":
            ncols = st[1]
            t.append(dict(gT=ein(f"s{i}_gT", [128, DC]), w=ein(f"s{i}_w", [D, ncols]),
                          res=ein(f"s{i}_res", [NT, ncols]) if st[3] else None,
                          y=eout(f"s{i}_y", [NT, ncols])))
        else:
            t.append(dict(gR=ein(f"s{i}_gR", [128, D]), y=eout(f"s{i}_y", [NT, D])))
    with ExitStack() as es:
        k = K(nc, es)
        for i, st in enumerate(stages):
            k.pfx = f"s{i}_"
            d = t[i]
            if st[0] == "ffn":
                emit_ffn(k, cur, d["gT"], ident, d["wg"], d["wu"], d["wd"], d["y"], NT, nff)
            elif st[0] == "lin":
                emit_linear(k, cur, d["gT"], ident, d["w"], d["res"], d["y"], st[1], st[2], NT)
            else:
                emit_fnorm(k, cur, d["gR"], d["y"], NT)
            cur = d["y"]
    return nc


T = 2048
NTT = T // 128
HD = 128
SCALE = HD ** -0.5
MNEG = -float(2 ** 20)


class Att:
    def __init__(self, k):
        self.k = k
        self.ps = [k.ps(f"pss{i}", [128, 512]) for i in range(2)]
        self.Bps = [Buf(f"pss{i}") for i in range(2)]
        self.po = [k.ps(f"po{i}", [128, 512]) for i in range(4)]
        self.Bpo = [Buf(f"po{i}") for i in range(4)]
        self.PT = [k.sb(f"PT{i}", [128, 512], BF16) for i in range(2)]
        self.BPT = [Buf(f"PT{i}") for i in range(2)]
        self.n = 0

    def run(self, *, kT, BkT, qT, BqT, v1, Bv1, VW, plan, idb, Bidb, masks, Bmasks,
            out_cb, KP=128, bias_k=None, Bbias=None, pre=None, sel=None, rhs_all=False):
        k = self.k
        for g in range(4):
            tiles = plan[g]
            first = {}; last = {}
            for (kt, mi, qlo, qhi) in tiles:
                for qt in range(qlo, qhi + 1):
                    first.setdefault(qt, kt); last[qt] = kt
            for (kt, mi, qlo, qhi) in tiles:
                s = self.n % 2; self.n += 1
                ps, Bps = self.ps[s], self.Bps[s]
                c0, c1 = qlo * 128, (qhi + 1) * 128
                nmm = 1 + (mi is not None) + (sel is not None)
                i = 0
                k.op("pe", lambda e: e.matmul(ps[0:KP, c0:c1], lhsT=kT[:, kt * 128:kt * 128 + KP],
                                              rhs=qT[:, g * 512 + c0:g * 512 + c1], start=True, stop=(nmm == 1)),
                     reads=[BkT, BqT], writes=[Bps])
                i += 1
                if mi is not None:
                    k.op("pe", lambda e: e.matmul(ps[0:KP, c0:c1], lhsT=idb[:, 0:KP],
                                                  rhs=masks[:, mi, g * 512 + c0:g * 512 + c1] if rhs_all else masks[:, mi, c0:c1],
                                                  start=False, stop=(i == nmm - 1)),
                         reads=[Bidb, Bmasks], pwrites=[Bps])
                    i += 1
                if sel is not None:
                    esel, selT, Bsel = sel
                    k.op("pe", lambda e: e.matmul(ps[0:KP, c0:c1], lhsT=esel[:, kt, :],
                                                  rhs=selT[:, g * 512 + c0:g * 512 + c1], start=False, stop=True),
                         reads=[Bsel], pwrites=[Bps])
                PT, BPT = self.PT[s], self.BPT[s]
                if pre is not None:
                    tmp, Btmp, qb, Bqb = pre
                    k.op("dve", lambda e: e.scalar_tensor_tensor(
                        out=tmp[s][:, c0:c1], in0=ps[:, c0:c1], scalar=SCALE, in1=qb[:, g * 512 + c0:g * 512 + c1],
                        op0=ALU.mult, op1=ALU.add), reads=[Bps, Bqb], writes=[Btmp[s]])
                    k.op("act", lambda e: e.activation(out=PT[:, c0:c1], in_=tmp[s][:, c0:c1], func=AF.Exp,
                                                       bias=bias_k[:, kt:kt + 1], scale=1.0),
                         reads=[Btmp[s], Bbias], writes=[BPT])
                else:
                    k.op("act", lambda e: e.activation(out=PT[0:KP, c0:c1], in_=ps[0:KP, c0:c1], func=AF.Exp,
                                                       scale=SCALE),
                         reads=[Bps], writes=[BPT])
                for qt in range(qlo, qhi + 1):
                    st_, sp_ = (first[qt] == kt), (last[qt] == kt)
                    k.op("pe", lambda e: e.matmul(self.po[qt][:, 0:VW], lhsT=PT[0:KP, qt * 128:(qt + 1) * 128],
                                                  rhs=v1[0:KP, kt, 0:VW], start=st_, stop=sp_),
                         reads=[BPT, Bv1], writes=[self.Bpo[qt]] if st_ else [],
                         pwrites=[] if st_ else [self.Bpo[qt]])
            for qt in range(4):
                out_cb(g, qt, self.po[qt], self.Bpo[qt])


def causal_plan():
    plan = []
    for g in range(4):
        tl = [(kt, None, 0, 3) for kt in range(4 * g)]
        tl += [(4 * g + r, r, r, 3) for r in range(4)]
        plan.append(tl)
    return plan


def window_plan():
    plan = []
    for g in range(4):
        tl = []
        if g > 0:
            tl += [(4 * g - 4 + r, 4 + r, 0, r) for r in range(4)]
        tl += [(4 * g + r, r, r, 3) for r in range(4)]
        plan.append(tl)
    return plan


def _rope(k, dst, Bdst, src, srcsw, C, S, Bcs, xa, Bxa, xb, Bxb, ctr):
    for hf in range(2):
        s = ctr[0] % 2; ctr[0] += 1
        cs = slice(hf * 1024, (hf + 1) * 1024)
        k.dma("sp", xa[s][:], src[:, cs], writes=[Bxa[s]])
        k.dma("sp", xb[s][:], srcsw[:, cs], writes=[Bxb[s]])
        k.op("dve", lambda e: e.tensor_tensor(out=xa[s][:], in0=xa[s][:], in1=C[:, cs], op=ALU.mult),
             reads=[Bxa[s], Bcs], pwrites=[Bxa[s]])
        k.op("pool", lambda e: e.tensor_tensor(out=xb[s][:], in0=xb[s][:], in1=S[:, cs], op=ALU.mult),
             reads=[Bxb[s], Bcs], pwrites=[Bxb[s]])
        k.op("dve", lambda e: e.tensor_tensor(out=dst[:, cs], in0=xa[s][:], in1=xb[s][:], op=ALU.add),
             reads=[Bxa[s], Bxb[s]], writes=[Bdst] if hf == 0 else [], pwrites=[] if hf == 0 else [Bdst])


def build_att_even(lam_init, nfox=8, ndiff=4):
    nc = bass.Bass("TRN2", target_bir_lowering=False)
    dt_ = lambda n, sh: nc.dram_tensor(n, sh, F32, kind="ExternalInput").ap()
    fqT = dt_("fqT", [8, 128, T]); fkT = dt_("fkT", [8, 128, T]); fv = dt_("fv", [8, T, 128])
    fg = dt_("fg", [T, 8]); bfr = dt_("bfr", [128, 128])
    dqT = dt_("dqT", [8, 128, T]); dqTs = dt_("dqTs", [8, 128, T])
    dkT = dt_("dkT", [8, 128, T]); dkTs = dt_("dkTs", [8, 128, T]); dv = dt_("dv", [4, T, 256])
    ropeC = dt_("ropeC", [128, T]); ropeS = dt_("ropeS", [128, T])
    maskc = dt_("maskc", [128, 4 * 512]); ident = dt_("ident", [128, 128])
    triu = dt_("triu", [128, 128]); selh = dt_("selh", [8, 8 * 128])
    lamv = dt_("lamv", [128, 4 * 128]); subg = dt_("subg", [128, 256])
    o = nc.dram_tensor("o", [T, 2048], F32, kind="ExternalOutput").ap()
    with ExitStack() as es:
        k = K(nc, es)
        att = Att(k)
        pm = [k.ps(f"pm{i}", [128, 512]) for i in range(2)]
        Bpm = [Buf(f"pm{i}") for i in range(2)]
        idf = k.sb("idf", [128, 128], F32); Bidf = Buf("idf")
        idb = k.sb("idb", [128, 128], BF16); Bidb = Buf("idb")
        tri = k.sb("tri", [128, 128], F32); Btri = Buf("tri")
        onesf = k.sb("onesf", [128, 128], F32); Bones = Buf("onesf")
        mk = k.sb("mk", [128, 4, 512], BF16); Bmk = Buf("mk")
        C = k.sb("C", [128, T], F32); S = k.sb("S", [128, T], F32); Bcs = Buf("cs")
        sh = k.sb("sh", [8, 8, 128], F32); Bsh = Buf("sh")
        k.dma("sp", idf[:], ident, writes=[Bidf])
        _cast_dma(k, idb[:], ident, 128, writes=[Bidb])
        k.dma("sp", tri[:], triu, writes=[Btri])
        k.op("dve", lambda e: e.memset(onesf[:], 1.0), writes=[Bones])
        _cast_dma(k, mk[:], maskc.rearrange("p (m n) -> p m n", m=4), 512, writes=[Bmk])
        k.dma("sp", C[:], ropeC, writes=[Bcs])
        k.dma("sp", S[:], ropeS, pwrites=[Bcs])
        k.dma("sp", sh[:], selh.rearrange("k (h m) -> k h m", h=8), writes=[Bsh])
        lv = k.sb("lv", [128, 4, 128], F32); Blv = Buf("lv")
        lt = k.sb("lt", [128, 2, 128], F32); Blt = Buf("lt")
        ls = k.sb("ls", [128, 8], F32); Bls = Buf("ls")
        k.dma("sp", lv[:], lamv.rearrange("p (a n) -> p a n", a=4), writes=[Blv])
        for i in range(2):
            k.op("dve", lambda e: e.tensor_tensor(out=lt[:, i, :], in0=lv[:, 2 * i, :], in1=lv[:, 2 * i + 1, :],
                                                  op=ALU.mult), reads=[Blv], pwrites=[Blt])
        k.op("dve", lambda e: e.tensor_reduce(out=ls[:, 0:2], in_=lt[:], axis=AX.X, op=ALU.add),
             reads=[Blt], writes=[Bls])
        k.op("act", lambda e: e.activation(out=ls[:, 2:4], in_=ls[:, 0:2], func=AF.Exp), reads=[Bls], pwrites=[Bls])
        k.op("dve", lambda e: e.scalar_tensor_tensor(out=ls[:, 4:5], in0=ls[:, 3:4], scalar=-float(lam_init),
                                                     in1=ls[:, 2:3], op0=ALU.add, op1=ALU.subtract),
             reads=[Bls], pwrites=[Bls])
        sg_ = k.sb("subgs", [128, 256], F32); Bsg_ = Buf("subg")
        k.dma("sp", sg_[:], subg, writes=[Bsg_])
        fgs = k.sb("fgs", [128, NTT, 8], F32); Bfg = Buf("fgs")
        bfs = k.sb("bfs", [128, NTT, 8], F32); Bbf = Buf("bfs")
        cn = k.sb("cn", [128, NTT, 8], F32); Bcn = Buf("cn")
        off = k.sb("off", [128, NTT, 8], F32); Boff = Buf("off")
        cnh = k.sb("cnh", [128, 8, NTT], F32); Bcnh = Buf("cnh")
        cnT = k.sb("cnT", [8, T], F32); BcnT = Buf("cnT")
        k.dma("sp", fgs[:], fg.rearrange("(t p) h -> p t h", p=128), writes=[Bfg])
        k.dma("sp", bfs[:], bfr.rearrange("p (t h) -> p t h", h=8), writes=[Bbf])
        k.op("dve", lambda e: e.tensor_tensor(out=fgs[:], in0=fgs[:], in1=bfs[:], op=ALU.add),
             reads=[Bfg, Bbf], pwrites=[Bfg])
        k.op("act", lambda e: e.activation(out=fgs[:], in_=fgs[:], func=AF.Exp, scale=-1.0), reads=[Bfg], pwrites=[Bfg])
        k.op("act", lambda e: e.activation(out=fgs[:], in_=fgs[:], func=AF.Ln, bias=1.0, scale=1.0),
             reads=[Bfg], pwrites=[Bfg])
        fl = fgs[:].rearrange("p t h -> p (t h)")
        k.op("pe", lambda e: e.matmul(pm[0][:, 0:128], lhsT=tri[:], rhs=fl, start=True, stop=True),
             reads=[Btri, Bfg], writes=[Bpm[0]])
        k.op("pe", lambda e: e.matmul(pm[1][:, 0:128], lhsT=onesf[:], rhs=fl, start=True, stop=True),
             reads=[Bones, Bfg], writes=[Bpm[1]])
        k.op("dve", lambda e: e.memset(off[:, 0, :], 0.0), writes=[Boff])
        k.op("act", lambda e: e.copy(out=cn[:].rearrange("p t h -> p (t h)"), in_=pm[1][:, 0:128]),
             reads=[Bpm[1]], writes=[Bcn])
        for i in range(1, NTT):
            k.op("dve", lambda e: e.tensor_tensor(out=off[:, i, :], in0=off[:, i - 1, :], in1=cn[:, i - 1, :],
                                                  op=ALU.add), reads=[Boff, Bcn], pwrites=[Boff])
        k.op("dve", lambda e: e.tensor_tensor(out=cn[:].rearrange("p t h -> p (t h)"), in0=pm[0][:, 0:128],
                                              in1=off[:].rearrange("p t h -> p (t h)"), op=ALU.add),
             reads=[Bpm[0], Boff], writes=[Bcn])
        k.op("dve", lambda e: e.tensor_copy(out=cnh[:], in_=cn[:].rearrange("p t h -> p h t")),
             reads=[Bcn], writes=[Bcnh])
        for i4 in range(4):
            p = pm[i4 % 2]; Bp_ = Bpm[i4 % 2]
            for j in range(4):
                i = i4 * 4 + j
                k.op("pe", lambda e: e.transpose(out=p[0:8, j * 128:(j + 1) * 128], in_=cn[:, i, :], identity=idf[:]),
                     reads=[Bcn, Bidf], writes=[Bp_] if j == 0 else [], pwrites=[] if j == 0 else [Bp_])
            k.op("act", lambda e: e.mul(out=cnT[:, i4 * 512:(i4 + 1) * 512], in_=p[0:8, :], mul=-1.0),
                 reads=[Bp_], pwrites=[BcnT])
        qb = [k.sb(f"qb{i}", [128, T], BF16) for i in range(2)]; Bqb = [Buf(f"qb{i}") for i in range(2)]
        kb = [k.sb(f"kb{i}", [128, T], BF16) for i in range(2)]; Bkb = [Buf(f"kb{i}") for i in range(2)]
        v1 = [k.sb(f"v1{i}", [128, NTT, 257], BF16) for i in range(2)]; Bv1 = [Buf(f"v1{i}") for i in range(2)]
        qbias = k.sb("qbias", [128, T], F32); Bqbias = Buf("qbias")
        tmp = [k.sb(f"tmp{i}", [128, 512], F32) for i in range(2)]; Btmp = [Buf(f"tmp{i}") for i in range(2)]
        ob = [k.sb(f"ob{i}", [128, NTT, 256], F32) for i in range(2)]; Bob = [Buf(f"ob{i}") for i in range(2)]
        rsb = k.sb("rsb", [128, 64], F32); Brs = Buf("rsb")
        xa = [k.sb(f"xa{i}", [128, 1024], F32) for i in range(2)]; Bxa = [Buf(f"xa{i}") for i in range(2)]
        xb = [k.sb(f"xb{i}", [128, 1024], F32) for i in range(2)]; Bxb = [Buf(f"xb{i}") for i in range(2)]
        junk = k.sb("junk", [128, 256], F32); Bjunk = Buf("junk")
        nst = k.sb("nst", [128, 4, NTT], F32); Bnst = Buf("nst")
        ctr = [0]
        plan = causal_plan()
        rc = [0]

        def norm_cb(obuf, Bobuf, VW):
            def cb(g, qt, po, Bpo):
                qi = 4 * g + qt
                c = rc[0] % 64; rc[0] += 1
                k.op("dve", lambda e: e.reciprocal(out=rsb[:, c:c + 1], in_=po[:, VW:VW + 1]),
                     reads=[Bpo], pwrites=[Brs])
                if qi % 2 == 0:
                    k.op("dve", lambda e: e.tensor_scalar(out=obuf[:, qi, 0:VW], in0=po[:, 0:VW], scalar1=rsb[:, c:c + 1],
                                                          scalar2=None, op0=ALU.mult),
                         reads=[Bpo, Brs], pwrites=[Bobuf])
                else:
                    k.op("act", lambda e: e.activation(out=obuf[:, qi, 0:VW], in_=po[:, 0:VW], func=AF.Copy,
                                                       scale=rsb[:, c:c + 1]),
                         reads=[Bpo, Brs], pwrites=[Bobuf])
            return cb

        for h in range(nfox):
            s = h % 2
            _cast_dma(k, qb[s][:], fqT[h], T, writes=[Bqb[s]])
            _cast_dma(k, kb[s][:], fkT[h], T, writes=[Bkb[s]])
            _cast_dma(k, v1[s][:, :, 0:128], fv[h].rearrange("(t p) d -> p t d", p=128), 128, writes=[Bv1[s]])
            k.op("pool", lambda e: e.memset(v1[s][:, :, 128:129], 1.0), pwrites=[Bv1[s]])
            for g in range(4):
                p = pm[g % 2]; Bp_ = Bpm[g % 2]
                k.op("pe", lambda e: e.matmul(p[:], lhsT=sh[:, h, :], rhs=cnT[:, g * 512:(g + 1) * 512],
                                              start=True, stop=True), reads=[Bsh, BcnT], writes=[Bp_])
                k.op("act", lambda e: e.copy(out=qbias[:, g * 512:(g + 1) * 512], in_=p[:]),
                     reads=[Bp_], writes=[Bqbias] if g == 0 else [], pwrites=[] if g == 0 else [Bqbias])
            k.op("dve", lambda e: e.memset(ob[s][:, 0, 0:1], 0.0), writes=[Bob[s]])
            att.run(kT=kb[s], BkT=Bkb[s], qT=qb[s], BqT=Bqb[s], v1=v1[s], Bv1=Bv1[s], VW=129, plan=plan,
                    idb=idb, Bidb=Bidb, masks=mk, Bmasks=Bmk, out_cb=norm_cb(ob[s], Bob[s], 128),
                    bias_k=cnh[:, h, :], Bbias=Bcnh, pre=(tmp, Btmp, qbias, Bqbias))
            k.dma("sp", o[:, h * 128:(h + 1) * 128].rearrange("(t p) d -> p t d", p=128), ob[s][:, :, 0:128],
                  reads=[Bob[s]], owner=Bob[s])
        for hd in range(ndiff):
            vs_ = hd % 2
            _cast_dma(k, v1[vs_][:, :, 0:256], dv[hd].rearrange("(t p) d -> p t d", p=128), 256, writes=[Bv1[vs_]])
            k.op("pool", lambda e: e.memset(v1[vs_][:, :, 256:257], 1.0), pwrites=[Bv1[vs_]])
            for m in range(2):
                _rope(k, qb[m], Bqb[m], dqT[hd * 2 + m], dqTs[hd * 2 + m], C, S, Bcs, xa, Bxa, xb, Bxb, ctr)
                _rope(k, kb[m], Bkb[m], dkT[hd * 2 + m], dkTs[hd * 2 + m], C, S, Bcs, xa, Bxa, xb, Bxb, ctr)
                k.op("dve", lambda e: e.memset(ob[m][:, 0, 0:1], 0.0), writes=[Bob[m]])
                att.run(kT=kb[m], BkT=Bkb[m], qT=qb[m], BqT=Bqb[m], v1=v1[vs_], Bv1=Bv1[vs_], VW=257, plan=plan,
                        idb=idb, Bidb=Bidb, masks=mk, Bmasks=Bmk, out_cb=norm_cb(ob[m], Bob[m], 256))
            f0 = ob[0][:].rearrange("p t d -> p (t d)"); f1 = ob[1][:].rearrange("p t d -> p (t d)")
            k.op("dve", lambda e: e.scalar_tensor_tensor(out=f0, in0=f1, scalar=ls[:, 4:5], in1=f0,
                                                         op0=ALU.mult, op1=ALU.add),
                 reads=[Bob[1], Bob[0], Bls], writes=[Bob[0]])
            for qi in range(NTT):
                k.op("act", lambda e: e.activation(out=junk[:], in_=ob[0][:, qi, :], func=AF.Square,
                                                   accum_out=nst[:, 0, qi:qi + 1]),
                     reads=[Bob[0]], writes=[Bjunk], pwrites=[Bnst])
            k.op("dve", lambda e: e.tensor_scalar(out=nst[:, 1, :], in0=nst[:, 0, :], scalar1=1.0 / 256, scalar2=RMS_EPS,
                                                  op0=ALU.mult, op1=ALU.add), reads=[Bnst], pwrites=[Bnst])
            k.op("act", lambda e: e.activation(out=nst[:, 2, :], in_=nst[:, 1, :], func=AF.Sqrt), reads=[Bnst], pwrites=[Bnst])
            k.op("dve", lambda e: e.reciprocal(out=nst[:, 3, :], in_=nst[:, 2, :]), reads=[Bnst], pwrites=[Bnst])
            k.op("dve", lambda e: e.tensor_scalar(out=nst[:, 3, :], in0=nst[:, 3, :], scalar1=float(1.0 - lam_init),
                                                  scalar2=None, op0=ALU.mult), reads=[Bnst], pwrites=[Bnst])
            for qi in range(NTT):
                k.op("dve", lambda e: e.scalar_tensor_tensor(out=ob[1][:, qi, :], in0=ob[0][:, qi, :],
                                                             scalar=nst[:, 3, qi:qi + 1], in1=sg_[:],
                                                             op0=ALU.mult, op1=ALU.mult),
                     reads=[Bob[0], Bnst, Bsg_], writes=[Bob[1]] if qi == 0 else [], pwrites=[] if qi == 0 else [Bob[1]])
            k.dma("sp", o[:, 1024 + hd * 256:1024 + (hd + 1) * 256].rearrange("(t p) d -> p t d", p=128), ob[1][:],
                  reads=[Bob[1]], owner=Bob[1])
        k.finish(Bob)
        print(f"[att_even] ins={k.nins} waits={k.nwait} sems={k.nsem}")
    return nc


def _consts():
    c = {}
    c["ident"] = np.eye(128, dtype=np.float32)
    half = 64
    inv = (np.float32(10000.0) ** (-np.arange(half, dtype=np.float32) / np.float32(half))).astype(np.float32)
    ang = np.arange(T, dtype=np.float32)[:, None] * inv[None, :]
    cos = np.cos(ang).astype(np.float32).T; sin = np.sin(ang).astype(np.float32).T
    c["ropeC"] = np.ascontiguousarray(np.concatenate([cos, cos], 0))
    c["ropeS"] = np.ascontiguousarray(np.concatenate([-sin, sin], 0))
    ps = np.arange(128)[:, None]; tq = np.arange(512)[None, :]
    mc = np.stack([np.where(tq >= 128 * r + ps, 0.0, MNEG) for r in range(4)], 1)
    mw = np.stack([np.where(tq < 128 * r + ps, 0.0, MNEG) for r in range(4)], 1)
    c["maskc"] = np.ascontiguousarray(mc.reshape(128, 4 * 512).astype(np.float32))
    c["maskcw"] = np.ascontiguousarray(np.concatenate([mc, mw], 1).reshape(128, 8 * 512).astype(np.float32))
    c["triu"] = np.triu(np.ones((128, 128), np.float32))
    sh = np.zeros((8, 8, 128), np.float32)
    for h in range(8):
        sh[h, h, :] = 1.0
    c["selh"] = sh.reshape(8, 8 * 128)
    return c


def _swap_halves(a, axis):
    return np.concatenate(np.split(a, 2, axis=axis)[::-1], axis=axis)


def _even_inputs(proj_b, hh, p, c):
    fq, fk, fv, fgate, dq, dk, dv = np.split(proj_b, np.cumsum([2048, 2048, 2048, 16, 2048, 2048])[:].tolist(), axis=1)
    hs = slice(hh * 8, hh * 8 + 8); ds_ = slice(hh * 4, hh * 4 + 4)
    tr = lambda a: np.ascontiguousarray(a.reshape(T, 16, 128)[:, hs].transpose(1, 2, 0))
    dqT = dq.reshape(T, 8, 2, 128)[:, ds_].transpose(1, 2, 3, 0).reshape(8, 128, T)
    dkT = dk.reshape(T, 8, 2, 128)[:, ds_].transpose(1, 2, 3, 0).reshape(8, 128, T)
    ca = np.ascontiguousarray
    return {
        "fqT": tr(fq), "fkT": tr(fk), "fv": ca(fv.reshape(T, 16, 128)[:, hs].transpose(1, 0, 2)),
        "fg": ca(fgate[:, hs]), "bfr": ca(np.broadcast_to(np.tile(p["even_b_forget"][0][hs], NTT), (128, 128))),
        "dqT": ca(dqT), "dqTs": ca(_swap_halves(dqT, 1)), "dkT": ca(dkT), "dkTs": ca(_swap_halves(dkT, 1)),
        "dv": ca(dv.reshape(T, 8, 256)[:, ds_].transpose(1, 0, 2)),
        "ropeC": c["ropeC"], "ropeS": c["ropeS"], "maskc": c["maskc"], "ident": c["ident"],
        "triu": c["triu"], "selh": c["selh"],
        "lamv": ca(np.broadcast_to(np.concatenate([p["even_lambda_q1"][0], p["even_lambda_k1"][0],
                                                   p["even_lambda_q2"][0], p["even_lambda_k2"][0]]), (128, 512))),
        "subg": ca(np.broadcast_to(p["even_subln"][0], (128, 256))),
    }


NCMP = 127
NSA_NG = 2


def build_att_nsa(ngroups=2, nheads=8):
    nc = bass.Bass("TRN2", target_bir_lowering=False)
    dt_ = lambda n, sh: nc.dram_tensor(n, sh, F32, kind="ExternalInput").ap()
    NG = ngroups
    qT = dt_("qT", [8 * NG, 128, T]); qTs = dt_("qTs", [8 * NG, 128, T])
    kcb = dt_("kcb", [NG, 128, 32 * NCMP]); vcb = dt_("vcb", [NG, 128, 32 * NCMP])
    ksT = dt_("ksT", [NG, 128, T]); ksTs = dt_("ksTs", [NG, 128, T])
    kwT = dt_("kwT", [NG, 128, T]); kwTs = dt_("kwTs", [NG, 128, T])
    vs = dt_("vs", [NG, T, 128]); vw = dt_("vw", [NG, T, 128])
    gpre = dt_("gpre", [T, 24 * NG])
    w1k = dt_("w1k", [128, 32 * 256]); w1v = dt_("w1v", [128, 32 * 256])
    b1 = dt_("b1", [128, 4]); w2k = dt_("w2k", [128, 256]); w2v = dt_("w2v", [128, 256])
    pek = dt_("pek", [128, 32]); pev = dt_("pev", [128, 32])
    ropeC = dt_("ropeC", [128, T]); ropeS = dt_("ropeS", [128, T])
    maskcw = dt_("maskcw", [128, 8 * 512]); ident = dt_("ident", [128, 128])
    cmask = dt_("cmask", [128, T]); ovl = dt_("ovl", [128, 33])
    fconst = dt_("fconst", [128, NTT * 32]); esel = dt_("esel", [33, NTT * 128])
    o = nc.dram_tensor("o", [T, 1024 * NG], F32, kind="ExternalOutput").ap()
    with ExitStack() as es:
        k = K(nc, es)
        att = Att(k)
        pm = [k.ps(f"pm{i}", [128, 512]) for i in range(2)]
        Bpm = [Buf(f"pm{i}") for i in range(2)]
        idf = k.sb("idf", [128, 128], F32); Bidf = Buf("idf")
        idb = k.sb("idb", [128, 128], BF16); Bidb = Buf("idb")
        mk = k.sb("mk", [128, 8, 512], BF16); Bmk = Buf("mk")
        cmb = k.sb("cmb", [128, 1, T], BF16); Bcmb = Buf("cmb")
        C = k.sb("C", [128, T], F32); S = k.sb("S", [128, T], F32); Bcs = Buf("cs")
        fc = k.sb("fc", [128, NTT * 32], F32); Bfc = Buf("fc")
        eselb = k.sb("eselb", [33, NTT, 128], BF16); selT1 = k.sb("selT1", [33, T], BF16); Bsel = Buf("sel")
        vco = k.sb("vco", [128, 1, 161], BF16); Bvco = Buf("vco")
        gp = k.sb("gp", [128, NTT, 24 * NG], F32); Bgp = Buf("gp")
        b1s = k.sb("b1s", [128, 4], F32); Bb1 = Buf("b1s")
        pes = k.sb("pes", [128, 2, 32], F32); Bpe = Buf("pes")
        w2b = k.sb("w2b", [128, 2, 2, 128], BF16); Bw2 = Buf("w2b")
        k.dma("sp", idf[:], ident, writes=[Bidf])
        _cast_dma(k, idb[:], ident, 128, writes=[Bidb])
        _cast_dma(k, mk[:], maskcw.rearrange("p (m n) -> p m n", m=8), 512, writes=[Bmk])
        _cast_dma(k, cmb[:, 0, :], cmask, T, writes=[Bcmb])
        k.dma("sp", C[:], ropeC, writes=[Bcs])
        k.dma("sp", S[:], ropeS, pwrites=[Bcs])
        k.dma("sp", fc[:], fconst, writes=[Bfc])
        _cast_dma(k, eselb[:], esel.rearrange("j (t s) -> j t s", t=NTT), 128, writes=[Bsel])
        k.op("pool", lambda e: e.memset(selT1[32:33, :], 1.0), pwrites=[Bsel])
        _cast_dma(k, vco[:, 0, 128:161], ovl, 33, writes=[Bvco])
        k.dma("sp", gp[:], gpre.rearrange("(t p) c -> p t c", p=128), writes=[Bgp])
        k.op("act", lambda e: e.activation(out=gp[:], in_=gp[:], func=AF.Sigmoid), reads=[Bgp], pwrites=[Bgp])
        k.dma("sp", b1s[:], b1, writes=[Bb1])
        k.dma("sp", pes[:, 0, :], pek, writes=[Bpe])
        k.dma("sp", pes[:, 1, :], pev, pwrites=[Bpe])
        _cast_dma(k, w2b[:, 0, :, :], w2k.rearrange("p (c d) -> p c d", c=2), 128, writes=[Bw2])
        _cast_dma(k, w2b[:, 1, :, :], w2v.rearrange("p (c d) -> p c d", c=2), 128, pwrites=[Bw2])
        w1b = k.sb("w1b", [128, 32, 256], BF16); Bw1 = Buf("w1b")
        blk32 = [k.sb(f"blk32{i}", [128, 8, NCMP], F32) for i in range(2)]; Bblk32 = [Buf(f"blk32{i}") for i in range(2)]
        blkb = k.sb("blkb", [128, 32, NCMP], BF16); Bblkb = Buf("blkb")
        H1 = k.sb("H1", [128, 2, NCMP], BF16); BH1 = Buf("H1")
        kcm = k.sb("kcm", [128, 128], BF16); Bkcm = Buf("kcm")
        ksr = k.sb("ksr", [128, T], BF16); Bksr = Buf("ksr")
        kwr = k.sb("kwr", [128, T], BF16); Bkwr = Buf("kwr")
        v1s = k.sb("v1s", [128, NTT, 129], BF16); Bv1s = Buf("v1s")
        v1w = k.sb("v1w", [128, NTT, 129], BF16); Bv1w = Buf("v1w")
        qb = k.sb("qb", [128, T], BF16); Bqb = Buf("qb")
        qr = k.sb("qr", [128, T], BF16); Bqr = Buf("qr")
        xa = [k.sb(f"xa{i}", [128, 1024], F32) for i in range(2)]; Bxa = [Buf(f"xa{i}") for i in range(2)]
        xb = [k.sb(f"xb{i}", [128, 1024], F32) for i in range(2)]; Bxb = [Buf(f"xb{i}") for i in range(2)]
        acc = [k.sb(f"acc{i}", [128, NTT, 128], F32) for i in range(2)]; Bacc = [Buf(f"acc{i}") for i in range(2)]
        imp = k.sb("imp", [128, NTT, 32], F32); Bimp = Buf("imp")
        sc = k.sb("sc", [128, NTT, 32], F32); Bsc = Buf("sc")
        selm = k.sb("selm", [128, NTT, 32], F32); Bselm = Buf("selm")
        okm = k.sb("okm", [128, NTT * 32], F32); Bokm = Buf("okm")
        m8 = k.sb("m8", [128, 16], F32); Bm8 = Buf("m8")
        wk = k.sb("wk", [128, 32], F32); Bwk = Buf("wk")
        rsb = k.sb("rsb", [128, 128], F32); Brs = Buf("rsb")
        ctr = [0]; rc = [0]; bc = [0]
        cplan = [[(0, 0, 0, 3)] for _ in range(4)]
        caus = causal_plan(); wplan = window_plan()

        def compress(which, src, gi):
            _cast_dma(k, w1b[:], (w1k if which == 0 else w1v).rearrange("p (l h) -> p l h", l=32), 256, writes=[Bw1])
            for l8 in range(4):
                s = bc[0] % 2; bc[0] += 1
                k.dma("sp", blk32[s][:], src[gi][:, l8 * 8 * NCMP:(l8 + 1) * 8 * NCMP].rearrange("p (l n) -> p l n", l=8),
                      writes=[Bblk32[s]])
                for j in range(8):
                    l = l8 * 8 + j
                    k.op("dve", lambda e: e.tensor_scalar(out=blkb[:, l, :], in0=blk32[s][:, j, :],
                                                          scalar1=pes[:, which, l:l + 1], scalar2=None, op0=ALU.add),
                         reads=[Bblk32[s], Bpe], writes=[Bblkb] if l == 0 else [], pwrites=[] if l == 0 else [Bblkb])
            for hc in range(2):
                p = pm[hc]; Bp_ = Bpm[hc]
                for l in range(32):
                    k.op("pe", lambda e: e.matmul(p[:, 0:NCMP], lhsT=w1b[:, l, hc * 128:(hc + 1) * 128], rhs=blkb[:, l, :],
                                                  start=(l == 0), stop=(l == 31)),
                         reads=[Bw1, Bblkb], writes=[Bp_] if l == 0 else [], pwrites=[] if l == 0 else [Bp_])
                k.op("act", lambda e: e.activation(out=H1[:, hc, :], in_=p[:, 0:NCMP], func=AF.Silu,
                                                   bias=b1s[:, which * 2 + hc:which * 2 + hc + 1], scale=1.0),
                     reads=[Bp_, Bb1], writes=[BH1] if hc == 0 else [], pwrites=[] if hc == 0 else [BH1])
            p = pm[0]; Bp_ = Bpm[0]
            if which == 0:
                for hc in range(2):
                    k.op("pe", lambda e: e.matmul(p[:, 0:NCMP], lhsT=w2b[:, 0, hc, :], rhs=H1[:, hc, :],
                                                  start=(hc == 0), stop=(hc == 1)),
                         reads=[Bw2, BH1], writes=[Bp_] if hc == 0 else [], pwrites=[] if hc == 0 else [Bp_])
                k.op("act", lambda e: e.copy(out=kcm[:, 0:NCMP], in_=p[:, 0:NCMP]), reads=[Bp_], writes=[Bkcm])
            else:
                for hc in range(2):
                    k.op("pe", lambda e: e.matmul(p[0:NCMP, 0:128], lhsT=H1[:, hc, :], rhs=w2b[:, 1, hc, :],
                                                  start=(hc == 0), stop=(hc == 1)),
                         reads=[Bw2, BH1], writes=[Bp_] if hc == 0 else [], pwrites=[] if hc == 0 else [Bp_])
                k.op("act", lambda e: e.copy(out=vco[0:NCMP, 0, 0:128], in_=p[0:NCMP, 0:128]), reads=[Bp_], pwrites=[Bvco])

        def rs_of(po, Bpo, col, eps):
            c = rc[0] % 64; rc[0] += 1
            if eps:
                k.op("dve", lambda e: e.tensor_scalar(out=rsb[:, 64 + c:65 + c], in0=po[:, col:col + 1], scalar1=1e-30,
                                                      scalar2=None, op0=ALU.add), reads=[Bpo], pwrites=[Brs])
                k.op("dve", lambda e: e.reciprocal(out=rsb[:, c:c + 1], in_=rsb[:, 64 + c:65 + c]), reads=[Brs], pwrites=[Brs])
            else:
                k.op("dve", lambda e: e.reciprocal(out=rsb[:, c:c + 1], in_=po[:, col:col + 1]), reads=[Bpo], pwrites=[Brs])
            return c

        for gi in range(ngroups):
            if gi > 0:
                k.rotate()
            compress(0, kcb, gi)
            compress(1, vcb, gi)
            _rope(k, ksr, Bksr, ksT[gi], ksTs[gi], C, S, Bcs, xa, Bxa, xb, Bxb, ctr)
            _rope(k, kwr, Bkwr, kwT[gi], kwTs[gi], C, S, Bcs, xa, Bxa, xb, Bxb, ctr)
            _cast_dma(k, v1s[:, :, 0:128], vs[gi].rearrange("(t p) d -> p t d", p=128), 128, writes=[Bv1s])
            k.op("pool", lambda e: e.memset(v1s[:, :, 128:129], 1.0), pwrites=[Bv1s])
            _cast_dma(k, v1w[:, :, 0:128], vw[gi].rearrange("(t p) d -> p t d", p=128), 128, writes=[Bv1w])
            k.op("pool", lambda e: e.memset(v1w[:, :, 128:129], 1.0), pwrites=[Bv1w])
            for r in range(nheads):
                h = gi * 8 + r
                _cast_dma(k, qb[:], qT[h], T, writes=[Bqb])

                def cb1(g, qt, po, Bpo, r=r):
                    qi = 4 * g + qt
                    c = rs_of(po, Bpo, 0, True)
                    if r == 0:
                        k.op("dve", lambda e: e.tensor_scalar(out=imp[:, qi, :], in0=po[:, 1:33], scalar1=rsb[:, c:c + 1],
                                                              scalar2=None, op0=ALU.mult),
                             reads=[Bpo, Brs], writes=[Bimp] if qi == 0 else [], pwrites=[] if qi == 0 else [Bimp])
                    else:
                        k.op("dve", lambda e: e.scalar_tensor_tensor(out=imp[:, qi, :], in0=po[:, 1:33], scalar=rsb[:, c:c + 1],
                                                                     in1=imp[:, qi, :], op0=ALU.mult, op1=ALU.add),
                             reads=[Bpo, Brs, Bimp], pwrites=[Bimp])
                att.run(kT=kcm, BkT=Bkcm, qT=qb, BqT=Bqb, v1=vco[:, :, 128:161], Bv1=Bvco, VW=33, plan=cplan,
                        idb=idb, Bidb=Bidb, masks=cmb, Bmasks=Bcmb, out_cb=cb1, KP=NCMP, rhs_all=True)
            k.op("dve", lambda e: e.tensor_tensor(out=sc[:].rearrange("p t j -> p (t j)"),
                                                  in0=imp[:].rearrange("p t j -> p (t j)"), in1=fc[:], op=ALU.add),
                 reads=[Bimp, Bfc], writes=[Bsc])
            for i in range(NTT):
                k.op("dve", lambda e: e.max(out=m8[:, 0:8], in_=sc[:, i, :]), reads=[Bsc], writes=[Bm8])
                k.op("dve", lambda e: e.match_replace(out=wk[:], in_to_replace=m8[:, 0:8], in_values=sc[:, i, :],
                                                      imm_value=-3e9), reads=[Bsc, Bm8], writes=[Bwk])
                k.op("dve", lambda e: e.max(out=m8[:, 8:16], in_=wk[:]), reads=[Bwk], pwrites=[Bm8])
                k.op("dve", lambda e: e.tensor_scalar(out=selm[:, i, :], in0=sc[:, i, :], scalar1=m8[:, 15:16], scalar2=None,
                                                      op0=ALU.is_ge), reads=[Bsc, Bm8],
                     writes=[Bselm] if i == 0 else [], pwrites=[] if i == 0 else [Bselm])
            k.op("dve", lambda e: e.tensor_scalar(out=okm[:], in0=sc[:].rearrange("p t j -> p (t j)"), scalar1=-5e8,
                                                  scalar2=None, op0=ALU.is_gt), reads=[Bsc], writes=[Bokm])
            k.op("dve", lambda e: e.tensor_tensor(out=selm[:].rearrange("p t j -> p (t j)"),
                                                  in0=selm[:].rearrange("p t j -> p (t j)"), in1=okm[:], op=ALU.mult),
                 reads=[Bselm, Bokm], pwrites=[Bselm])
            for i4 in range(4):
                p = pm[i4 % 2]; Bp_ = Bpm[i4 % 2]
                for j in range(4):
                    i = i4 * 4 + j
                    k.op("pe", lambda e: e.transpose(out=p[0:32, j * 128:(j + 1) * 128], in_=selm[:, i, :], identity=idf[:]),
                         reads=[Bselm, Bidf], writes=[Bp_] if j == 0 else [], pwrites=[] if j == 0 else [Bp_])
                k.op("act", lambda e: e.copy(out=selT1[0:32, i4 * 512:(i4 + 1) * 512], in_=p[0:32, :]),
                     reads=[Bp_], pwrites=[Bsel])
            for r in range(nheads):
                h = gi * 8 + r
                a = acc[h % 2]; Ba = Bacc[h % 2]
                _cast_dma(k, qb[:], qT[h], T, writes=[Bqb])
                _rope(k, qr, Bqr, qT[h], qTs[h], C, S, Bcs, xa, Bxa, xb, Bxb, ctr)

                def mk_cb(br, first, eps, a=a, Ba=Ba, h=h):
                    def cb(g, qt, po, Bpo):
                        qi = 4 * g + qt
                        c = rs_of(po, Bpo, 128, eps)
                        k.op("dve", lambda e: e.tensor_tensor(out=rsb[:, 64 + c:65 + c], in0=rsb[:, c:c + 1],
                                                              in1=gp[:, qi, h * 3 + br:h * 3 + br + 1], op=ALU.mult),
                             reads=[Brs, Bgp], pwrites=[Brs])
                        if first:
                            k.op("act", lambda e: e.activation(out=a[:, qi, :], in_=po[:, 0:128], func=AF.Copy,
                                                               scale=rsb[:, 64 + c:65 + c]),
                                 reads=[Bpo, Brs], writes=[Ba] if qi == 0 else [], pwrites=[] if qi == 0 else [Ba])
                        else:
                            k.op("dve", lambda e: e.scalar_tensor_tensor(out=a[:, qi, :], in0=po[:, 0:128],
                                                                         scalar=rsb[:, 64 + c:65 + c], in1=a[:, qi, :],
                                                                         op0=ALU.mult, op1=ALU.add),
                                 reads=[Bpo, Brs, Ba], pwrites=[Ba])
                    return cb
                att.run(kT=kcm, BkT=Bkcm, qT=qb, BqT=Bqb, v1=vco[:, :, 0:129], Bv1=Bvco, VW=129, plan=cplan,
                        idb=idb, Bidb=Bidb, masks=cmb, Bmasks=Bcmb, out_cb=mk_cb(0, True, True), KP=NCMP, rhs_all=True)
                att.run(kT=ksr, BkT=Bksr, qT=qr, BqT=Bqr, v1=v1s, Bv1=Bv1s, VW=129, plan=caus,
                        idb=idb, Bidb=Bidb, masks=mk, Bmasks=Bmk, out_cb=mk_cb(1, False, False),
                        sel=(eselb, selT1, Bsel))
                att.run(kT=kwr, BkT=Bkwr, qT=qr, BqT=Bqr, v1=v1w, Bv1=Bv1w, VW=129, plan=wplan,
                        idb=idb, Bidb=Bidb, masks=mk, Bmasks=Bmk, out_cb=mk_cb(2, False, False))
                k.dma("sp", o[:, h * 128:(h + 1) * 128].rearrange("(t p) d -> p t d", p=128), a[:],
                      reads=[Ba], owner=Ba)
        k.finish(Bacc)
        print(f"[att_nsa] ins={k.nins} waits={k.nwait} sems={k.nsem}")
    return nc


def _nsa_consts():
    c = {}
    n = np.arange(128)[:, None]; t = np.arange(T)[None, :]
    cm = np.where((t >= 16 * n + 31) & (n < NCMP), 0.0, MNEG).astype(np.float32)
    c["cmask"] = np.ascontiguousarray(cm)
    cmp_start = np.arange(NCMP) * 16; sel_start = np.arange(32) * 64
    ov = ((cmp_start[:, None] < sel_start[None, :] + 64) & (cmp_start[:, None] + 32 > sel_start[None, :])).astype(np.float32)
    ovl = np.zeros((128, 33), np.float32); ovl[:, 0] = 1.0; ovl[:NCMP, 1:] = ov
    c["ovl"] = ovl
    tt = np.arange(T); blk = tt // 64; j = np.arange(32)[None, :]
    valid = j <= blk[:, None]
    forced = (j == 0) | (j == blk[:, None]) | (j == blk[:, None] - 1)
    fcst = np.where(valid, np.where(forced, 1e9, 0.0), -1e9).astype(np.float32)
    c["fconst"] = np.ascontiguousarray(fcst.reshape(NTT, 128, 32).transpose(1, 0, 2).reshape(128, NTT * 32))
    es_ = np.zeros((33, NTT, 128), np.float32)
    for kt in range(NTT):
        es_[2 * kt, kt, 0:64] = -MNEG
        es_[2 * kt + 1, kt, 64:128] = -MNEG
    es_[32, :, :] = MNEG
    c["esel"] = es_.reshape(33, NTT * 128)
    return c


def _nsa_inputs(proj_b, g0, ng, p, c, cn):
    ca = np.ascontiguousarray
    q, kc, vc, ks, vs, kw, vw, gates = np.split(proj_b, np.cumsum([4096] + [512] * 6).tolist(), axis=1)
    hs = slice(g0 * 8, (g0 + ng) * 8); gs_ = slice(g0, g0 + ng)
    qT = q.reshape(T, 32, 128)[:, hs].transpose(1, 2, 0)
    grp = lambda a: a.reshape(T, 4, 128)[:, gs_].transpose(1, 0, 2)
    idx = 16 * np.arange(NCMP)[None, :] + np.arange(32)[:, None]
    blk = lambda a: ca(grp(a)[:, idx].transpose(0, 3, 1, 2).reshape(ng, 128, 32 * NCMP))
    trT = lambda a: grp(a).transpose(0, 2, 1)
    w1l = lambda w: ca(w.reshape(32, 128, 256).transpose(1, 0, 2).reshape(128, 32 * 256))
    w2l = lambda w: ca(w.reshape(2, 128, 128).transpose(1, 0, 2).reshape(128, 256))
    b1 = np.concatenate([p["odd_cmp_k_b1"][0].reshape(2, 128).T, p["odd_cmp_v_b1"][0].reshape(2, 128).T], 1)
    return {
        "qT": ca(qT), "qTs": ca(_swap_halves(qT, 1)),
        "kcb": blk(kc), "vcb": blk(vc),
        "ksT": ca(trT(ks)), "ksTs": ca(_swap_halves(trT(ks), 1)),
        "kwT": ca(trT(kw)), "kwTs": ca(_swap_halves(trT(kw), 1)),
        "vs": ca(grp(vs)), "vw": ca(grp(vw)),
        "gpre": ca(gates[:, g0 * 24:(g0 + ng) * 24]),
        "w1k": w1l(p["odd_cmp_k_w1"][0]), "w1v": w1l(p["odd_cmp_v_w1"][0]), "b1": ca(b1),
        "w2k": w2l(p["odd_cmp_k_w2"][0]), "w2v": w2l(p["odd_cmp_v_w2"][0]),
        "pek": ca(p["odd_cmp_pe_k"][0].T), "pev": ca(p["odd_cmp_pe_v"][0].T),
        "ropeC": c["ropeC"], "ropeS": c["ropeS"], "maskcw": c["maskcw"], "ident": c["ident"],
        "cmask": cn["cmask"], "ovl": cn["ovl"], "fconst": cn["fconst"], "esel": cn["esel"],
    }


_PROGS = {}


def _prog(key, fn):
    if key not in _PROGS:
        _PROGS[key] = fn()
    return _PROGS[key]


def _launch(nc, in_maps, out_name):
    res = run_bass_kernel_spmd(nc, in_maps, core_ids=list(range(NCORES)))
    return [r[out_name] for r in res.results]


def _run_chain(key, stages, x0, feeds, c):
    nc = _prog(key, lambda: build_chain(stages))
    ims = []
    for i in range(NCORES):
        m = {"ident": c["ident"], "x0": np.ascontiguousarray(x0[i * 1024:(i + 1) * 1024])}
        for name, (arr, sharded) in feeds.items():
            m[name] = np.ascontiguousarray(arr[i * 1024:(i + 1) * 1024]) if sharded else arr
        ims.append(m)
    res = run_bass_kernel_spmd(nc, ims, core_ids=list(range(NCORES)))
    return lambda name: np.concatenate([r[name] for r in res.results], 0)


def _ffn_feeds(i, p, which, layer):
    return {f"s{i}_gT": (_gT(p[f"{which}_norm"][layer]), False),
            f"s{i}_wg": (_ffn_weight_layout(p[f"{which}_w_gate"][layer]), False),
            f"s{i}_wu": (_ffn_weight_layout(p[f"{which}_w_up"][layer]), False),
            f"s{i}_wd": (np.ascontiguousarray(p[f"{which}_w_down"][layer]), False)}


def kernel(**inp):
    p = {k_: np.asarray(v, dtype=np.float32) for k_, v in inp.items()}
    c = _consts(); cn = _nsa_consts()
    B = 4
    ones_gT = np.ones((128, DC), np.float32)
    x = p["x"].reshape(B * T, D)
    feeds = _ffn_feeds(0, p, "ffn1", 0)
    feeds.update({"s1_gT": (_gT(p["mix_norm"][0]), False), "s1_w": (np.ascontiguousarray(p["even_w_in"][0]), False)})
    out = _run_chain("A", [("ffn",), ("lin", 12304, True, False)], x, feeds, c)
    h = out("s0_y"); proj = out("s1_y")
    lam_init = 0.8 - 0.6 * math.exp(-0.3 * 0)
    nc = _prog("even", lambda: build_att_even(lam_init))
    ims = [_even_inputs(proj[(i // 2) * T:(i // 2 + 1) * T], i % 2, p, c) for i in range(NCORES)]
    outs = _launch(nc, ims, "o")
    attn = np.empty((B * T, D), np.float32)
    for i in range(NCORES):
        b, hh = i // 2, i % 2
        attn[b * T:(b + 1) * T, hh * 1024:(hh + 1) * 1024] = outs[i][:, 0:1024]
        attn[b * T:(b + 1) * T, 2048 + hh * 1024:2048 + (hh + 1) * 1024] = outs[i][:, 1024:2048]
    del proj, ims, outs, out, feeds
    feeds = {"s0_gT": (ones_gT, False), "s0_w": (np.ascontiguousarray(p["even_w_out"][0]), False), "s0_res": (h, True)}
    feeds.update(_ffn_feeds(1, p, "ffn2", 0))
    feeds.update(_ffn_feeds(2, p, "ffn1", 1))
    feeds.update({"s3_gT": (_gT(p["mix_norm"][1]), False), "s3_w": (np.ascontiguousarray(p["odd_w_in"][0]), False)})
    out = _run_chain("C", [("lin", 4096, False, True), ("ffn",), ("ffn",), ("lin", 7264, True, False)], attn, feeds, c)
    h = out("s2_y"); proj = out("s3_y")
    del feeds, out
    nc = _prog("nsa", lambda: build_att_nsa(ngroups=NSA_NG))
    for L in range(2 // NSA_NG):
        ims = [_nsa_inputs(proj[(i // 2) * T:(i // 2 + 1) * T], (L * 2 + i % 2) if NSA_NG == 1 else 2 * (i % 2),
                           NSA_NG, p, c, cn) for i in range(NCORES)]
        outs = _launch(nc, ims, "o")
        for i in range(NCORES):
            b = i // 2
            g0 = (L * 2 + i % 2) if NSA_NG == 1 else 2 * (i % 2)
            attn[b * T:(b + 1) * T, g0 * 1024:(g0 + NSA_NG) * 1024] = outs[i]
    del proj, ims, outs
    feeds = {"s0_gT": (ones_gT, False), "s0_w": (np.ascontiguousarray(p["odd_w_out"][0]), False), "s0_res": (h, True)}
    feeds.update(_ffn_feeds(1, p, "ffn2", 1))
    feeds.update({"s2_gR": (np.ascontiguousarray(np.broadcast_to(p["final_norm"], (128, D))), False)})
    out = _run_chain("E", [("lin", 4096, False, True), ("ffn",), ("fnorm",)], attn, feeds, c)
    return out("s2_y").reshape(B, T, D).astype(np.float32)
```

```python
import math
import numpy as np
import concourse.bass as bass
import concourse.mybir as mybir
from concourse.bass_utils import run_bass_kernel_spmd
from contextlib import ExitStack

F32 = mybir.dt.float32
BF16 = mybir.dt.bfloat16
I32 = mybir.dt.int32
AF = mybir.ActivationFunctionType
ALU = mybir.AluOpType
AX = mybir.AxisListType

D = 4096
DC = D // 128
DFF = 11008
NFF = DFF // 128
NCORES = 8
RMS_EPS = 1e-6


class Buf:
    __slots__ = ("name", "w", "r", "dsem", "dcnt")

    def __init__(self, name):
        self.name = name
        self.w = {}
        self.r = {}
        self.dsem = None
        self.dcnt = 0


class K:
    def __init__(self, nc, es):
        self.nc = nc
        self.es = es
        self.engs = {"pe": nc.tensor, "dve": nc.vector, "act": nc.scalar,
                     "pool": nc.gpsimd, "sp": nc.sync}
        self.sem = {k: es.enter_context(nc.semaphore("s_" + k))
                    for k in ["pe", "dve", "act", "pool"]}
        self.cnt = {k: 0 for k in self.sem}
        self.waited = {k: {} for k in self.engs}
        self.nsem = 0
        self.nwait = 0
        self.nins = 0
        self.pend = {k: [] for k in self.engs}
        self.es_outer = es
        self.pfx = ""
        self.dsems = []

    def sb(self, name, shape, dt):
        return self.es.enter_context(self.nc.sbuf_tensor(self.pfx + name, shape, dt))

    def ps(self, name, shape, dt=F32):
        return self.es.enter_context(self.nc.psum_tensor(self.pfx + name, shape, dt))

    def newsem(self, name):
        self.nsem += 1
        return self.es_outer.enter_context(self.nc.semaphore(self.pfx + name))

    def rotate(self):
        self.barrier()
        self.nrot = getattr(self, "nrot", 0) + 1
        for e in list(self.sem):
            self.sem[e] = self.es_outer.enter_context(self.nc.semaphore(f"{self.pfx}s_{e}_r{self.nrot}"))
            self.cnt[e] = 0

    def barrier(self):
        evs = [(self.sem[e], self.cnt[e]) for e in self.sem if self.cnt[e] > 0]
        evs += [(b.dsem, b.dcnt) for b in self.dsems if b.dcnt > 0]
        for eng in self.engs:
            assert not self.pend[eng]
            wd = self.waited[eng]
            for (sm, v) in evs:
                if wd.get(sm.num, -1) >= v:
                    continue
                self.engs[eng].wait_ge(sm, v)
                self.nwait += 1
                wd[sm.num] = v

    def _deps(self, eng, reads, writes):
        need = {}

        def add(ev, kind):
            s, v = ev
            key = s.num
            if eng in self.sem and s.num == self.sem[eng].num:
                if eng == "pe" or kind != "raw":
                    return
            if key not in need or need[key][1] < v:
                need[key] = (s, v)

        for b in reads:
            for ev in b.w.values():
                add(ev, "raw")
        for b in writes:
            for ev in b.w.values():
                add(ev, "waw")
            for ev in b.r.values():
                add(ev, "war")
        e = self.engs[eng]
        wd = self.waited[eng]
        for key, (s, v) in need.items():
            if wd.get(key, -1) >= v:
                continue
            e.wait_ge(s, v)
            self.nwait += 1
            wd[key] = v

    def _record(self, ev, reads, writes, pwrites=()):
        key = ev[0].num
        for b in reads:
            b.r[key] = ev
        for b in writes:
            b.w = {key: ev}
            b.r = {}
        for b in pwrites:
            b.w[key] = ev

    def op(self, eng, fn, reads=(), writes=(), pwrites=(), inc=True):
        self._deps(eng, reads, list(writes) + list(pwrites))
        ins = fn(self.engs[eng])
        self.nins += 1
        if not inc:
            self.pend[eng].append((list(reads), list(writes), list(pwrites)))
            return ins
        self.cnt[eng] += 1
        ins.then_inc(self.sem[eng], 1)
        ev = (self.sem[eng], self.cnt[eng])
        for (r_, w_, pw_) in self.pend[eng]:
            self._record(ev, r_, w_, pw_)
        self.pend[eng] = []
        self._record(ev, reads, writes, pwrites)
        return ins

    def dma(self, q, out, in_, reads=(), writes=(), pwrites=(), owner=None, **kw):
        allw = list(writes) + list(pwrites)
        self._deps(q, reads, allw)
        if owner is None:
            owner = allw[0] if allw else reads[0]
        if owner.dsem is None:
            owner.dsem = self.newsem("d_" + owner.name)
            self.dsems.append(owner)
        ins = self.engs[q].dma_start(out=out, in_=in_, **kw)
        self.nins += 1
        owner.dcnt += 16
        ins.then_inc(owner.dsem, 16)
        ev = (owner.dsem, owner.dcnt)
        self._record(ev, reads, writes, pwrites)
        return ins

    def finish(self, bufs, eng="sp"):
        self._deps(eng, bufs, bufs)


def _cast_dma(k, dst, src, ncols, **kw):
    k.dma("pool", dst, src, max_dma_last_dim=2048, **kw)


def build_ffn(NT=1024, nff=NFF):
    nc = bass.Bass("TRN2", target_bir_lowering=False)
    x = nc.dram_tensor("x", [NT, D], F32, kind="ExternalInput").ap()
    gT = nc.dram_tensor("gT", [128, DC], F32, kind="ExternalInput").ap()
    ident = nc.dram_tensor("ident", [128, 128], F32, kind="ExternalInput").ap()
    wg = nc.dram_tensor("wg", [nff, 128, D], F32, kind="ExternalInput").ap()
    wu = nc.dram_tensor("wu", [nff, 128, D], F32, kind="ExternalInput").ap()
    wd = nc.dram_tensor("wd", [nff * 128, D], F32, kind="ExternalInput").ap()
    y = nc.dram_tensor("y", [NT, D], F32, kind="ExternalOutput").ap()
    with ExitStack() as es:
        k = K(nc, es)
        emit_ffn(k, x, gT, ident, wg, wu, wd, y, NT, nff)
    return nc


def emit_ffn(k, x, gT, ident, wg, wu, wd, y, NT=1024, nff=NFF):
    TB = 512
    NB = NT // TB
    with ExitStack() as ph:
        k.es = ph
        gs = k.sb("gs", [128, DC], F32); Bg = Buf("gs")
        ids = k.sb("ids", [128, 128], F32); Bid = Buf("ids")
        xs = [k.sb("xs0", [128, D], F32)] * 2
        Bxs = [Buf("xs0")] * 2
        st = [k.sb(f"st{i}", [128, 4], F32) for i in range(2)]
        Bst = [Buf(f"st{i}") for i in range(2)]
        junk = k.sb("junk", [128, D], BF16); Bjunk = Buf("junk")
        xnT = k.sb("xnT", [128, DC, TB], BF16); BxnT = Buf("xnT")
        hT = k.sb("hT", [128, nff, TB], BF16); BhT = Buf("hT")
        wgb = [k.sb(f"wgb{i}", [128, D], BF16) for i in range(2)]
        wub = [k.sb(f"wub{i}", [128, D], BF16) for i in range(2)]
        Bwg = [Buf(f"wgb{i}") for i in range(2)]
        Bwu = [Buf(f"wub{i}") for i in range(2)]
        wdb = [k.sb(f"wdb{i}", [128, 8, 512], BF16) for i in range(2)]
        Bwd = [Buf(f"wdb{i}") for i in range(2)]
        sg = [k.sb(f"sg{i}", [128, TB], F32) for i in range(2)]
        Bsg = [Buf(f"sg{i}") for i in range(2)]
        xr = [k.sb(f"xr{i}", [128, 512], F32) for i in range(2)]
        Bxr = [Buf(f"xr{i}") for i in range(2)]
        yo = [k.sb(f"yo{i}", [128, 512], F32) for i in range(2)]
        Byo = [Buf(f"yo{i}") for i in range(2)]
        pt = [k.ps(f"pt{i}", [128, 512]) for i in range(8)]
        Bp = [Buf(f"pt{i}") for i in range(8)]

        k.dma("sp", gs[:], gT, writes=[Bg])
        k.dma("sp", ids[:], ident, writes=[Bid])

        nxt = 0
        ev2 = 0
        for tb in range(NB):
            t0 = tb * TB
            for tt in range(4):
                s = tt % 2
                r0 = t0 + tt * 128
                k.dma("sp", xs[s][:], x[r0:r0 + 128, :], writes=[Bxs[s]])
                k.op("act", lambda e: e.activation(out=junk[:], in_=xs[s][:], func=AF.Square,
                                                   accum_out=st[s][:, 0:1]),
                     reads=[Bxs[s]], writes=[Bjunk, Bst[s]])
                k.op("dve", lambda e: e.tensor_scalar(out=st[s][:, 1:2], in0=st[s][:, 0:1],
                                                      scalar1=1.0 / D, scalar2=RMS_EPS,
                                                      op0=ALU.mult, op1=ALU.add),
                     reads=[Bst[s]], pwrites=[Bst[s]])
                k.op("act", lambda e: e.activation(out=st[s][:, 2:3], in_=st[s][:, 1:2], func=AF.Sqrt),
                     reads=[Bst[s]], pwrites=[Bst[s]])
                k.op("dve", lambda e: e.reciprocal(out=st[s][:, 3:4], in_=st[s][:, 2:3]),
                     reads=[Bst[s]], pwrites=[Bst[s]])
                k.op("dve", lambda e: e.tensor_scalar(out=xs[s][:], in0=xs[s][:], scalar1=st[s][:, 3:4],
                                                      scalar2=None, op0=ALU.mult),
                     reads=[Bst[s], Bxs[s]], pwrites=[Bxs[s]])
                for c4 in range(DC // 4):
                    p = nxt % 8; nxt += 1
                    for j in range(4):
                        c = c4 * 4 + j
                        k.op("pe", lambda e: e.transpose(out=pt[p][:, j * 128:(j + 1) * 128],
                                                         in_=xs[s][:, c * 128:(c + 1) * 128],
                                                         identity=ids[:]),
                             reads=[Bxs[s], Bid], writes=[Bp[p]] if j == 0 else [],
                             pwrites=[] if j == 0 else [Bp[p]], inc=(j == 3))
                    for j in range(4):
                        c = c4 * 4 + j
                        eng = "dve" if j % 2 == 0 else "act"
                        if eng == "dve":
                            k.op("dve", lambda e: e.tensor_scalar(
                                out=xnT[:, c, tt * 128:(tt + 1) * 128], in0=pt[p][:, j * 128:(j + 1) * 128],
                                scalar1=gs[:, c:c + 1], scalar2=None, op0=ALU.mult),
                                reads=[Bp[p], Bg], pwrites=[BxnT])
                        else:
                            k.op("act", lambda e: e.activation(
                                out=xnT[:, c, tt * 128:(tt + 1) * 128], in_=pt[p][:, j * 128:(j + 1) * 128],
                                func=AF.Copy, scale=gs[:, c:c + 1]),
                                reads=[Bp[p], Bg], pwrites=[BxnT])
            for f in range(nff):
                s = f % 2
                _cast_dma(k, wgb[s][:], wg[f], D, writes=[Bwg[s]])
                _cast_dma(k, wub[s][:], wu[f], D, writes=[Bwu[s]])
                pg, pu = pt[2 * s], pt[2 * s + 1]
                for c in range(DC):
                    k.op("pe", lambda e: e.matmul(pg[:], lhsT=wgb[s][:, c * 128:(c + 1) * 128],
                                                  rhs=xnT[:, c, :], start=(c == 0), stop=(c == DC - 1)),
                         reads=[Bwg[s], BxnT], writes=[Bp[2 * s]] if c == 0 else [],
                         pwrites=[] if c == 0 else [Bp[2 * s]], inc=(c == DC - 1))
                for c in range(DC):
                    k.op("pe", lambda e: e.matmul(pu[:], lhsT=wub[s][:, c * 128:(c + 1) * 128],
                                                  rhs=xnT[:, c, :], start=(c == 0), stop=(c == DC - 1)),
                         reads=[Bwu[s], BxnT], writes=[Bp[2 * s + 1]] if c == 0 else [],
                         pwrites=[] if c == 0 else [Bp[2 * s + 1]], inc=(c == DC - 1))
                k.op("act", lambda e: e.activation(out=sg[s][:], in_=pg[:], func=AF.Silu),
                     reads=[Bp[2 * s]], writes=[Bsg[s]])
                k.op("dve", lambda e: e.tensor_tensor(out=hT[:, f, :], in0=sg[s][:], in1=pu[:], op=ALU.mult),
                     reads=[Bsg[s], Bp[2 * s + 1]], pwrites=[BhT])
            ngrp = (nff + 7) // 8
            li = 0
            for db in range(D // 512):
                ps0 = 4 * (db % 2)
                for g in range(ngrp):
                    s = li % 2; li += 1
                    nk = min(8, nff - g * 8)
                    _cast_dma(k, wdb[s][:, 0:nk, :],
                              wd[g * 1024:g * 1024 + nk * 128, db * 512:(db + 1) * 512]
                              .rearrange("(k p) n -> p k n", p=128),
                              512, writes=[Bwd[s]])
                    for kk in range(nk):
                        f = g * 8 + kk
                        for tt in range(4):
                            first = (f == 0)
                            k.op("pe", lambda e: e.matmul(pt[ps0 + tt][:], lhsT=hT[:, f, tt * 128:(tt + 1) * 128],
                                                          rhs=wdb[s][:, kk, :], start=first, stop=(f == nff - 1)),
                                 reads=[BhT, Bwd[s]], writes=[Bp[ps0 + tt]] if first else [],
                                 pwrites=[] if first else [Bp[ps0 + tt]],
                                 inc=(f == nff - 1) or (kk == nk - 1 and tt == 3))
                for tt in range(4):
                    s2 = ev2 % 2; ev2 += 1
                    r0 = t0 + tt * 128
                    k.dma("sp", xr[s2][:], x[r0:r0 + 128, db * 512:(db + 1) * 512], writes=[Bxr[s2]])
                    k.op("dve", lambda e: e.scalar_tensor_tensor(
                        out=yo[s2][:], in0=pt[ps0 + tt][:], scalar=0.5, in1=xr[s2][:],
                        op0=ALU.mult, op1=ALU.add),
                        reads=[Bp[ps0 + tt], Bxr[s2]], writes=[Byo[s2]])
                    k.dma("sp", y[r0:r0 + 128, db * 512:(db + 1) * 512], yo[s2][:], reads=[Byo[s2]], owner=Byo[s2])
        k.barrier()
        print(f"[ffn] ins={k.nins} waits={k.nwait} sems={k.nsem} cnt={k.cnt}")


def _ffn_weight_layout(w):
    nff = w.shape[1] // 128
    return np.ascontiguousarray(w.reshape(DC, 128, nff, 128).transpose(2, 1, 0, 3)).reshape(nff, 128, D)


def _gT(g):
    return np.ascontiguousarray(g.reshape(DC, 128).T)


def build_linear(ncols, norm, residual, NT=1024):
    nc = bass.Bass("TRN2", target_bir_lowering=False)
    x = nc.dram_tensor("x", [NT, D], F32, kind="ExternalInput").ap()
    gT = nc.dram_tensor("gT", [128, DC], F32, kind="ExternalInput").ap()
    ident = nc.dram_tensor("ident", [128, 128], F32, kind="ExternalInput").ap()
    w = nc.dram_tensor("w", [D, ncols], F32, kind="ExternalInput").ap()
    res = nc.dram_tensor("res", [NT, ncols], F32, kind="ExternalInput").ap() if residual else None
    y = nc.dram_tensor("y", [NT, ncols], F32, kind="ExternalOutput").ap()
    with ExitStack() as es:
        k = K(nc, es)
        emit_linear(k, x, gT, ident, w, res, y, ncols, norm, NT)
    return nc


def emit_linear(k, x, gT, ident, w, res, y, ncols, norm, NT=1024):
    NTT = NT // 128
    residual = res is not None
    with ExitStack() as ph:
        k.es = ph
        gs = k.sb("gs", [128, DC], F32); Bg = Buf("gs")
        ids = k.sb("ids", [128, 128], F32); Bid = Buf("ids")
        xs = [k.sb(f"xs{i}", [128, D], F32) for i in range(2)]
        Bxs = [Buf(f"xs{i}") for i in range(2)]
        st = [k.sb(f"st{i}", [128, 4], F32) for i in range(2)]
        Bst = [Buf(f"st{i}") for i in range(2)]
        junk = k.sb("junk", [128, D], BF16); Bjunk = Buf("junk")
        xT = k.sb("xT", [128, DC, NT], BF16); BxT = Buf("xT")
        wb = [k.sb(f"wb{i}", [128, DC, 512], BF16) for i in range(2)]
        Bw = [Buf(f"wb{i}") for i in range(2)]
        rs = [k.sb(f"rs{i}", [128, 512], F32) for i in range(2)]
        Brs = [Buf(f"rs{i}") for i in range(2)]
        yo = [k.sb(f"yo{i}", [128, 512], F32) for i in range(2)]
        Byo = [Buf(f"yo{i}") for i in range(2)]
        pt = [k.ps(f"pt{i}", [128, 512]) for i in range(8)]
        Bp = [Buf(f"pt{i}") for i in range(8)]
        k.dma("sp", gs[:], gT, writes=[Bg])
        k.dma("sp", ids[:], ident, writes=[Bid])
        nxt = 0
        for tt in range(NTT):
            s = tt % 2
            k.dma("sp", xs[s][:], x[tt * 128:(tt + 1) * 128, :], writes=[Bxs[s]])
            if norm:
                k.op("act", lambda e: e.activation(out=junk[:], in_=xs[s][:], func=AF.Square,
                                                   accum_out=st[s][:, 0:1]),
                     reads=[Bxs[s]], writes=[Bjunk, Bst[s]])
                k.op("dve", lambda e: e.tensor_scalar(out=st[s][:, 1:2], in0=st[s][:, 0:1],
                                                      scalar1=1.0 / D, scalar2=RMS_EPS,
                                                      op0=ALU.mult, op1=ALU.add),
                     reads=[Bst[s]], pwrites=[Bst[s]])
                k.op("act", lambda e: e.activation(out=st[s][:, 2:3], in_=st[s][:, 1:2], func=AF.Sqrt),
                     reads=[Bst[s]], pwrites=[Bst[s]])
                k.op("dve", lambda e: e.reciprocal(out=st[s][:, 3:4], in_=st[s][:, 2:3]),
                     reads=[Bst[s]], pwrites=[Bst[s]])
                k.op("dve", lambda e: e.tensor_scalar(out=xs[s][:], in0=xs[s][:], scalar1=st[s][:, 3:4],
                                                      scalar2=None, op0=ALU.mult),
                     reads=[Bst[s], Bxs[s]], pwrites=[Bxs[s]])
            for c4 in range(DC // 4):
                p = nxt % 8; nxt += 1
                for j in range(4):
                    c = c4 * 4 + j
                    k.op("pe", lambda e: e.transpose(out=pt[p][:, j * 128:(j + 1) * 128],
                                                     in_=xs[s][:, c * 128:(c + 1) * 128], identity=ids[:]),
                         reads=[Bxs[s], Bid], writes=[Bp[p]] if j == 0 else [],
                         pwrites=[] if j == 0 else [Bp[p]], inc=(j == 3))
                for j in range(4):
                    c = c4 * 4 + j
                    if j % 2 == 0:
                        k.op("dve", lambda e: e.tensor_scalar(
                            out=xT[:, c, tt * 128:(tt + 1) * 128], in0=pt[p][:, j * 128:(j + 1) * 128],
                            scalar1=gs[:, c:c + 1], scalar2=None, op0=ALU.mult),
                            reads=[Bp[p], Bg], pwrites=[BxT])
                    else:
                        k.op("act", lambda e: e.activation(
                            out=xT[:, c, tt * 128:(tt + 1) * 128], in_=pt[p][:, j * 128:(j + 1) * 128],
                            func=AF.Copy, scale=gs[:, c:c + 1]),
                            reads=[Bp[p], Bg], pwrites=[BxT])
        ncb = (ncols + 511) // 512
        ev = 0
        for cb in range(ncb):
            c0 = cb * 512
            cw = min(512, ncols - c0)
            s = cb % 2
            for hf in range(2):
                _cast_dma(k, wb[s][:, hf * 16:(hf + 1) * 16, 0:cw],
                          w[hf * 2048:(hf + 1) * 2048, c0:c0 + cw].rearrange("(c p) n -> p c n", p=128),
                          cw, writes=[Bw[s]] if hf == 0 else [], pwrites=[] if hf == 0 else [Bw[s]])
            for tt in range(NTT):
                p = nxt % 8; nxt += 1
                for c in range(DC):
                    k.op("pe", lambda e: e.matmul(pt[p][:, 0:cw], lhsT=xT[:, c, tt * 128:(tt + 1) * 128],
                                                  rhs=wb[s][:, c, 0:cw], start=(c == 0), stop=(c == DC - 1)),
                         reads=[BxT, Bw[s]], writes=[Bp[p]] if c == 0 else [],
                         pwrites=[] if c == 0 else [Bp[p]], inc=(c == DC - 1))
                s2 = ev % 2; ev += 1
                if residual:
                    k.dma("sp", rs[s2][:, 0:cw], res[tt * 128:(tt + 1) * 128, c0:c0 + cw], writes=[Brs[s2]])
                    k.op("dve", lambda e: e.tensor_tensor(out=yo[s2][:, 0:cw], in0=pt[p][:, 0:cw],
                                                          in1=rs[s2][:, 0:cw], op=ALU.add),
                         reads=[Bp[p], Brs[s2]], writes=[Byo[s2]])
                else:
                    if ev % 2 == 0:
                        k.op("dve", lambda e: e.tensor_copy(out=yo[s2][:, 0:cw], in_=pt[p][:, 0:cw]),
                             reads=[Bp[p]], writes=[Byo[s2]])
                    else:
                        k.op("act", lambda e: e.copy(out=yo[s2][:, 0:cw], in_=pt[p][:, 0:cw]),
                             reads=[Bp[p]], writes=[Byo[s2]])
                k.dma("sp", y[tt * 128:(tt + 1) * 128, c0:c0 + cw], yo[s2][:, 0:cw],
                      reads=[Byo[s2]], owner=Byo[s2])
        k.barrier()
        print(f"[linear {ncols} n={norm} r={residual}] ins={k.nins} waits={k.nwait} sems={k.nsem} cnt={k.cnt}")


def build_fnorm(NT=1024):
    nc = bass.Bass("TRN2", target_bir_lowering=False)
    x = nc.dram_tensor("x", [NT, D], F32, kind="ExternalInput").ap()
    gR = nc.dram_tensor("gR", [128, D], F32, kind="ExternalInput").ap()
    y = nc.dram_tensor("y", [NT, D], F32, kind="ExternalOutput").ap()
    with ExitStack() as es:
        k = K(nc, es)
        emit_fnorm(k, x, gR, y, NT)
    return nc


def emit_fnorm(k, x, gR, y, NT=1024):
    with ExitStack() as ph:
        k.es = ph
        gs = k.sb("gs", [128, D], F32); Bg = Buf("gs")
        xs = [k.sb(f"xs{i}", [128, D], F32) for i in range(2)]
        Bxs = [Buf(f"xs{i}") for i in range(2)]
        ys = [k.sb(f"ys{i}", [128, D], F32) for i in range(2)]
        Bys = [Buf(f"ys{i}") for i in range(2)]
        st = [k.sb(f"st{i}", [128, 4], F32) for i in range(2)]
        Bst = [Buf(f"st{i}") for i in range(2)]
        junk = k.sb("junk", [128, D], BF16); Bjunk = Buf("junk")
        k.dma("sp", gs[:], gR, writes=[Bg])
        for tt in range(NT // 128):
            s = tt % 2
            k.dma("sp", xs[s][:], x[tt * 128:(tt + 1) * 128, :], writes=[Bxs[s]])
            k.op("act", lambda e: e.activation(out=junk[:], in_=xs[s][:], func=AF.Square,
                                               accum_out=st[s][:, 0:1]),
                 reads=[Bxs[s]], writes=[Bjunk, Bst[s]])
            k.op("dve", lambda e: e.tensor_scalar(out=st[s][:, 1:2], in0=st[s][:, 0:1],
                                                  scalar1=1.0 / D, scalar2=RMS_EPS, op0=ALU.mult, op1=ALU.add),
                 reads=[Bst[s]], pwrites=[Bst[s]])
            k.op("act", lambda e: e.activation(out=st[s][:, 2:3], in_=st[s][:, 1:2], func=AF.Sqrt),
                 reads=[Bst[s]], pwrites=[Bst[s]])
            k.op("dve", lambda e: e.reciprocal(out=st[s][:, 3:4], in_=st[s][:, 2:3]),
                 reads=[Bst[s]], pwrites=[Bst[s]])
            k.op("dve", lambda e: e.scalar_tensor_tensor(out=ys[s][:], in0=xs[s][:], scalar=st[s][:, 3:4],
                                                         in1=gs[:], op0=ALU.mult, op1=ALU.mult),
                 reads=[Bst[s], Bxs[s], Bg], writes=[Bys[s]])
            k.dma("sp", y[tt * 128:(tt + 1) * 128, :], ys[s][:], reads=[Bys[s]], owner=Bys[s])
        k.barrier()


def build_chain(stages, NT=1024, nff=NFF):
    nc = bass.Bass("TRN2", target_bir_lowering=False)
    ein = lambda n, sh: nc.dram_tensor(n, sh, F32, kind="ExternalInput").ap()
    eout = lambda n, sh: nc.dram_tensor(n, sh, F32, kind="ExternalOutput").ap()
    ident = ein("ident", [128, 128])
    cur = ein("x0", [NT, D])
    t = []
    for i, st in enumerate(stages):
        if st[0] == "ffn":
            t.append(dict(gT=ein(f"s{i}_gT", [128, DC]), wg=ein(f"s{i}_wg", [nff, 128, D]),
                          wu=ein(f"s{i}_wu", [nff, 128, D]), wd=ein(f"s{i}_wd", [nff * 128, D]),
                          y=eout(f"s{i}_y", [NT, D])))
        elif st[0] == "lin":
            ncols = st[1]
            t.append(dict(gT=ein(f"s{i}_gT", [128, DC]), w=ein(f"s{i}_w", [D, ncols]),
                          res=ein(f"s{i}_res", [NT, ncols]) if st[3] else None,
                          y=eout(f"s{i}_y", [NT, ncols])))
        else:
            t.append(dict(gR=ein(f"s{i}_gR", [128, D]), y=eout(f"s{i}_y", [NT, D])))
    with ExitStack() as es:
        k = K(nc, es)
        for i, st in enumerate(stages):
            k.pfx = f"s{i}_"
            d = t[i]
            if st[0] == "ffn":
                emit_ffn(k, cur, d["gT"], ident, d["wg"], d["wu"], d["wd"], d["y"], NT, nff)
            elif st[0] == "lin":
                emit_linear(k, cur, d["gT"], ident, d["w"], d["res"], d["y"], st[1], st[2], NT)
            else:
                emit_fnorm(k, cur, d["gR"], d["y"], NT)
            cur = d["y"]
    return nc


T = 2048
NTT = T // 128
HD = 128
SCALE = HD ** -0.5
MNEG = -float(2 ** 20)


class Att:
    def __init__(self, k):
        self.k = k
        self.ps = [k.ps(f"pss{i}", [128, 512]) for i in range(2)]
        self.Bps = [Buf(f"pss{i}") for i in range(2)]
        self.po = [k.ps(f"po{i}", [128, 512]) for i in range(4)]
        self.Bpo = [Buf(f"po{i}") for i in range(4)]
        self.PT = [k.sb(f"PT{i}", [128, 512], BF16) for i in range(2)]
        self.BPT = [Buf(f"PT{i}") for i in range(2)]
        self.n = 0

    def run(self, *, kT, BkT, qT, BqT, v1, Bv1, VW, plan, idb, Bidb, masks, Bmasks,
            out_cb, KP=128, bias_k=None, Bbias=None, pre=None, sel=None, rhs_all=False):
        k = self.k
        for g in range(4):
            tiles = plan[g]
            first = {}; last = {}
            for (kt, mi, qlo, qhi) in tiles:
                for qt in range(qlo, qhi + 1):
                    first.setdefault(qt, kt); last[qt] = kt
            for (kt, mi, qlo, qhi) in tiles:
                s = self.n % 2; self.n += 1
                ps, Bps = self.ps[s], self.Bps[s]
                c0, c1 = qlo * 128, (qhi + 1) * 128
                nmm = 1 + (mi is not None) + (sel is not None)
                i = 0
                k.op("pe", lambda e: e.matmul(ps[0:KP, c0:c1], lhsT=kT[:, kt * 128:kt * 128 + KP],
                                              rhs=qT[:, g * 512 + c0:g * 512 + c1], start=True, stop=(nmm == 1)),
                     reads=[BkT, BqT], writes=[Bps])
                i += 1
                if mi is not None:
                    k.op("pe", lambda e: e.matmul(ps[0:KP, c0:c1], lhsT=idb[:, 0:KP],
                                                  rhs=masks[:, mi, g * 512 + c0:g * 512 + c1] if rhs_all else masks[:, mi, c0:c1],
                                                  start=False, stop=(i == nmm - 1)),
                         reads=[Bidb, Bmasks], pwrites=[Bps])
                    i += 1
                if sel is not None:
                    esel, selT, Bsel = sel
                    k.op("pe", lambda e: e.matmul(ps[0:KP, c0:c1], lhsT=esel[:, kt, :],
                                                  rhs=selT[:, g * 512 + c0:g * 512 + c1], start=False, stop=True),
                         reads=[Bsel], pwrites=[Bps])
                PT, BPT = self.PT[s], self.BPT[s]
                if pre is not None:
                    tmp, Btmp, qb, Bqb = pre
                    k.op("dve", lambda e: e.scalar_tensor_tensor(
                        out=tmp[s][:, c0:c1], in0=ps[:, c0:c1], scalar=SCALE, in1=qb[:, g * 512 + c0:g * 512 + c1],
                        op0=ALU.mult, op1=ALU.add), reads=[Bps, Bqb], writes=[Btmp[s]])
                    k.op("act", lambda e: e.activation(out=PT[:, c0:c1], in_=tmp[s][:, c0:c1], func=AF.Exp,
                                                       bias=bias_k[:, kt:kt + 1], scale=1.0),
                         reads=[Btmp[s], Bbias], writes=[BPT])
                else:
                    k.op("act", lambda e: e.activation(out=PT[0:KP, c0:c1], in_=ps[0:KP, c0:c1], func=AF.Exp,
                                                       scale=SCALE),
                         reads=[Bps], writes=[BPT])
                for qt in range(qlo, qhi + 1):
                    st_, sp_ = (first[qt] == kt), (last[qt] == kt)
                    k.op("pe", lambda e: e.matmul(self.po[qt][:, 0:VW], lhsT=PT[0:KP, qt * 128:(qt + 1) * 128],
                                                  rhs=v1[0:KP, kt, 0:VW], start=st_, stop=sp_),
                         reads=[BPT, Bv1], writes=[self.Bpo[qt]] if st_ else [],
                         pwrites=[] if st_ else [self.Bpo[qt]])
            for qt in range(4):
                out_cb(g, qt, self.po[qt], self.Bpo[qt])


def causal_plan():
    plan = []
    for g in range(4):
        tl = [(kt, None, 0, 3) for kt in range(4 * g)]
        tl += [(4 * g + r, r, r, 3) for r in range(4)]
        plan.append(tl)
    return plan


def window_plan():
    plan = []
    for g in range(4):
        tl = []
        if g > 0:
            tl += [(4 * g - 4 + r, 4 + r, 0, r) for r in range(4)]
        tl += [(4 * g + r, r, r, 3) for r in range(4)]
        plan.append(tl)
    return plan


def _rope(k, dst, Bdst, src, srcsw, C, S, Bcs, xa, Bxa, xb, Bxb, ctr):
    for hf in range(2):
        s = ctr[0] % 2; ctr[0] += 1
        cs = slice(hf * 1024, (hf + 1) * 1024)
        k.dma("sp", xa[s][:], src[:, cs], writes=[Bxa[s]])
        k.dma("sp", xb[s][:], srcsw[:, cs], writes=[Bxb[s]])
        k.op("dve", lambda e: e.tensor_tensor(out=xa[s][:], in0=xa[s][:], in1=C[:, cs], op=ALU.mult),
             reads=[Bxa[s], Bcs], pwrites=[Bxa[s]])
        k.op("pool", lambda e: e.tensor_tensor(out=xb[s][:], in0=xb[s][:], in1=S[:, cs], op=ALU.mult),
             reads=[Bxb[s], Bcs], pwrites=[Bxb[s]])
        k.op("dve", lambda e: e.tensor_tensor(out=dst[:, cs], in0=xa[s][:], in1=xb[s][:], op=ALU.add),
             reads=[Bxa[s], Bxb[s]], writes=[Bdst] if hf == 0 else [], pwrites=[] if hf == 0 else [Bdst])


def build_att_even(lam_init, nfox=8, ndiff=4):
    nc = bass.Bass("TRN2", target_bir_lowering=False)
    dt_ = lambda n, sh: nc.dram_tensor(n, sh, F32, kind="ExternalInput").ap()
    fqT = dt_("fqT", [8, 128, T]); fkT = dt_("fkT", [8, 128, T]); fv = dt_("fv", [8, T, 128])
    fg = dt_("fg", [T, 8]); bfr = dt_("bfr", [128, 128])
    dqT = dt_("dqT", [8, 128, T]); dqTs = dt_("dqTs", [8, 128, T])
    dkT = dt_("dkT", [8, 128, T]); dkTs = dt_("dkTs", [8, 128, T]); dv = dt_("dv", [4, T, 256])
    ropeC = dt_("ropeC", [128, T]); ropeS = dt_("ropeS", [128, T])
    maskc = dt_("maskc", [128, 4 * 512]); ident = dt_("ident", [128, 128])
    triu = dt_("triu", [128, 128]); selh = dt_("selh", [8, 8 * 128])
    lamv = dt_("lamv", [128, 4 * 128]); subg = dt_("subg", [128, 256])
    o = nc.dram_tensor("o", [T, 2048], F32, kind="ExternalOutput").ap()
    with ExitStack() as es:
        k = K(nc, es)
        att = Att(k)
        pm = [k.ps(f"pm{i}", [128, 512]) for i in range(2)]
        Bpm = [Buf(f"pm{i}") for i in range(2)]
        idf = k.sb("idf", [128, 128], F32); Bidf = Buf("idf")
        idb = k.sb("idb", [128, 128], BF16); Bidb = Buf("idb")
        tri = k.sb("tri", [128, 128], F32); Btri = Buf("tri")
        onesf = k.sb("onesf", [128, 128], F32); Bones = Buf("onesf")
        mk = k.sb("mk", [128, 4, 512], BF16); Bmk = Buf("mk")
        C = k.sb("C", [128, T], F32); S = k.sb("S", [128, T], F32); Bcs = Buf("cs")
        sh = k.sb("sh", [8, 8, 128], F32); Bsh = Buf("sh")
        k.dma("sp", idf[:], ident, writes=[Bidf])
        _cast_dma(k, idb[:], ident, 128, writes=[Bidb])
        k.dma("sp", tri[:], triu, writes=[Btri])
        k.op("dve", lambda e: e.memset(onesf[:], 1.0), writes=[Bones])
        _cast_dma(k, mk[:], maskc.rearrange("p (m n) -> p m n", m=4), 512, writes=[Bmk])
        k.dma("sp", C[:], ropeC, writes=[Bcs])
        k.dma("sp", S[:], ropeS, pwrites=[Bcs])
        k.dma("sp", sh[:], selh.rearrange("k (h m) -> k h m", h=8), writes=[Bsh])
        lv = k.sb("lv", [128, 4, 128], F32); Blv = Buf("lv")
        lt = k.sb("lt", [128, 2, 128], F32); Blt = Buf("lt")
        ls = k.sb("ls", [128, 8], F32); Bls = Buf("ls")
        k.dma("sp", lv[:], lamv.rearrange("p (a n) -> p a n", a=4), writes=[Blv])
        for i in range(2):
            k.op("dve", lambda e: e.tensor_tensor(out=lt[:, i, :], in0=lv[:, 2 * i, :], in1=lv[:, 2 * i + 1, :],
                                                  op=ALU.mult), reads=[Blv], pwrites=[Blt])
        k.op("dve", lambda e: e.tensor_reduce(out=ls[:, 0:2], in_=lt[:], axis=AX.X, op=ALU.add),
             reads=[Blt], writes=[Bls])
        k.op("act", lambda e: e.activation(out=ls[:, 2:4], in_=ls[:, 0:2], func=AF.Exp), reads=[Bls], pwrites=[Bls])
        k.op("dve", lambda e: e.scalar_tensor_tensor(out=ls[:, 4:5], in0=ls[:, 3:4], scalar=-float(lam_init),
                                                     in1=ls[:, 2:3], op0=ALU.add, op1=ALU.subtract),
             reads=[Bls], pwrites=[Bls])
        sg_ = k.sb("subgs", [128, 256], F32); Bsg_ = Buf("subg")
        k.dma("sp", sg_[:], subg, writes=[Bsg_])
        fgs = k.sb("fgs", [128, NTT, 8], F32); Bfg = Buf("fgs")
        bfs = k.sb("bfs", [128, NTT, 8], F32); Bbf = Buf("bfs")
        cn = k.sb("cn", [128, NTT, 8], F32); Bcn = Buf("cn")
        off = k.sb("off", [128, NTT, 8], F32); Boff = Buf("off")
        cnh = k.sb("cnh", [128, 8, NTT], F32); Bcnh = Buf("cnh")
        cnT = k.sb("cnT", [8, T], F32); BcnT = Buf("cnT")
        k.dma("sp", fgs[:], fg.rearrange("(t p) h -> p t h", p=128), writes=[Bfg])
        k.dma("sp", bfs[:], bfr.rearrange("p (t h) -> p t h", h=8), writes=[Bbf])
        k.op("dve", lambda e: e.tensor_tensor(out=fgs[:], in0=fgs[:], in1=bfs[:], op=ALU.add),
             reads=[Bfg, Bbf], pwrites=[Bfg])
        k.op("act", lambda e: e.activation(out=fgs[:], in_=fgs[:], func=AF.Exp, scale=-1.0), reads=[Bfg], pwrites=[Bfg])
        k.op("act", lambda e: e.activation(out=fgs[:], in_=fgs[:], func=AF.Ln, bias=1.0, scale=1.0),
             reads=[Bfg], pwrites=[Bfg])
        fl = fgs[:].rearrange("p t h -> p (t h)")
        k.op("pe", lambda e: e.matmul(pm[0][:, 0:128], lhsT=tri[:], rhs=fl, start=True, stop=True),
             reads=[Btri, Bfg], writes=[Bpm[0]])
        k.op("pe", lambda e: e.matmul(pm[1][:, 0:128], lhsT=onesf[:], rhs=fl, start=True, stop=True),
             reads=[Bones, Bfg], writes=[Bpm[1]])
        k.op("dve", lambda e: e.memset(off[:, 0, :], 0.0), writes=[Boff])
        k.op("act", lambda e: e.copy(out=cn[:].rearrange("p t h -> p (t h)"), in_=pm[1][:, 0:128]),
             reads=[Bpm[1]], writes=[Bcn])
        for i in range(1, NTT):
            k.op("dve", lambda e: e.tensor_tensor(out=off[:, i, :], in0=off[:, i - 1, :], in1=cn[:, i - 1, :],
                                                  op=ALU.add), reads=[Boff, Bcn], pwrites=[Boff])
        k.op("dve", lambda e: e.tensor_tensor(out=cn[:].rearrange("p t h -> p (t h)"), in0=pm[0][:, 0:128],
                                              in1=off[:].rearrange("p t h -> p (t h)"), op=ALU.add),
             reads=[Bpm[0], Boff], writes=[Bcn])
        k.op("dve", lambda e: e.tensor_copy(out=cnh[:], in_=cn[:].rearrange("p t h -> p h t")),
             reads=[Bcn], writes=[Bcnh])
        for i4 in range(4):
            p = pm[i4 % 2]; Bp_ = Bpm[i4 % 2]
            for j in range(4):
                i = i4 * 4 + j
                k.op("pe", lambda e: e.transpose(out=p[0:8, j * 128:(j + 1) * 128], in_=cn[:, i, :], identity=idf[:]),
                     reads=[Bcn, Bidf], writes=[Bp_] if j == 0 else [], pwrites=[] if j == 0 else [Bp_])
            k.op("act", lambda e: e.mul(out=cnT[:, i4 * 512:(i4 + 1) * 512], in_=p[0:8, :], mul=-1.0),
                 reads=[Bp_], pwrites=[BcnT])
        qb = [k.sb(f"qb{i}", [128, T], BF16) for i in range(2)]; Bqb = [Buf(f"qb{i}") for i in range(2)]
        kb = [k.sb(f"kb{i}", [128, T], BF16) for i in range(2)]; Bkb = [Buf(f"kb{i}") for i in range(2)]
        v1 = [k.sb(f"v1{i}", [128, NTT, 257], BF16) for i in range(2)]; Bv1 = [Buf(f"v1{i}") for i in range(2)]
        qbias = k.sb("qbias", [128, T], F32); Bqbias = Buf("qbias")
        tmp = [k.sb(f"tmp{i}", [128, 512], F32) for i in range(2)]; Btmp = [Buf(f"tmp{i}") for i in range(2)]
        ob = [k.sb(f"ob{i}", [128, NTT, 256], F32) for i in range(2)]; Bob = [Buf(f"ob{i}") for i in range(2)]
        rsb = k.sb("rsb", [128, 64], F32); Brs = Buf("rsb")
        xa = [k.sb(f"xa{i}", [128, 1024], F32) for i in range(2)]; Bxa = [Buf(f"xa{i}") for i in range(2)]
        xb = [k.sb(f"xb{i}", [128, 1024], F32) for i in range(2)]; Bxb = [Buf(f"xb{i}") for i in range(2)]
        junk = k.sb("junk", [128, 256], F32); Bjunk = Buf("junk")
        nst = k.sb("nst", [128, 4, NTT], F32); Bnst = Buf("nst")
        ctr = [0]
        plan = causal_plan()
        rc = [0]

        def norm_cb(obuf, Bobuf, VW):
            def cb(g, qt, po, Bpo):
                qi = 4 * g + qt
                c = rc[0] % 64; rc[0] += 1
                k.op("dve", lambda e: e.reciprocal(out=rsb[:, c:c + 1], in_=po[:, VW:VW + 1]),
                     reads=[Bpo], pwrites=[Brs])
                if qi % 2 == 0:
                    k.op("dve", lambda e: e.tensor_scalar(out=obuf[:, qi, 0:VW], in0=po[:, 0:VW], scalar1=rsb[:, c:c + 1],
                                                          scalar2=None, op0=ALU.mult),
                         reads=[Bpo, Brs], pwrites=[Bobuf])
                else:
                    k.op("act", lambda e: e.activation(out=obuf[:, qi, 0:VW], in_=po[:, 0:VW], func=AF.Copy,
                                                       scale=rsb[:, c:c + 1]),
                         reads=[Bpo, Brs], pwrites=[Bobuf])
            return cb

        for h in range(nfox):
            s = h % 2
            _cast_dma(k, qb[s][:], fqT[h], T, writes=[Bqb[s]])
            _cast_dma(k, kb[s][:], fkT[h], T, writes=[Bkb[s]])
            _cast_dma(k, v1[s][:, :, 0:128], fv[h].rearrange("(t p) d -> p t d", p=128), 128, writes=[Bv1[s]])
            k.op("pool", lambda e: e.memset(v1[s][:, :, 128:129], 1.0), pwrites=[Bv1[s]])
            for g in range(4):
                p = pm[g % 2]; Bp_ = Bpm[g % 2]
                k.op("pe", lambda e: e.matmul(p[:], lhsT=sh[:, h, :], rhs=cnT[:, g * 512:(g + 1) * 512],
                                              start=True, stop=True), reads=[Bsh, BcnT], writes=[Bp_])
                k.op("act", lambda e: e.copy(out=qbias[:, g * 512:(g + 1) * 512], in_=p[:]),
                     reads=[Bp_], writes=[Bqbias] if g == 0 else [], pwrites=[] if g == 0 else [Bqbias])
            k.op("dve", lambda e: e.memset(ob[s][:, 0, 0:1], 0.0), writes=[Bob[s]])
            att.run(kT=kb[s], BkT=Bkb[s], qT=qb[s], BqT=Bqb[s], v1=v1[s], Bv1=Bv1[s], VW=129, plan=plan,
                    idb=idb, Bidb=Bidb, masks=mk, Bmasks=Bmk, out_cb=norm_cb(ob[s], Bob[s], 128),
                    bias_k=cnh[:, h, :], Bbias=Bcnh, pre=(tmp, Btmp, qbias, Bqbias))
            k.dma("sp", o[:, h * 128:(h + 1) * 128].rearrange("(t p) d -> p t d", p=128), ob[s][:, :, 0:128],
                  reads=[Bob[s]], owner=Bob[s])
        for hd in range(ndiff):
            vs_ = hd % 2
            _cast_dma(k, v1[vs_][:, :, 0:256], dv[hd].rearrange("(t p) d -> p t d", p=128), 256, writes=[Bv1[vs_]])
            k.op("pool", lambda e: e.memset(v1[vs_][:, :, 256:257], 1.0), pwrites=[Bv1[vs_]])
            for m in range(2):
                _rope(k, qb[m], Bqb[m], dqT[hd * 2 + m], dqTs[hd * 2 + m], C, S, Bcs, xa, Bxa, xb, Bxb, ctr)
                _rope(k, kb[m], Bkb[m], dkT[hd * 2 + m], dkTs[hd * 2 + m], C, S, Bcs, xa, Bxa, xb, Bxb, ctr)
                k.op("dve", lambda e: e.memset(ob[m][:, 0, 0:1], 0.0), writes=[Bob[m]])
                att.run(kT=kb[m], BkT=Bkb[m], qT=qb[m], BqT=Bqb[m], v1=v1[vs_], Bv1=Bv1[vs_], VW=257, plan=plan,
                        idb=idb, Bidb=Bidb, masks=mk, Bmasks=Bmk, out_cb=norm_cb(ob[m], Bob[m], 256))
            f0 = ob[0][:].rearrange("p t d -> p (t d)"); f1 = ob[1][:].rearrange("p t d -> p (t d)")
            k.op("dve", lambda e: e.scalar_tensor_tensor(out=f0, in0=f1, scalar=ls[:, 4:5], in1=f0,
                                                         op0=ALU.mult, op1=ALU.add),
                 reads=[Bob[1], Bob[0], Bls], writes=[Bob[0]])
            for qi in range(NTT):
                k.op("act", lambda e: e.activation(out=junk[:], in_=ob[0][:, qi, :], func=AF.Square,
                                                   accum_out=nst[:, 0, qi:qi + 1]),
                     reads=[Bob[0]], writes=[Bjunk], pwrites=[Bnst])
            k.op("dve", lambda e: e.tensor_scalar(out=nst[:, 1, :], in0=nst[:, 0, :], scalar1=1.0 / 256, scalar2=RMS_EPS,
                                                  op0=ALU.mult, op1=ALU.add), reads=[Bnst], pwrites=[Bnst])
            k.op("act", lambda e: e.activation(out=nst[:, 2, :], in_=nst[:, 1, :], func=AF.Sqrt), reads=[Bnst], pwrites=[Bnst])
            k.op("dve", lambda e: e.reciprocal(out=nst[:, 3, :], in_=nst[:, 2, :]), reads=[Bnst], pwrites=[Bnst])
            k.op("dve", lambda e: e.tensor_scalar(out=nst[:, 3, :], in0=nst[:, 3, :], scalar1=float(1.0 - lam_init),
                                                  scalar2=None, op0=ALU.mult), reads=[Bnst], pwrites=[Bnst])
            for qi in range(NTT):
                k.op("dve", lambda e: e.scalar_tensor_tensor(out=ob[1][:, qi, :], in0=ob[0][:, qi, :],
                                                             scalar=nst[:, 3, qi:qi + 1], in1=sg_[:],
                                                             op0=ALU.mult, op1=ALU.mult),
                     reads=[Bob[0], Bnst, Bsg_], writes=[Bob[1]] if qi == 0 else [], pwrites=[] if qi == 0 else [Bob[1]])
            k.dma("sp", o[:, 1024 + hd * 256:1024 + (hd + 1) * 256].rearrange("(t p) d -> p t d", p=128), ob[1][:],
                  reads=[Bob[1]], owner=Bob[1])
        k.finish(Bob)
        print(f"[att_even] ins={k.nins} waits={k.nwait} sems={k.nsem}")
    return nc


def _consts():
    c = {}
    c["ident"] = np.eye(128, dtype=np.float32)
    half = 64
    inv = (np.float32(10000.0) ** (-np.arange(half, dtype=np.float32) / np.float32(half))).astype(np.float32)
    ang = np.arange(T, dtype=np.float32)[:, None] * inv[None, :]
    cos = np.cos(ang).astype(np.float32).T; sin = np.sin(ang).astype(np.float32).T
    c["ropeC"] = np.ascontiguousarray(np.concatenate([cos, cos], 0))
    c["ropeS"] = np.ascontiguousarray(np.concatenate([-sin, sin], 0))
    ps = np.arange(128)[:, None]; tq = np.arange(512)[None, :]
    mc = np.stack([np.where(tq >= 128 * r + ps, 0.0, MNEG) for r in range(4)], 1)
    mw = np.stack([np.where(tq < 128 * r + ps, 0.0, MNEG) for r in range(4)], 1)
    c["maskc"] = np.ascontiguousarray(mc.reshape(128, 4 * 512).astype(np.float32))
    c["maskcw"] = np.ascontiguousarray(np.concatenate([mc, mw], 1).reshape(128, 8 * 512).astype(np.float32))
    c["triu"] = np.triu(np.ones((128, 128), np.float32))
    sh = np.zeros((8, 8, 128), np.float32)
    for h in range(8):
        sh[h, h, :] = 1.0
    c["selh"] = sh.reshape(8, 8 * 128)
    return c


def _swap_halves(a, axis):
    return np.concatenate(np.split(a, 2, axis=axis)[::-1], axis=axis)


def _even_inputs(proj_b, hh, p, c):
    fq, fk, fv, fgate, dq, dk, dv = np.split(proj_b, np.cumsum([2048, 2048, 2048, 16, 2048, 2048])[:].tolist(), axis=1)
    hs = slice(hh * 8, hh * 8 + 8); ds_ = slice(hh * 4, hh * 4 + 4)
    tr = lambda a: np.ascontiguousarray(a.reshape(T, 16, 128)[:, hs].transpose(1, 2, 0))
    dqT = dq.reshape(T, 8, 2, 128)[:, ds_].transpose(1, 2, 3, 0).reshape(8, 128, T)
    dkT = dk.reshape(T, 8, 2, 128)[:, ds_].transpose(1, 2, 3, 0).reshape(8, 128, T)
    ca = np.ascontiguousarray
    return {
        "fqT": tr(fq), "fkT": tr(fk), "fv": ca(fv.reshape(T, 16, 128)[:, hs].transpose(1, 0, 2)),
        "fg": ca(fgate[:, hs]), "bfr": ca(np.broadcast_to(np.tile(p["even_b_forget"][0][hs], NTT), (128, 128))),
        "dqT": ca(dqT), "dqTs": ca(_swap_halves(dqT, 1)), "dkT": ca(dkT), "dkTs": ca(_swap_halves(dkT, 1)),
        "dv": ca(dv.reshape(T, 8, 256)[:, ds_].transpose(1, 0, 2)),
        "ropeC": c["ropeC"], "ropeS": c["ropeS"], "maskc": c["maskc"], "ident": c["ident"],
        "triu": c["triu"], "selh": c["selh"],
        "lamv": ca(np.broadcast_to(np.concatenate([p["even_lambda_q1"][0], p["even_lambda_k1"][0],
                                                   p["even_lambda_q2"][0], p["even_lambda_k2"][0]]), (128, 512))),
        "subg": ca(np.broadcast_to(p["even_subln"][0], (128, 256))),
    }


NCMP = 127
NSA_NG = 2


def build_att_nsa(ngroups=2, nheads=8):
    nc = bass.Bass("TRN2", target_bir_lowering=False)
    dt_ = lambda n, sh: nc.dram_tensor(n, sh, F32, kind="ExternalInput").ap()
    NG = ngroups
    qT = dt_("qT", [8 * NG, 128, T]); qTs = dt_("qTs", [8 * NG, 128, T])
    kcb = dt_("kcb", [NG, 128, 32 * NCMP]); vcb = dt_("vcb", [NG, 128, 32 * NCMP])
    ksT = dt_("ksT", [NG, 128, T]); ksTs = dt_("ksTs", [NG, 128, T])
    kwT = dt_("kwT", [NG, 128, T]); kwTs = dt_("kwTs", [NG, 128, T])
    vs = dt_("vs", [NG, T, 128]); vw = dt_("vw", [NG, T, 128])
    gpre = dt_("gpre", [T, 24 * NG])
    w1k = dt_("w1k", [128, 32 * 256]); w1v = dt_("w1v", [128, 32 * 256])
    b1 = dt_("b1", [128, 4]); w2k = dt_("w2k", [128, 256]); w2v = dt_("w2v", [128, 256])
    pek = dt_("pek", [128, 32]); pev = dt_("pev", [128, 32])
    ropeC = dt_("ropeC", [128, T]); ropeS = dt_("ropeS", [128, T])
    maskcw = dt_("maskcw", [128, 8 * 512]); ident = dt_("ident", [128, 128])
    cmask = dt_("cmask", [128, T]); ovl = dt_("ovl", [128, 33])
    fconst = dt_("fconst", [128, NTT * 32]); esel = dt_("esel", [33, NTT * 128])
    o = nc.dram_tensor("o", [T, 1024 * NG], F32, kind="ExternalOutput").ap()
    with ExitStack() as es:
        k = K(nc, es)
        att = Att(k)
        pm = [k.ps(f"pm{i}", [128, 512]) for i in range(2)]
        Bpm = [Buf(f"pm{i}") for i in range(2)]
        idf = k.sb("idf", [128, 128], F32); Bidf = Buf("idf")
        idb = k.sb("idb", [128, 128], BF16); Bidb = Buf("idb")
        mk = k.sb("mk", [128, 8, 512], BF16); Bmk = Buf("mk")
        cmb = k.sb("cmb", [128, 1, T], BF16); Bcmb = Buf("cmb")
        C = k.sb("C", [128, T], F32); S = k.sb("S", [128, T], F32); Bcs = Buf("cs")
        fc = k.sb("fc", [128, NTT * 32], F32); Bfc = Buf("fc")
        eselb = k.sb("eselb", [33, NTT, 128], BF16); selT1 = k.sb("selT1", [33, T], BF16); Bsel = Buf("sel")
        vco = k.sb("vco", [128, 1, 161], BF16); Bvco = Buf("vco")
        gp = k.sb("gp", [128, NTT, 24 * NG], F32); Bgp = Buf("gp")
        b1s = k.sb("b1s", [128, 4], F32); Bb1 = Buf("b1s")
        pes = k.sb("pes", [128, 2, 32], F32); Bpe = Buf("pes")
        w2b = k.sb("w2b", [128, 2, 2, 128], BF16); Bw2 = Buf("w2b")
        k.dma("sp", idf[:], ident, writes=[Bidf])
        _cast_dma(k, idb[:], ident, 128, writes=[Bidb])
        _cast_dma(k, mk[:], maskcw.rearrange("p (m n) -> p m n", m=8), 512, writes=[Bmk])
        _cast_dma(k, cmb[:, 0, :], cmask, T, writes=[Bcmb])
        k.dma("sp", C[:], ropeC, writes=[Bcs])
        k.dma("sp", S[:], ropeS, pwrites=[Bcs])
        k.dma("sp", fc[:], fconst, writes=[Bfc])
        _cast_dma(k, eselb[:], esel.rearrange("j (t s) -> j t s", t=NTT), 128, writes=[Bsel])
        k.op("pool", lambda e: e.memset(selT1[32:33, :], 1.0), pwrites=[Bsel])
        _cast_dma(k, vco[:, 0, 128:161], ovl, 33, writes=[Bvco])
        k.dma("sp", gp[:], gpre.rearrange("(t p) c -> p t c", p=128), writes=[Bgp])
        k.op("act", lambda e: e.activation(out=gp[:], in_=gp[:], func=AF.Sigmoid), reads=[Bgp], pwrites=[Bgp])
        k.dma("sp", b1s[:], b1, writes=[Bb1])
        k.dma("sp", pes[:, 0, :], pek, writes=[Bpe])
        k.dma("sp", pes[:, 1, :], pev, pwrites=[Bpe])
        _cast_dma(k, w2b[:, 0, :, :], w2k.rearrange("p (c d) -> p c d", c=2), 128, writes=[Bw2])
        _cast_dma(k, w2b[:, 1, :, :], w2v.rearrange("p (c d) -> p c d", c=2), 128, pwrites=[Bw2])
        w1b = k.sb("w1b", [128, 32, 256], BF16); Bw1 = Buf("w1b")
        blk32 = [k.sb(f"blk32{i}", [128, 8, NCMP], F32) for i in range(2)]; Bblk32 = [Buf(f"blk32{i}") for i in range(2)]
        blkb = k.sb("blkb", [128, 32, NCMP], BF16); Bblkb = Buf("blkb")
        H1 = k.sb("H1", [128, 2, NCMP], BF16); BH1 = Buf("H1")
        kcm = k.sb("kcm", [128, 128], BF16); Bkcm = Buf("kcm")
        ksr = k.sb("ksr", [128, T], BF16); Bksr = Buf("ksr")
        kwr = k.sb("kwr", [128, T], BF16); Bkwr = Buf("kwr")
        v1s = k.sb("v1s", [128, NTT, 129], BF16); Bv1s = Buf("v1s")
        v1w = k.sb("v1w", [128, NTT, 129], BF16); Bv1w = Buf("v1w")
        qb = k.sb("qb", [128, T], BF16); Bqb = Buf("qb")
        qr = k.sb("qr", [128, T], BF16); Bqr = Buf("qr")
        xa = [k.sb(f"xa{i}", [128, 1024], F32) for i in range(2)]; Bxa = [Buf(f"xa{i}") for i in range(2)]
        xb = [k.sb(f"xb{i}", [128, 1024], F32) for i in range(2)]; Bxb = [Buf(f"xb{i}") for i in range(2)]
        acc = [k.sb(f"acc{i}", [128, NTT, 128], F32) for i in range(2)]; Bacc = [Buf(f"acc{i}") for i in range(2)]
        imp = k.sb("imp", [128, NTT, 32], F32); Bimp = Buf("imp")
        sc = k.sb("sc", [128, NTT, 32], F32); Bsc = Buf("sc")
        selm = k.sb("selm", [128, NTT, 32], F32); Bselm = Buf("selm")
        okm = k.sb("okm", [128, NTT * 32], F32); Bokm = Buf("okm")
        m8 = k.sb("m8", [128, 16], F32); Bm8 = Buf("m8")
        wk = k.sb("wk", [128, 32], F32); Bwk = Buf("wk")
        rsb = k.sb("rsb", [128, 128], F32); Brs = Buf("rsb")
        ctr = [0]; rc = [0]; bc = [0]
        cplan = [[(0, 0, 0, 3)] for _ in range(4)]
        caus = causal_plan(); wplan = window_plan()

        def compress(which, src, gi):
            _cast_dma(k, w1b[:], (w1k if which == 0 else w1v).rearrange("p (l h) -> p l h", l=32), 256, writes=[Bw1])
            for l8 in range(4):
                s = bc[0] % 2; bc[0] += 1
                k.dma("sp", blk32[s][:], src[gi][:, l8 * 8 * NCMP:(l8 + 1) * 8 * NCMP].rearrange("p (l n) -> p l n", l=8),
                      writes=[Bblk32[s]])
                for j in range(8):
                    l = l8 * 8 + j
                    k.op("dve", lambda e: e.tensor_scalar(out=blkb[:, l, :], in0=blk32[s][:, j, :],
                                                          scalar1=pes[:, which, l:l + 1], scalar2=None, op0=ALU.add),
                         reads=[Bblk32[s], Bpe], writes=[Bblkb] if l == 0 else [], pwrites=[] if l == 0 else [Bblkb])
            for hc in range(2):
                p = pm[hc]; Bp_ = Bpm[hc]
                for l in range(32):
                    k.op("pe", lambda e: e.matmul(p[:, 0:NCMP], lhsT=w1b[:, l, hc * 128:(hc + 1) * 128], rhs=blkb[:, l, :],
                                                  start=(l == 0), stop=(l == 31)),
                         reads=[Bw1, Bblkb], writes=[Bp_] if l == 0 else [], pwrites=[] if l == 0 else [Bp_])
                k.op("act", lambda e: e.activation(out=H1[:, hc, :], in_=p[:, 0:NCMP], func=AF.Silu,
                                                   bias=b1s[:, which * 2 + hc:which * 2 + hc + 1], scale=1.0),
                     reads=[Bp_, Bb1], writes=[BH1] if hc == 0 else [], pwrites=[] if hc == 0 else [BH1])
            p = pm[0]; Bp_ = Bpm[0]
            if which == 0:
                for hc in range(2):
                    k.op("pe", lambda e: e.matmul(p[:, 0:NCMP], lhsT=w2b[:, 0, hc, :], rhs=H1[:, hc, :],
                                                  start=(hc == 0), stop=(hc == 1)),
                         reads=[Bw2, BH1], writes=[Bp_] if hc == 0 else [], pwrites=[] if hc == 0 else [Bp_])
                k.op("act", lambda e: e.copy(out=kcm[:, 0:NCMP], in_=p[:, 0:NCMP]), reads=[Bp_], writes=[Bkcm])
            else:
                for hc in range(2):
                    k.op("pe", lambda e: e.matmul(p[0:NCMP, 0:128], lhsT=H1[:, hc, :], rhs=w2b[:, 1, hc, :],
                                                  start=(hc == 0), stop=(hc == 1)),
                         reads=[Bw2, BH1], writes=[Bp_] if hc == 0 else [], pwrites=[] if hc == 0 else [Bp_])
                k.op("act", lambda e: e.copy(out=vco[0:NCMP, 0, 0:128], in_=p[0:NCMP, 0:128]), reads=[Bp_], pwrites=[Bvco])

        def rs_of(po, Bpo, col, eps):
            c = rc[0] % 64; rc[0] += 1
            if eps:
                k.op("dve", lambda e: e.tensor_scalar(out=rsb[:, 64 + c:65 + c], in0=po[:, col:col + 1], scalar1=1e-30,
                                                      scalar2=None, op0=ALU.add), reads=[Bpo], pwrites=[Brs])
                k.op("dve", lambda e: e.reciprocal(out=rsb[:, c:c + 1], in_=rsb[:, 64 + c:65 + c]), reads=[Brs], pwrites=[Brs])
            else:
                k.op("dve", lambda e: e.reciprocal(out=rsb[:, c:c + 1], in_=po[:, col:col + 1]), reads=[Bpo], pwrites=[Brs])
            return c

        for gi in range(ngroups):
            if gi > 0:
                k.rotate()
            compress(0, kcb, gi)
            compress(1, vcb, gi)
            _rope(k, ksr, Bksr, ksT[gi], ksTs[gi], C, S, Bcs, xa, Bxa, xb, Bxb, ctr)
            _rope(k, kwr, Bkwr, kwT[gi], kwTs[gi], C, S, Bcs, xa, Bxa, xb, Bxb, ctr)
            _cast_dma(k, v1s[:, :, 0:128], vs[gi].rearrange("(t p) d -> p t d", p=128), 128, writes=[Bv1s])
            k.op("pool", lambda e: e.memset(v1s[:, :, 128:129], 1.0), pwrites=[Bv1s])
            _cast_dma(k, v1w[:, :, 0:128], vw[gi].rearrange("(t p) d -> p t d", p=128), 128, writes=[Bv1w])
            k.op("pool", lambda e: e.memset(v1w[:, :, 128:129], 1.0), pwrites=[Bv1w])
            for r in range(nheads):
                h = gi * 8 + r
                _cast_dma(k, qb[:], qT[h], T, writes=[Bqb])

                def cb1(g, qt, po, Bpo, r=r):
                    qi = 4 * g + qt
                    c = rs_of(po, Bpo, 0, True)
                    if r == 0:
                        k.op("dve", lambda e: e.tensor_scalar(out=imp[:, qi, :], in0=po[:, 1:33], scalar1=rsb[:, c:c + 1],
                                                              scalar2=None, op0=ALU.mult),
                             reads=[Bpo, Brs], writes=[Bimp] if qi == 0 else [], pwrites=[] if qi == 0 else [Bimp])
                    else:
                        k.op("dve", lambda e: e.scalar_tensor_tensor(out=imp[:, qi, :], in0=po[:, 1:33], scalar=rsb[:, c:c + 1],
                                                                     in1=imp[:, qi, :], op0=ALU.mult, op1=ALU.add),
                             reads=[Bpo, Brs, Bimp], pwrites=[Bimp])
                att.run(kT=kcm, BkT=Bkcm, qT=qb, BqT=Bqb, v1=vco[:, :, 128:161], Bv1=Bvco, VW=33, plan=cplan,
                        idb=idb, Bidb=Bidb, masks=cmb, Bmasks=Bcmb, out_cb=cb1, KP=NCMP, rhs_all=True)
            k.op("dve", lambda e: e.tensor_tensor(out=sc[:].rearrange("p t j -> p (t j)"),
                                                  in0=imp[:].rearrange("p t j -> p (t j)"), in1=fc[:], op=ALU.add),
                 reads=[Bimp, Bfc], writes=[Bsc])
            for i in range(NTT):
                k.op("dve", lambda e: e.max(out=m8[:, 0:8], in_=sc[:, i, :]), reads=[Bsc], writes=[Bm8])
                k.op("dve", lambda e: e.match_replace(out=wk[:], in_to_replace=m8[:, 0:8], in_values=sc[:, i, :],
                                                      imm_value=-3e9), reads=[Bsc, Bm8], writes=[Bwk])
                k.op("dve", lambda e: e.max(out=m8[:, 8:16], in_=wk[:]), reads=[Bwk], pwrites=[Bm8])
                k.op("dve", lambda e: e.tensor_scalar(out=selm[:, i, :], in0=sc[:, i, :], scalar1=m8[:, 15:16], scalar2=None,
                                                      op0=ALU.is_ge), reads=[Bsc, Bm8],
                     writes=[Bselm] if i == 0 else [], pwrites=[] if i == 0 else [Bselm])
            k.op("dve", lambda e: e.tensor_scalar(out=okm[:], in0=sc[:].rearrange("p t j -> p (t j)"), scalar1=-5e8,
                                                  scalar2=None, op0=ALU.is_gt), reads=[Bsc], writes=[Bokm])
            k.op("dve", lambda e: e.tensor_tensor(out=selm[:].rearrange("p t j -> p (t j)"),
                                                  in0=selm[:].rearrange("p t j -> p (t j)"), in1=okm[:], op=ALU.mult),
                 reads=[Bselm, Bokm], pwrites=[Bselm])
            for i4 in range(4):
                p = pm[i4 % 2]; Bp_ = Bpm[i4 % 2]
                for j in range(4):
                    i = i4 * 4 + j
                    k.op("pe", lambda e: e.transpose(out=p[0:32, j * 128:(j + 1) * 128], in_=selm[:, i, :], identity=idf[:]),
                         reads=[Bselm, Bidf], writes=[Bp_] if j == 0 else [], pwrites=[] if j == 0 else [Bp_])
                k.op("act", lambda e: e.copy(out=selT1[0:32, i4 * 512:(i4 + 1) * 512], in_=p[0:32, :]),
                     reads=[Bp_], pwrites=[Bsel])
            for r in range(nheads):
                h = gi * 8 + r
                a = acc[h % 2]; Ba = Bacc[h % 2]
                _cast_dma(k, qb[:], qT[h], T, writes=[Bqb])
                _rope(k, qr, Bqr, qT[h], qTs[h], C, S, Bcs, xa, Bxa, xb, Bxb, ctr)

                def mk_cb(br, first, eps, a=a, Ba=Ba, h=h):
                    def cb(g, qt, po, Bpo):
                        qi = 4 * g + qt
                        c = rs_of(po, Bpo, 128, eps)
                        k.op("dve", lambda e: e.tensor_tensor(out=rsb[:, 64 + c:65 + c], in0=rsb[:, c:c + 1],
                                                              in1=gp[:, qi, h * 3 + br:h * 3 + br + 1], op=ALU.mult),
                             reads=[Brs, Bgp], pwrites=[Brs])
                        if first:
                            k.op("act", lambda e: e.activation(out=a[:, qi, :], in_=po[:, 0:128], func=AF.Copy,
                                                               scale=rsb[:, 64 + c:65 + c]),
                                 reads=[Bpo, Brs], writes=[Ba] if qi == 0 else [], pwrites=[] if qi == 0 else [Ba])
                        else:
                            k.op("dve", lambda e: e.scalar_tensor_tensor(out=a[:, qi, :], in0=po[:, 0:128],
                                                                         scalar=rsb[:, 64 + c:65 + c], in1=a[:, qi, :],
                                                                         op0=ALU.mult, op1=ALU.add),
                                 reads=[Bpo, Brs, Ba], pwrites=[Ba])
                    return cb
                att.run(kT=kcm, BkT=Bkcm, qT=qb, BqT=Bqb, v1=vco[:, :, 0:129], Bv1=Bvco, VW=129, plan=cplan,
                        idb=idb, Bidb=Bidb, masks=cmb, Bmasks=Bcmb, out_cb=mk_cb(0, True, True), KP=NCMP, rhs_all=True)
                att.run(kT=ksr, BkT=Bksr, qT=qr, BqT=Bqr, v1=v1s, Bv1=Bv1s, VW=129, plan=caus,
                        idb=idb, Bidb=Bidb, masks=mk, Bmasks=Bmk, out_cb=mk_cb(1, False, False),
                        sel=(eselb, selT1, Bsel))
                att.run(kT=kwr, BkT=Bkwr, qT=qr, BqT=Bqr, v1=v1w, Bv1=Bv1w, VW=129, plan=wplan,
                        idb=idb, Bidb=Bidb, masks=mk, Bmasks=Bmk, out_cb=mk_cb(2, False, False))
                k.dma("sp", o[:, h * 128:(h + 1) * 128].rearrange("(t p) d -> p t d", p=128), a[:],
                      reads=[Ba], owner=Ba)
        k.finish(Bacc)
        print(f"[att_nsa] ins={k.nins} waits={k.nwait} sems={k.nsem}")
    return nc


def _nsa_consts():
    c = {}
    n = np.arange(128)[:, None]; t = np.arange(T)[None, :]
    cm = np.where((t >= 16 * n + 31) & (n < NCMP), 0.0, MNEG).astype(np.float32)
    c["cmask"] = np.ascontiguousarray(cm)
    cmp_start = np.arange(NCMP) * 16; sel_start = np.arange(32) * 64
    ov = ((cmp_start[:, None] < sel_start[None, :] + 64) & (cmp_start[:, None] + 32 > sel_start[None, :])).astype(np.float32)
    ovl = np.zeros((128, 33), np.float32); ovl[:, 0] = 1.0; ovl[:NCMP, 1:] = ov
    c["ovl"] = ovl
    tt = np.arange(T); blk = tt // 64; j = np.arange(32)[None, :]
    valid = j <= blk[:, None]
    forced = (j == 0) | (j == blk[:, None]) | (j == blk[:, None] - 1)
    fcst = np.where(valid, np.where(forced, 1e9, 0.0), -1e9).astype(np.float32)
    c["fconst"] = np.ascontiguousarray(fcst.reshape(NTT, 128, 32).transpose(1, 0, 2).reshape(128, NTT * 32))
    es_ = np.zeros((33, NTT, 128), np.float32)
    for kt in range(NTT):
        es_[2 * kt, kt, 0:64] = -MNEG
        es_[2 * kt + 1, kt, 64:128] = -MNEG
    es_[32, :, :] = MNEG
    c["esel"] = es_.reshape(33, NTT * 128)
    return c


def _nsa_inputs(proj_b, g0, ng, p, c, cn):
    ca = np.ascontiguousarray
    q, kc, vc, ks, vs, kw, vw, gates = np.split(proj_b, np.cumsum([4096] + [512] * 6).tolist(), axis=1)
    hs = slice(g0 * 8, (g0 + ng) * 8); gs_ = slice(g0, g0 + ng)
    qT = q.reshape(T, 32, 128)[:, hs].transpose(1, 2, 0)
    grp = lambda a: a.reshape(T, 4, 128)[:, gs_].transpose(1, 0, 2)
    idx = 16 * np.arange(NCMP)[None, :] + np.arange(32)[:, None]
    blk = lambda a: ca(grp(a)[:, idx].transpose(0, 3, 1, 2).reshape(ng, 128, 32 * NCMP))
    trT = lambda a: grp(a).transpose(0, 2, 1)
    w1l = lambda w: ca(w.reshape(32, 128, 256).transpose(1, 0, 2).reshape(128, 32 * 256))
    w2l = lambda w: ca(w.reshape(2, 128, 128).transpose(1, 0, 2).reshape(128, 256))
    b1 = np.concatenate([p["odd_cmp_k_b1"][0].reshape(2, 128).T, p["odd_cmp_v_b1"][0].reshape(2, 128).T], 1)
    return {
        "qT": ca(qT), "qTs": ca(_swap_halves(qT, 1)),
        "kcb": blk(kc), "vcb": blk(vc),
        "ksT": ca(trT(ks)), "ksTs": ca(_swap_halves(trT(ks), 1)),
        "kwT": ca(trT(kw)), "kwTs": ca(_swap_halves(trT(kw), 1)),
        "vs": ca(grp(vs)), "vw": ca(grp(vw)),
        "gpre": ca(gates[:, g0 * 24:(g0 + ng) * 24]),
        "w1k": w1l(p["odd_cmp_k_w1"][0]), "w1v": w1l(p["odd_cmp_v_w1"][0]), "b1": ca(b1),
        "w2k": w2l(p["odd_cmp_k_w2"][0]), "w2v": w2l(p["odd_cmp_v_w2"][0]),
        "pek": ca(p["odd_cmp_pe_k"][0].T), "pev": ca(p["odd_cmp_pe_v"][0].T),
        "ropeC": c["ropeC"], "ropeS": c["ropeS"], "maskcw": c["maskcw"], "ident": c["ident"],
        "cmask": cn["cmask"], "ovl": cn["ovl"], "fconst": cn["fconst"], "esel": cn["esel"],
    }


_PROGS = {}


def _prog(key, fn):
    if key not in _PROGS:
        _PROGS[key] = fn()
    return _PROGS[key]


def _launch(nc, in_maps, out_name):
    res = run_bass_kernel_spmd(nc, in_maps, core_ids=list(range(NCORES)))
    return [r[out_name] for r in res.results]


def _run_chain(key, stages, x0, feeds, c):
    nc = _prog(key, lambda: build_chain(stages))
    ims = []
    for i in range(NCORES):
        m = {"ident": c["ident"], "x0": np.ascontiguousarray(x0[i * 1024:(i + 1) * 1024])}
        for name, (arr, sharded) in feeds.items():
            m[name] = np.ascontiguousarray(arr[i * 1024:(i + 1) * 1024]) if sharded else arr
        ims.append(m)
    res = run_bass_kernel_spmd(nc, ims, core_ids=list(range(NCORES)))
    return lambda name: np.concatenate([r[name] for r in res.results], 0)


def _ffn_feeds(i, p, which, layer):
    return {f"s{i}_gT": (_gT(p[f"{which}_norm"][layer]), False),
            f"s{i}_wg": (_ffn_weight_layout(p[f"{which}_w_gate"][layer]), False),
            f"s{i}_wu": (_ffn_weight_layout(p[f"{which}_w_up"][layer]), False),
            f"s{i}_wd": (np.ascontiguousarray(p[f"{which}_w_down"][layer]), False)}


def kernel(**inp):
    p = {k_: np.asarray(v, dtype=np.float32) for k_, v in inp.items()}
    c = _consts(); cn = _nsa_consts()
    B = 4
    ones_gT = np.ones((128, DC), np.float32)
    x = p["x"].reshape(B * T, D)
    feeds = _ffn_feeds(0, p, "ffn1", 0)
    feeds.update({"s1_gT": (_gT(p["mix_norm"][0]), False), "s1_w": (np.ascontiguousarray(p["even_w_in"][0]), False)})
    out = _run_chain("A", [("ffn",), ("lin", 12304, True, False)], x, feeds, c)
    h = out("s0_y"); proj = out("s1_y")
    lam_init = 0.8 - 0.6 * math.exp(-0.3 * 0)
    nc = _prog("even", lambda: build_att_even(lam_init))
    ims = [_even_inputs(proj[(i // 2) * T:(i // 2 + 1) * T], i % 2, p, c) for i in range(NCORES)]
    outs = _launch(nc, ims, "o")
    attn = np.empty((B * T, D), np.float32)
    for i in range(NCORES):
        b, hh = i // 2, i % 2
        attn[b * T:(b + 1) * T, hh * 1024:(hh + 1) * 1024] = outs[i][:, 0:1024]
        attn[b * T:(b + 1) * T, 2048 + hh * 1024:2048 + (hh + 1) * 1024] = outs[i][:, 1024:2048]
    del proj, ims, outs, out, feeds
    feeds = {"s0_gT": (ones_gT, False), "s0_w": (np.ascontiguousarray(p["even_w_out"][0]), False), "s0_res": (h, True)}
    feeds.update(_ffn_feeds(1, p, "ffn2", 0))
    out = _run_chain("C1", [("lin", 4096, False, True), ("ffn",)], attn, feeds, c)
    h = out("s1_y")
    del feeds, out
    feeds = _ffn_feeds(0, p, "ffn1", 1)
    feeds.update({"s1_gT": (_gT(p["mix_norm"][1]), False), "s1_w": (np.ascontiguousarray(p["odd_w_in"][0]), False)})
    out = _run_chain("C2", [("ffn",), ("lin", 7264, True, False)], h, feeds, c)
    h = out("s0_y"); proj = out("s1_y")
    del feeds, out
    nc = _prog("nsa", lambda: build_att_nsa(ngroups=NSA_NG))
    for L in range(2 // NSA_NG):
        ims = [_nsa_inputs(proj[(i // 2) * T:(i // 2 + 1) * T], (L * 2 + i % 2) if NSA_NG == 1 else 2 * (i % 2),
                           NSA_NG, p, c, cn) for i in range(NCORES)]
        outs = _launch(nc, ims, "o")
        for i in range(NCORES):
            b = i // 2
            g0 = (L * 2 + i % 2) if NSA_NG == 1 else 2 * (i % 2)
            attn[b * T:(b + 1) * T, g0 * 1024:(g0 + NSA_NG) * 1024] = outs[i]
    del proj, ims, outs
    feeds = {"s0_gT": (ones_gT, False), "s0_w": (np.ascontiguousarray(p["odd_w_out"][0]), False), "s0_res": (h, True)}
    feeds.update(_ffn_feeds(1, p, "ffn2", 1))
    feeds.update({"s2_gR": (np.ascontiguousarray(np.broadcast_to(p["final_norm"], (128, D))), False)})
    out = _run_chain("E", [("lin", 4096, False, True), ("ffn",), ("fnorm",)], attn, feeds, c)
    return out("s2_y").reshape(B, T, D).astype(np.float32)
```

```python
import math
import numpy as np
import concourse.bass as bass
import concourse.mybir as mybir
from concourse.bass_utils import run_bass_kernel_spmd
from contextlib import ExitStack

F32 = mybir.dt.float32
BF16 = mybir.dt.bfloat16
I32 = mybir.dt.int32
AF = mybir.ActivationFunctionType
ALU = mybir.AluOpType
AX = mybir.AxisListType

D = 4096
DC = D // 128
DFF = 11008
NFF = DFF // 128
NCORES = 8
RMS_EPS = 1e-6


class Buf:
    __slots__ = ("name", "w", "r", "dsem", "dcnt")

    def __init__(self, name):
        self.name = name
        self.w = {}
        self.r = {}
        self.dsem = None
        self.dcnt = 0


class K:
    def __init__(self, nc, es):
        self.nc = nc
        self.es = es
        self.engs = {"pe": nc.tensor, "dve": nc.vector, "act": nc.scalar,
                     "pool": nc.gpsimd, "sp": nc.sync}
        self.sem = {k: es.enter_context(nc.semaphore("s_" + k))
                    for k in ["pe", "dve", "act", "pool"]}
        self.cnt = {k: 0 for k in self.sem}
        self.waited = {k: {} for k in self.engs}
        self.nsem = 0
        self.nwait = 0
        self.nins = 0
        self.pend = {k: [] for k in self.engs}
        self.es_outer = es
        self.pfx = ""
        self.dsems = []

    def sb(self, name, shape, dt):
        return self.es.enter_context(self.nc.sbuf_tensor(self.pfx + name, shape, dt))

    def ps(self, name, shape, dt=F32):
        return self.es.enter_context(self.nc.psum_tensor(self.pfx + name, shape, dt))

    def newsem(self, name):
        self.nsem += 1
        return self.es_outer.enter_context(self.nc.semaphore(self.pfx + name))

    def rotate(self):
        self.barrier()
        self.nrot = getattr(self, "nrot", 0) + 1
        for e in list(self.sem):
            self.sem[e] = self.es_outer.enter_context(self.nc.semaphore(f"{self.pfx}s_{e}_r{self.nrot}"))
            self.cnt[e] = 0

    def barrier(self):
        evs = [(self.sem[e], self.cnt[e]) for e in self.sem if self.cnt[e] > 0]
        evs += [(b.dsem, b.dcnt) for b in self.dsems if b.dcnt > 0]
        for eng in self.engs:
            assert not self.pend[eng]
            wd = self.waited[eng]
            for (sm, v) in evs:
                if wd.get(sm.num, -1) >= v:
                    continue
                self.engs[eng].wait_ge(sm, v)
                self.nwait += 1
                wd[sm.num] = v

    def _deps(self, eng, reads, writes):
        need = {}

        def add(ev, kind):
            s, v = ev
            key = s.num
            if eng in self.sem and s.num == self.sem[eng].num:
                if eng == "pe" or kind != "raw":
                    return
            if key not in need or need[key][1] < v:
                need[key] = (s, v)

        for b in reads:
            for ev in b.w.values():
                add(ev, "raw")
        for b in writes:
            for ev in b.w.values():
                add(ev, "waw")
            for ev in b.r.values():
                add(ev, "war")
        e = self.engs[eng]
        wd = self.waited[eng]
        for key, (s, v) in need.items():
            if wd.get(key, -1) >= v:
                continue
            e.wait_ge(s, v)
            self.nwait += 1
            wd[key] = v

    def _record(self, ev, reads, writes, pwrites=()):
        key = ev[0].num
        for b in reads:
            b.r[key] = ev
        for b in writes:
            b.w = {key: ev}
            b.r = {}
        for b in pwrites:
            b.w[key] = ev

    def op(self, eng, fn, reads=(), writes=(), pwrites=(), inc=True):
        self._deps(eng, reads, list(writes) + list(pwrites))
        ins = fn(self.engs[eng])
        self.nins += 1
        if not inc:
            self.pend[eng].append((list(reads), list(writes), list(pwrites)))
            return ins
        self.cnt[eng] += 1
        ins.then_inc(self.sem[eng], 1)
        ev = (self.sem[eng], self.cnt[eng])
        for (r_, w_, pw_) in self.pend[eng]:
            self._record(ev, r_, w_, pw_)
        self.pend[eng] = []
        self._record(ev, reads, writes, pwrites)
        return ins

    def dma(self, q, out, in_, reads=(), writes=(), pwrites=(), owner=None, **kw):
        allw = list(writes) + list(pwrites)
        self._deps(q, reads, allw)
        if owner is None:
            owner = allw[0] if allw else reads[0]
        if owner.dsem is None:
            owner.dsem = self.newsem("d_" + owner.name)
            self.dsems.append(owner)
        ins = self.engs[q].dma_start(out=out, in_=in_, **kw)
        self.nins += 1
        owner.dcnt += 16
        ins.then_inc(owner.dsem, 16)
        ev = (owner.dsem, owner.dcnt)
        self._record(ev, reads, writes, pwrites)
        return ins

    def finish(self, bufs, eng="sp"):
        self._deps(eng, bufs, bufs)


def _cast_dma(k, dst, src, ncols, **kw):
    k.dma("pool", dst, src, max_dma_last_dim=2048, **kw)


def build_ffn(NT=1024, nff=NFF):
    nc = bass.Bass("TRN2", target_bir_lowering=False)
    x = nc.dram_tensor("x", [NT, D], F32, kind="ExternalInput").ap()
    gT = nc.dram_tensor("gT", [128, DC], F32, kind="ExternalInput").ap()
    ident = nc.dram_tensor("ident", [128, 128], F32, kind="ExternalInput").ap()
    wg = nc.dram_tensor("wg", [nff, 128, D], F32, kind="ExternalInput").ap()
    wu = nc.dram_tensor("wu", [nff, 128, D], F32, kind="ExternalInput").ap()
    wd = nc.dram_tensor("wd", [nff * 128, D], F32, kind="ExternalInput").ap()
    y = nc.dram_tensor("y", [NT, D], F32, kind="ExternalOutput").ap()
    with ExitStack() as es:
        k = K(nc, es)
        emit_ffn(k, x, gT, ident, wg, wu, wd, y, NT, nff)
    return nc


def emit_ffn(k, x, gT, ident, wg, wu, wd, y, NT=1024, nff=NFF):
    TB = 512
    NB = NT // TB
    with ExitStack() as ph:
        k.es = ph
        gs = k.sb("gs", [128, DC], F32); Bg = Buf("gs")
        ids = k.sb("ids", [128, 128], F32); Bid = Buf("ids")
        xs = [k.sb("xs0", [128, D], F32)] * 2
        Bxs = [Buf("xs0")] * 2
        st = [k.sb(f"st{i}", [128, 4], F32) for i in range(2)]
        Bst = [Buf(f"st{i}") for i in range(2)]
        junk = k.sb("junk", [128, D], BF16); Bjunk = Buf("junk")
        xnT = k.sb("xnT", [128, DC, TB], BF16); BxnT = Buf("xnT")
        hT = k.sb("hT", [128, nff, TB], BF16); BhT = Buf("hT")
        wgb = [k.sb(f"wgb{i}", [128, D], BF16) for i in range(2)]
        wub = [k.sb(f"wub{i}", [128, D], BF16) for i in range(2)]
        Bwg = [Buf(f"wgb{i}") for i in range(2)]
        Bwu = [Buf(f"wub{i}") for i in range(2)]
        wdb = [k.sb(f"wdb{i}", [128, 8, 512], BF16) for i in range(2)]
        Bwd = [Buf(f"wdb{i}") for i in range(2)]
        sg = [k.sb(f"sg{i}", [128, TB], F32) for i in range(2)]
        Bsg = [Buf(f"sg{i}") for i in range(2)]
        xr = [k.sb(f"xr{i}", [128, 512], F32) for i in range(2)]
        Bxr = [Buf(f"xr{i}") for i in range(2)]
        yo = [k.sb(f"yo{i}", [128, 512], F32) for i in range(2)]
        Byo = [Buf(f"yo{i}") for i in range(2)]
        pt = [k.ps(f"pt{i}", [128, 512]) for i in range(8)]
        Bp = [Buf(f"pt{i}") for i in range(8)]

        k.dma("sp", gs[:], gT, writes=[Bg])
        k.dma("sp", ids[:], ident, writes=[Bid])

        nxt = 0
        ev2 = 0
        for tb in range(NB):
            t0 = tb * TB
            for tt in range(4):
                s = tt % 2
                r0 = t0 + tt * 128
                k.dma("sp", xs[s][:], x[r0:r0 + 128, :], writes=[Bxs[s]])
                k.op("act", lambda e: e.activation(out=junk[:], in_=xs[s][:], func=AF.Square,
                                                   accum_out=st[s][:, 0:1]),
                     reads=[Bxs[s]], writes=[Bjunk, Bst[s]])
                k.op("dve", lambda e: e.tensor_scalar(out=st[s][:, 1:2], in0=st[s][:, 0:1],
                                                      scalar1=1.0 / D, scalar2=RMS_EPS,
                                                      op0=ALU.mult, op1=ALU.add),
                     reads=[Bst[s]], pwrites=[Bst[s]])
                k.op("act", lambda e: e.activation(out=st[s][:, 2:3], in_=st[s][:, 1:2], func=AF.Sqrt),
                     reads=[Bst[s]], pwrites=[Bst[s]])
                k.op("dve", lambda e: e.reciprocal(out=st[s][:, 3:4], in_=st[s][:, 2:3]),
                     reads=[Bst[s]], pwrites=[Bst[s]])
                k.op("dve", lambda e: e.tensor_scalar(out=xs[s][:], in0=xs[s][:], scalar1=st[s][:, 3:4],
                                                      scalar2=None, op0=ALU.mult),
                     reads=[Bst[s], Bxs[s]], pwrites=[Bxs[s]])
                for c4 in range(DC // 4):
                    p = nxt % 8; nxt += 1
                    for j in range(4):
                        c = c4 * 4 + j
                        k.op("pe", lambda e: e.transpose(out=pt[p][:, j * 128:(j + 1) * 128],
                                                         in_=xs[s][:, c * 128:(c + 1) * 128],
                                                         identity=ids[:]),
                             reads=[Bxs[s], Bid], writes=[Bp[p]] if j == 0 else [],
                             pwrites=[] if j == 0 else [Bp[p]], inc=(j == 3))
                    for j in range(4):
                        c = c4 * 4 + j
                        eng = "dve" if j % 2 == 0 else "act"
                        if eng == "dve":
                            k.op("dve", lambda e: e.tensor_scalar(
                                out=xnT[:, c, tt * 128:(tt + 1) * 128], in0=pt[p][:, j * 128:(j + 1) * 128],
                                scalar1=gs[:, c:c + 1], scalar2=None, op0=ALU.mult),
                                reads=[Bp[p], Bg], pwrites=[BxnT])
                        else:
                            k.op("act", lambda e: e.activation(
                                out=xnT[:, c, tt * 128:(tt + 1) * 128], in_=pt[p][:, j * 128:(j + 1) * 128],
                                func=AF.Copy, scale=gs[:, c:c + 1]),
                                reads=[Bp[p], Bg], pwrites=[BxnT])
            for f in range(nff):
                s = f % 2
                _cast_dma(k, wgb[s][:], wg[f], D, writes=[Bwg[s]])
                _cast_dma(k, wub[s][:], wu[f], D, writes=[Bwu[s]])
                pg, pu = pt[2 * s], pt[2 * s + 1]
                for c in range(DC):
                    k.op("pe", lambda e: e.matmul(pg[:], lhsT=wgb[s][:, c * 128:(c + 1) * 128],
                                                  rhs=xnT[:, c, :], start=(c == 0), stop=(c == DC - 1)),
                         reads=[Bwg[s], BxnT], writes=[Bp[2 * s]] if c == 0 else [],
                         pwrites=[] if c == 0 else [Bp[2 * s]], inc=(c == DC - 1))
                for c in range(DC):
                    k.op("pe", lambda e: e.matmul(pu[:], lhsT=wub[s][:, c * 128:(c + 1) * 128],
                                                  rhs=xnT[:, c, :], start=(c == 0), stop=(c == DC - 1)),
                         reads=[Bwu[s], BxnT], writes=[Bp[2 * s + 1]] if c == 0 else [],
                         pwrites=[] if c == 0 else [Bp[2 * s + 1]], inc=(c == DC - 1))
                k.op("act", lambda e: e.activation(out=sg[s][:], in_=pg[:], func=AF.Silu),
                     reads=[Bp[2 * s]], writes=[Bsg[s]])
                k.op("dve", lambda e: e.tensor_tensor(out=hT[:, f, :], in0=sg[s][:], in1=pu[:], op=ALU.mult),
                     reads=[Bsg[s], Bp[2 * s + 1]], pwrites=[BhT])
            ngrp = (nff + 7) // 8
            li = 0
            for db in range(D // 512):
                ps0 = 4 * (db % 2)
                for g in range(ngrp):
                    s = li % 2; li += 1
                    nk = min(8, nff - g * 8)
                    _cast_dma(k, wdb[s][:, 0:nk, :],
                              wd[g * 1024:g * 1024 + nk * 128, db * 512:(db + 1) * 512]
                              .rearrange("(k p) n -> p k n", p=128),
                              512, writes=[Bwd[s]])
                    for kk in range(nk):
                        f = g * 8 + kk
                        for tt in range(4):
                            first = (f == 0)
                            k.op("pe", lambda e: e.matmul(pt[ps0 + tt][:], lhsT=hT[:, f, tt * 128:(tt + 1) * 128],
                                                          rhs=wdb[s][:, kk, :], start=first, stop=(f == nff - 1)),
                                 reads=[BhT, Bwd[s]], writes=[Bp[ps0 + tt]] if first else [],
                                 pwrites=[] if first else [Bp[ps0 + tt]],
                                 inc=(f == nff - 1) or (kk == nk - 1 and tt == 3))
                for tt in range(4):
                    s2 = ev2 % 2; ev2 += 1
                    r0 = t0 + tt * 128
                    k.dma("sp", xr[s2][:], x[r0:r0 + 128, db * 512:(db + 1) * 512], writes=[Bxr[s2]])
                    k.op("dve", lambda e: e.scalar_tensor_tensor(
                        out=yo[s2][:], in0=pt[ps0 + tt][:], scalar=0.5, in1=xr[s2][:],
                        op0=ALU.mult, op1=ALU.add),
                        reads=[Bp[ps0 + tt], Bxr[s2]], writes=[Byo[s2]])
                    k.dma("sp", y[r0:r0 + 128, db * 512:(db + 1) * 512], yo[s2][:], reads=[Byo[s2]], owner=Byo[s2])
        k.barrier()
        print(f"[ffn] ins={k.nins} waits={k.nwait} sems={k.nsem} cnt={k.cnt}")


def _ffn_weight_layout(w):
    nff = w.shape[1] // 128
    return np.ascontiguousarray(w.reshape(DC, 128, nff, 128).transpose(2, 1, 0, 3)).reshape(nff, 128, D)


def _gT(g):
    return np.ascontiguousarray(g.reshape(DC, 128).T)


def build_linear(ncols, norm, residual, NT=1024):
    nc = bass.Bass("TRN2", target_bir_lowering=False)
    x = nc.dram_tensor("x", [NT, D], F32, kind="ExternalInput").ap()
    gT = nc.dram_tensor("gT", [128, DC], F32, kind="ExternalInput").ap()
    ident = nc.dram_tensor("ident", [128, 128], F32, kind="ExternalInput").ap()
    w = nc.dram_tensor("w", [D, ncols], F32, kind="ExternalInput").ap()
    res = nc.dram_tensor("res", [NT, ncols], F32, kind="ExternalInput").ap() if residual else None
    y = nc.dram_tensor("y", [NT, ncols], F32, kind="ExternalOutput").ap()
    with ExitStack() as es:
        k = K(nc, es)
        emit_linear(k, x, gT, ident, w, res, y, ncols, norm, NT)
    return nc


def emit_linear(k, x, gT, ident, w, res, y, ncols, norm, NT=1024):
    NTT = NT // 128
    residual = res is not None
    with ExitStack() as ph:
        k.es = ph
        gs = k.sb("gs", [128, DC], F32); Bg = Buf("gs")
        ids = k.sb("ids", [128, 128], F32); Bid = Buf("ids")
        xs = [k.sb(f"xs{i}", [128, D], F32) for i in range(2)]
        Bxs = [Buf(f"xs{i}") for i in range(2)]
        st = [k.sb(f"st{i}", [128, 4], F32) for i in range(2)]
        Bst = [Buf(f"st{i}") for i in range(2)]
        junk = k.sb("junk", [128, D], BF16); Bjunk = Buf("junk")
        xT = k.sb("xT", [128, DC, NT], BF16); BxT = Buf("xT")
        wb = [k.sb(f"wb{i}", [128, DC, 512], BF16) for i in range(2)]
        Bw = [Buf(f"wb{i}") for i in range(2)]
        rs = [k.sb(f"rs{i}", [128, 512], F32) for i in range(2)]
        Brs = [Buf(f"rs{i}") for i in range(2)]
        yo = [k.sb(f"yo{i}", [128, 512], F32) for i in range(2)]
        Byo = [Buf(f"yo{i}") for i in range(2)]
        pt = [k.ps(f"pt{i}", [128, 512]) for i in range(8)]
        Bp = [Buf(f"pt{i}") for i in range(8)]
        k.dma("sp", gs[:], gT, writes=[Bg])
        k.dma("sp", ids[:], ident, writes=[Bid])
        nxt = 0
        for tt in range(NTT):
            s = tt % 2
            k.dma("sp", xs[s][:], x[tt * 128:(tt + 1) * 128, :], writes=[Bxs[s]])
            if norm:
                k.op("act", lambda e: e.activation(out=junk[:], in_=xs[s][:], func=AF.Square,
                                                   accum_out=st[s][:, 0:1]),
                     reads=[Bxs[s]], writes=[Bjunk, Bst[s]])
                k.op("dve", lambda e: e.tensor_scalar(out=st[s][:, 1:2], in0=st[s][:, 0:1],
                                                      scalar1=1.0 / D, scalar2=RMS_EPS,
                                                      op0=ALU.mult, op1=ALU.add),
                     reads=[Bst[s]], pwrites=[Bst[s]])
                k.op("act", lambda e: e.activation(out=st[s][:, 2:3], in_=st[s][:, 1:2], func=AF.Sqrt),
                     reads=[Bst[s]], pwrites=[Bst[s]])
                k.op("dve", lambda e: e.reciprocal(out=st[s][:, 3:4], in_=st[s][:, 2:3]),
                     reads=[Bst[s]], pwrites=[Bst[s]])
                k.op("dve", lambda e: e.tensor_scalar(out=xs[s][:], in0=xs[s][:], scalar1=st[s][:, 3:4],
                                                      scalar2=None, op0=ALU.mult),
                     reads=[Bst[s], Bxs[s]], pwrites=[Bxs[s]])
            for c4 in range(DC // 4):
                p = nxt % 8; nxt += 1
                for j in range(4):
                    c = c4 * 4 + j
                    k.op("pe", lambda e: e.transpose(out=pt[p][:, j * 128:(j + 1) * 128],
                                                     in_=xs[s][:, c * 128:(c + 1) * 128], identity=ids[:]),
                         reads=[Bxs[s], Bid], writes=[Bp[p]] if j == 0 else [],
                         pwrites=[] if j == 0 else [Bp[p]], inc=(j == 3))
                for j in range(4):
                    c = c4 * 4 + j
                    if j % 2 == 0:
                        k.op("dve", lambda e: e.tensor_scalar(
                            out=xT[:, c, tt * 128:(tt + 1) * 128], in0=pt[p][:, j * 128:(j + 1) * 128],
                            scalar1=gs[:, c:c + 1], scalar2=None, op0=ALU.mult),
                            reads=[Bp[p], Bg], pwrites=[BxT])
                    else:
                        k.op("act", lambda e: e.activation(
                            out=xT[:, c, tt * 128:(tt + 1) * 128], in_=pt[p][:, j * 128:(j + 1) * 128],
                            func=AF.Copy, scale=gs[:, c:c + 1]),
                            reads=[Bp[p], Bg], pwrites=[BxT])
        ncb = (ncols + 511) // 512
        ev = 0
        for cb in range(ncb):
            c0 = cb * 512
            cw = min(512, ncols - c0)
            s = cb % 2
            for hf in range(2):
                _cast_dma(k, wb[s][:, hf * 16:(hf + 1) * 16, 0:cw],
                          w[hf * 2048:(hf + 1) * 2048, c0:c0 + cw].rearrange("(c p) n -> p c n", p=128),
                          cw, writes=[Bw[s]] if hf == 0 else [], pwrites=[] if hf == 0 else [Bw[s]])
            for tt in range(NTT):
                p = nxt % 8; nxt += 1
                for c in range(DC):
                    k.op("pe", lambda e: e.matmul(pt[p][:, 0:cw], lhsT=xT[:, c, tt * 128:(tt + 1) * 128],
                                                  rhs=wb[s][:, c, 0:cw], start=(c == 0), stop=(c == DC - 1)),
                         reads=[BxT, Bw[s]], writes=[Bp[p]] if c == 0 else [],
                         pwrites=[] if c == 0 else [Bp[p]], inc=(c == DC - 1))
                s2 = ev % 2; ev += 1
                if residual:
                    k.dma("sp", rs[s2][:, 0:cw], res[tt * 128:(tt + 1) * 128, c0:c0 + cw], writes=[Brs[s2]])
                    k.op("dve", lambda e: e.tensor_tensor(out=yo[s2][:, 0:cw], in0=pt[p][:, 0:cw],
                                                          in1=rs[s2][:, 0:cw], op=ALU.add),
                         reads=[Bp[p], Brs[s2]], writes=[Byo[s2]])
                else:
                    if ev % 2 == 0:
                        k.op("dve", lambda e: e.tensor_copy(out=yo[s2][:, 0:cw], in_=pt[p][:, 0:cw]),
                             reads=[Bp[p]], writes=[Byo[s2]])
                    else:
                        k.op("act", lambda e: e.copy(out=yo[s2][:, 0:cw], in_=pt[p][:, 0:cw]),
                             reads=[Bp[p]], writes=[Byo[s2]])
                k.dma("sp", y[tt * 128:(tt + 1) * 128, c0:c0 + cw], yo[s2][:, 0:cw],
                      reads=[Byo[s2]], owner=Byo[s2])
        k.barrier()
        print(f"[linear {ncols} n={norm} r={residual}] ins={k.nins} waits={k.nwait} sems={k.nsem} cnt={k.cnt}")


def build_fnorm(NT=1024):
    nc = bass.Bass("TRN2", target_bir_lowering=False)
    x = nc.dram_tensor("x", [NT, D], F32, kind="ExternalInput").ap()
    gR = nc.dram_tensor("gR", [128, D], F32, kind="ExternalInput").ap()
    y = nc.dram_tensor("y", [NT, D], F32, kind="ExternalOutput").ap()
    with ExitStack() as es:
        k = K(nc, es)
        emit_fnorm(k, x, gR, y, NT)
    return nc


def emit_fnorm(k, x, gR, y, NT=1024):
    with ExitStack() as ph:
        k.es = ph
        gs = k.sb("gs", [128, D], F32); Bg = Buf("gs")
        xs = [k.sb(f"xs{i}", [128, D], F32) for i in range(2)]
        Bxs = [Buf(f"xs{i}") for i in range(2)]
        ys = [k.sb(f"ys{i}", [128, D], F32) for i in range(2)]
        Bys = [Buf(f"ys{i}") for i in range(2)]
        st = [k.sb(f"st{i}", [128, 4], F32) for i in range(2)]
        Bst = [Buf(f"st{i}") for i in range(2)]
        junk = k.sb("junk", [128, D], BF16); Bjunk = Buf("junk")
        k.dma("sp", gs[:], gR, writes=[Bg])
        for tt in range(NT // 128):
            s = tt % 2
            k.dma("sp", xs[s][:], x[tt * 128:(tt + 1) * 128, :], writes=[Bxs[s]])
            k.op("act", lambda e: e.activation(out=junk[:], in_=xs[s][:], func=AF.Square,
                                               accum_out=st[s][:, 0:1]),
                 reads=[Bxs[s]], writes=[Bjunk, Bst[s]])
            k.op("dve", lambda e: e.tensor_scalar(out=st[s][:, 1:2], in0=st[s][:, 0:1],
                                                  scalar1=1.0 / D, scalar2=RMS_EPS, op0=ALU.mult, op1=ALU.add),
                 reads=[Bst[s]], pwrites=[Bst[s]])
            k.op("act", lambda e: e.activation(out=st[s][:, 2:3], in_=st[s][:, 1:2], func=AF.Sqrt),
                 reads=[Bst[s]], pwrites=[Bst[s]])
            k.op("dve", lambda e: e.reciprocal(out=st[s][:, 3:4], in_=st[s][:, 2:3]),
                 reads=[Bst[s]], pwrites=[Bst[s]])
            k.op("dve", lambda e: e.scalar_tensor_tensor(out=ys[s][:], in0=xs[s][:], scalar=st[s][:, 3:4],
                                                         in1=gs[:], op0=ALU.mult, op1=ALU.mult),
                 reads=[Bst[s], Bxs[s], Bg], writes=[Bys[s]])
            k.dma("sp", y[tt * 128:(tt + 1) * 128, :], ys[s][:], reads=[Bys[s]], owner=Bys[s])
        k.barrier()


def build_chain(stages, NT=1024, nff=NFF):
    nc = bass.Bass("TRN2", target_bir_lowering=False)
    ein = lambda n, sh: nc.dram_tensor(n, sh, F32, kind="ExternalInput").ap()
    eout = lambda n, sh: nc.dram_tensor(n, sh, F32, kind="ExternalOutput").ap()
    ident = ein("ident", [128, 128])
    cur = ein("x0", [NT, D])
    t = []
    for i, st in enumerate(stages):
        if st[0] == "ffn":
            t.append(dict(gT=ein(f"s{i}_gT", [128, DC]), wg=ein(f"s{i}_wg", [nff, 128, D]),
                          wu=ein(f"s{i}_wu", [nff, 128, D]), wd=ein(f"s{i}_wd", [nff * 128, D]),
                          y=eout(f"s{i}_y", [NT, D])))
        elif st[0] == "lin":
            ncols = st[1]
            t.append(dict(gT=ein(f"s{i}_gT", [128, DC]), w=ein(f"s{i}_w", [D, ncols]),
                          res=ein(f"s{i}_res", [NT, ncols]) if st[3] else None,
                          y=eout(f"s{i}_y", [NT, ncols])))
        else:
            t.append(dict(gR=ein(f"s{i}_gR", [128, D]), y=eout(f"s{i}_y", [NT, D])))
    with ExitStack() as es:
        k = K(nc, es)
        for i, st in enumerate(stages):
            k.pfx = f"s{i}_"
            d = t[i]
            if st[0] == "ffn":
                emit_ffn(k, cur, d["gT"], ident, d["wg"], d["wu"], d["wd"], d["y"], NT, nff)
            elif st[0] == "lin":
                emit_linear(k, cur, d["gT"], ident, d["w"], d["res"], d["y"], st[1], st[2], NT)
            else:
                emit_fnorm(k, cur, d["gR"], d["y"], NT)
            cur = d["y"]
    return nc


T = 2048
NTT = T // 128
HD = 128
SCALE = HD ** -0.5
MNEG = -float(2 ** 20)


class Att:
    def __init__(self, k):
        self.k = k
        self.ps = [k.ps(f"pss{i}", [128, 512]) for i in range(2)]
        self.Bps = [Buf(f"pss{i}") for i in range(2)]
        self.po = [k.ps(f"po{i}", [128, 512]) for i in range(4)]
        self.Bpo = [Buf(f"po{i}") for i in range(4)]
        self.PT = [k.sb(f"PT{i}", [128, 512], BF16) for i in range(2)]
        self.BPT = [Buf(f"PT{i}") for i in range(2)]
        self.n = 0

    def run(self, *, kT, BkT, qT, BqT, v1, Bv1, VW, plan, idb, Bidb, masks, Bmasks,
            out_cb, KP=128, bias_k=None, Bbias=None, pre=None, sel=None, rhs_all=False):
        k = self.k
        tiles = []
        for g in range(4):
            first = {}; last = {}
            for (kt, mi, qlo, qhi) in plan[g]:
                for qt in range(qlo, qhi + 1):
                    first.setdefault(qt, kt); last[qt] = kt
            n = len(plan[g])
            for j, (kt, mi, qlo, qhi) in enumerate(plan[g]):
                tiles.append((g, kt, mi, qlo, qhi, first, last, j == n - 1))

        def score(t):
            g, kt, mi, qlo, qhi, first, last, endg = t
            s = self.n % 2; self.n += 1
            ps, Bps = self.ps[s], self.Bps[s]
            c0, c1 = qlo * 128, (qhi + 1) * 128
            nmm = 1 + (mi is not None) + (sel is not None)
            k.op("pe", lambda e: e.matmul(ps[0:KP, c0:c1], lhsT=kT[:, kt * 128:kt * 128 + KP],
                                          rhs=qT[:, g * 512 + c0:g * 512 + c1], start=True, stop=(nmm == 1)),
                 reads=[BkT, BqT], writes=[Bps])
            i = 1
            if mi is not None:
                k.op("pe", lambda e: e.matmul(ps[0:KP, c0:c1], lhsT=idb[:, 0:KP],
                                              rhs=masks[:, mi, g * 512 + c0:g * 512 + c1] if rhs_all else masks[:, mi, c0:c1],
                                              start=False, stop=(i == nmm - 1)),
                     reads=[Bidb, Bmasks], pwrites=[Bps])
                i += 1
            if sel is not None:
                esel, selT, Bsel = sel
                k.op("pe", lambda e: e.matmul(ps[0:KP, c0:c1], lhsT=esel[:, kt, :],
                                              rhs=selT[:, g * 512 + c0:g * 512 + c1], start=False, stop=True),
                     reads=[Bsel], pwrites=[Bps])
            return s

        def expo(t, s):
            g, kt, mi, qlo, qhi, first, last, endg = t
            ps, Bps = self.ps[s], self.Bps[s]
            PT, BPT = self.PT[s], self.BPT[s]
            c0, c1 = qlo * 128, (qhi + 1) * 128
            if pre is not None:
                tmp, Btmp, qb, Bqb = pre
                k.op("dve", lambda e: e.scalar_tensor_tensor(
                    out=tmp[s][:, c0:c1], in0=ps[:, c0:c1], scalar=SCALE, in1=qb[:, g * 512 + c0:g * 512 + c1],
                    op0=ALU.mult, op1=ALU.add), reads=[Bps, Bqb], writes=[Btmp[s]])
                k.op("act", lambda e: e.activation(out=PT[:, c0:c1], in_=tmp[s][:, c0:c1], func=AF.Exp,
                                                   bias=bias_k[:, kt:kt + 1], scale=1.0),
                     reads=[Btmp[s], Bbias], writes=[BPT])
            else:
                k.op("act", lambda e: e.activation(out=PT[0:KP, c0:c1], in_=ps[0:KP, c0:c1], func=AF.Exp,
                                                   scale=SCALE),
                     reads=[Bps], writes=[BPT])

        def pv(t, s):
            g, kt, mi, qlo, qhi, first, last, endg = t
            PT, BPT = self.PT[s], self.BPT[s]
            for qt in range(qlo, qhi + 1):
                st_, sp_ = (first[qt] == kt), (last[qt] == kt)
                k.op("pe", lambda e: e.matmul(self.po[qt][:, 0:VW], lhsT=PT[0:KP, qt * 128:(qt + 1) * 128],
                                              rhs=v1[0:KP, kt, 0:VW], start=st_, stop=sp_),
                     reads=[BPT, Bv1], writes=[self.Bpo[qt]] if st_ else [],
                     pwrites=[] if st_ else [self.Bpo[qt]])
            if endg:
                for qt in range(4):
                    out_cb(g, qt, self.po[qt], self.Bpo[qt])

        s_next = score(tiles[0])
        for i, t in enumerate(tiles):
            s_cur = s_next
            if i + 1 < len(tiles):
                s_next = score(tiles[i + 1])
            expo(t, s_cur)
            pv(t, s_cur)


def causal_plan():
    plan = []
    for g in range(4):
        tl = [(kt, None, 0, 3) for kt in range(4 * g)]
        tl += [(4 * g + r, r, r, 3) for r in range(4)]
        plan.append(tl)
    return plan


def window_plan():
    plan = []
    for g in range(4):
        tl = []
        if g > 0:
            tl += [(4 * g - 4 + r, 4 + r, 0, r) for r in range(4)]
        tl += [(4 * g + r, r, r, 3) for r in range(4)]
        plan.append(tl)
    return plan


def _rope(k, dst, Bdst, src, srcsw, C, S, Bcs, xa, Bxa, xb, Bxb, ctr):
    for hf in range(2):
        s = ctr[0] % 2; ctr[0] += 1
        cs = slice(hf * 1024, (hf + 1) * 1024)
        k.dma("sp", xa[s][:], src[:, cs], writes=[Bxa[s]])
        k.dma("sp", xb[s][:], srcsw[:, cs], writes=[Bxb[s]])
        k.op("dve", lambda e: e.tensor_tensor(out=xa[s][:], in0=xa[s][:], in1=C[:, cs], op=ALU.mult),
             reads=[Bxa[s], Bcs], pwrites=[Bxa[s]])
        k.op("pool", lambda e: e.tensor_tensor(out=xb[s][:], in0=xb[s][:], in1=S[:, cs], op=ALU.mult),
             reads=[Bxb[s], Bcs], pwrites=[Bxb[s]])
        k.op("dve", lambda e: e.tensor_tensor(out=dst[:, cs], in0=xa[s][:], in1=xb[s][:], op=ALU.add),
             reads=[Bxa[s], Bxb[s]], writes=[Bdst] if hf == 0 else [], pwrites=[] if hf == 0 else [Bdst])


def build_att_even(lam_init, nfox=8, ndiff=4):
    nc = bass.Bass("TRN2", target_bir_lowering=False)
    dt_ = lambda n, sh: nc.dram_tensor(n, sh, F32, kind="ExternalInput").ap()
    fqT = dt_("fqT", [8, 128, T]); fkT = dt_("fkT", [8, 128, T]); fv = dt_("fv", [8, T, 128])
    fg = dt_("fg", [T, 8]); bfr = dt_("bfr", [128, 128])
    dqT = dt_("dqT", [8, 128, T]); dqTs = dt_("dqTs", [8, 128, T])
    dkT = dt_("dkT", [8, 128, T]); dkTs = dt_("dkTs", [8, 128, T]); dv = dt_("dv", [4, T, 256])
    ropeC = dt_("ropeC", [128, T]); ropeS = dt_("ropeS", [128, T])
    maskc = dt_("maskc", [128, 4 * 512]); ident = dt_("ident", [128, 128])
    triu = dt_("triu", [128, 128]); selh = dt_("selh", [8, 8 * 128])
    lamv = dt_("lamv", [128, 4 * 128]); subg = dt_("subg", [128, 256])
    o = nc.dram_tensor("o", [T, 2048], F32, kind="ExternalOutput").ap()
    with ExitStack() as es:
        k = K(nc, es)
        att = Att(k)
        pm = [k.ps(f"pm{i}", [128, 512]) for i in range(2)]
        Bpm = [Buf(f"pm{i}") for i in range(2)]
        idf = k.sb("idf", [128, 128], F32); Bidf = Buf("idf")
        idb = k.sb("idb", [128, 128], BF16); Bidb = Buf("idb")
        tri = k.sb("tri", [128, 128], F32); Btri = Buf("tri")
        onesf = k.sb("onesf", [128, 128], F32); Bones = Buf("onesf")
        mk = k.sb("mk", [128, 4, 512], BF16); Bmk = Buf("mk")
        C = k.sb("C", [128, T], F32); S = k.sb("S", [128, T], F32); Bcs = Buf("cs")
        sh = k.sb("sh", [8, 8, 128], F32); Bsh = Buf("sh")
        k.dma("sp", idf[:], ident, writes=[Bidf])
        _cast_dma(k, idb[:], ident, 128, writes=[Bidb])
        k.dma("sp", tri[:], triu, writes=[Btri])
        k.op("dve", lambda e: e.memset(onesf[:], 1.0), writes=[Bones])
        _cast_dma(k, mk[:], maskc.rearrange("p (m n) -> p m n", m=4), 512, writes=[Bmk])
        k.dma("sp", C[:], ropeC, writes=[Bcs])
        k.dma("sp", S[:], ropeS, pwrites=[Bcs])
        k.dma("sp", sh[:], selh.rearrange("k (h m) -> k h m", h=8), writes=[Bsh])
        lv = k.sb("lv", [128, 4, 128], F32); Blv = Buf("lv")
        lt = k.sb("lt", [128, 2, 128], F32); Blt = Buf("lt")
        ls = k.sb("ls", [128, 8], F32); Bls = Buf("ls")
        k.dma("sp", lv[:], lamv.rearrange("p (a n) -> p a n", a=4), writes=[Blv])
        for i in range(2):
            k.op("dve", lambda e: e.tensor_tensor(out=lt[:, i, :], in0=lv[:, 2 * i, :], in1=lv[:, 2 * i + 1, :],
                                                  op=ALU.mult), reads=[Blv], pwrites=[Blt])
        k.op("dve", lambda e: e.tensor_reduce(out=ls[:, 0:2], in_=lt[:], axis=AX.X, op=ALU.add),
             reads=[Blt], writes=[Bls])
        k.op("act", lambda e: e.activation(out=ls[:, 2:4], in_=ls[:, 0:2], func=AF.Exp), reads=[Bls], pwrites=[Bls])
        k.op("dve", lambda e: e.scalar_tensor_tensor(out=ls[:, 4:5], in0=ls[:, 3:4], scalar=-float(lam_init),
                                                     in1=ls[:, 2:3], op0=ALU.add, op1=ALU.subtract),
             reads=[Bls], pwrites=[Bls])
        sg_ = k.sb("subgs", [128, 256], F32); Bsg_ = Buf("subg")
        k.dma("sp", sg_[:], subg, writes=[Bsg_])
        fgs = k.sb("fgs", [128, NTT, 8], F32); Bfg = Buf("fgs")
        bfs = k.sb("bfs", [128, NTT, 8], F32); Bbf = Buf("bfs")
        cn = k.sb("cn", [128, NTT, 8], F32); Bcn = Buf("cn")
        off = k.sb("off", [128, NTT, 8], F32); Boff = Buf("off")
        cnh = k.sb("cnh", [128, 8, NTT], F32); Bcnh = Buf("cnh")
        cnT = k.sb("cnT", [8, T], F32); BcnT = Buf("cnT")
        k.dma("sp", fgs[:], fg.rearrange("(t p) h -> p t h", p=128), writes=[Bfg])
        k.dma("sp", bfs[:], bfr.rearrange("p (t h) -> p t h", h=8), writes=[Bbf])
        k.op("dve", lambda e: e.tensor_tensor(out=fgs[:], in0=fgs[:], in1=bfs[:], op=ALU.add),
             reads=[Bfg, Bbf], pwrites=[Bfg])
        k.op("act", lambda e: e.activation(out=fgs[:], in_=fgs[:], func=AF.Exp, scale=-1.0), reads=[Bfg], pwrites=[Bfg])
        k.op("act", lambda e: e.activation(out=fgs[:], in_=fgs[:], func=AF.Ln, bias=1.0, scale=1.0),
             reads=[Bfg], pwrites=[Bfg])
        fl = fgs[:].rearrange("p t h -> p (t h)")
        k.op("pe", lambda e: e.matmul(pm[0][:, 0:128], lhsT=tri[:], rhs=fl, start=True, stop=True),
             reads=[Btri, Bfg], writes=[Bpm[0]])
        k.op("pe", lambda e: e.matmul(pm[1][:, 0:128], lhsT=onesf[:], rhs=fl, start=True, stop=True),
             reads=[Bones, Bfg], writes=[Bpm[1]])
        k.op("dve", lambda e: e.memset(off[:, 0, :], 0.0), writes=[Boff])
        k.op("act", lambda e: e.copy(out=cn[:].rearrange("p t h -> p (t h)"), in_=pm[1][:, 0:128]),
             reads=[Bpm[1]], writes=[Bcn])
        for i in range(1, NTT):
            k.op("dve", lambda e: e.tensor_tensor(out=off[:, i, :], in0=off[:, i - 1, :], in1=cn[:, i - 1, :],
                                                  op=ALU.add), reads=[Boff, Bcn], pwrites=[Boff])
        k.op("dve", lambda e: e.tensor_tensor(out=cn[:].rearrange("p t h -> p (t h)"), in0=pm[0][:, 0:128],
                                              in1=off[:].rearrange("p t h -> p (t h)"), op=ALU.add),
             reads=[Bpm[0], Boff], writes=[Bcn])
        k.op("dve", lambda e: e.tensor_copy(out=cnh[:], in_=cn[:].rearrange("p t h -> p h t")),
             reads=[Bcn], writes=[Bcnh])
        for i4 in range(4):
            p = pm[i4 % 2]; Bp_ = Bpm[i4 % 2]
            for j in range(4):
                i = i4 * 4 + j
                k.op("pe", lambda e: e.transpose(out=p[0:8, j * 128:(j + 1) * 128], in_=cn[:, i, :], identity=idf[:]),
                     reads=[Bcn, Bidf], writes=[Bp_] if j == 0 else [], pwrites=[] if j == 0 else [Bp_])
            k.op("act", lambda e: e.mul(out=cnT[:, i4 * 512:(i4 + 1) * 512], in_=p[0:8, :], mul=-1.0),
                 reads=[Bp_], pwrites=[BcnT])
        qb = [k.sb(f"qb{i}", [128, T], BF16) for i in range(2)]; Bqb = [Buf(f"qb{i}") for i in range(2)]
        kb = [k.sb(f"kb{i}", [128, T], BF16) for i in range(2)]; Bkb = [Buf(f"kb{i}") for i in range(2)]
        v1 = [k.sb(f"v1{i}", [128, NTT, 257], BF16) for i in range(2)]; Bv1 = [Buf(f"v1{i}") for i in range(2)]
        qbias = k.sb("qbias", [128, T], F32); Bqbias = Buf("qbias")
        tmp = [k.sb(f"tmp{i}", [128, 512], F32) for i in range(2)]; Btmp = [Buf(f"tmp{i}") for i in range(2)]
        ob = [k.sb(f"ob{i}", [128, NTT, 256], F32) for i in range(2)]; Bob = [Buf(f"ob{i}") for i in range(2)]
        rsb = k.sb("rsb", [128, 64], F32); Brs = Buf("rsb")
        xa = [k.sb(f"xa{i}", [128, 1024], F32) for i in range(2)]; Bxa = [Buf(f"xa{i}") for i in range(2)]
        xb = [k.sb(f"xb{i}", [128, 1024], F32) for i in range(2)]; Bxb = [Buf(f"xb{i}") for i in range(2)]
        junk = k.sb("junk", [128, 256], F32); Bjunk = Buf("junk")
        nst = k.sb("nst", [128, 4, NTT], F32); Bnst = Buf("nst")
        ctr = [0]
        plan = causal_plan()
        rc = [0]

        def norm_cb(obuf, Bobuf, VW):
            def cb(g, qt, po, Bpo):
                qi = 4 * g + qt
                c = rc[0] % 64; rc[0] += 1
                k.op("dve", lambda e: e.reciprocal(out=rsb[:, c:c + 1], in_=po[:, VW:VW + 1]),
                     reads=[Bpo], pwrites=[Brs])
                if qi % 2 == 0:
                    k.op("dve", lambda e: e.tensor_scalar(out=obuf[:, qi, 0:VW], in0=po[:, 0:VW], scalar1=rsb[:, c:c + 1],
                                                          scalar2=None, op0=ALU.mult),
                         reads=[Bpo, Brs], pwrites=[Bobuf])
                else:
                    k.op("act", lambda e: e.activation(out=obuf[:, qi, 0:VW], in_=po[:, 0:VW], func=AF.Copy,
                                                       scale=rsb[:, c:c + 1]),
                         reads=[Bpo, Brs], pwrites=[Bobuf])
            return cb

        for h in range(nfox):
            s = h % 2
            _cast_dma(k, qb[s][:], fqT[h], T, writes=[Bqb[s]])
            _cast_dma(k, kb[s][:], fkT[h], T, writes=[Bkb[s]])
            _cast_dma(k, v1[s][:, :, 0:128], fv[h].rearrange("(t p) d -> p t d", p=128), 128, writes=[Bv1[s]])
            k.op("pool", lambda e: e.memset(v1[s][:, :, 128:129], 1.0), pwrites=[Bv1[s]])
            for g in range(4):
                p = pm[g % 2]; Bp_ = Bpm[g % 2]
                k.op("pe", lambda e: e.matmul(p[:], lhsT=sh[:, h, :], rhs=cnT[:, g * 512:(g + 1) * 512],
                                              start=True, stop=True), reads=[Bsh, BcnT], writes=[Bp_])
                k.op("act", lambda e: e.copy(out=qbias[:, g * 512:(g + 1) * 512], in_=p[:]),
                     reads=[Bp_], writes=[Bqbias] if g == 0 else [], pwrites=[] if g == 0 else [Bqbias])
            k.op("dve", lambda e: e.memset(ob[s][:, 0, 0:1], 0.0), writes=[Bob[s]])
            att.run(kT=kb[s], BkT=Bkb[s], qT=qb[s], BqT=Bqb[s], v1=v1[s], Bv1=Bv1[s], VW=129, plan=plan,
                    idb=idb, Bidb=Bidb, masks=mk, Bmasks=Bmk, out_cb=norm_cb(ob[s], Bob[s], 128),
                    bias_k=cnh[:, h, :], Bbias=Bcnh, pre=(tmp, Btmp, qbias, Bqbias))
            k.dma("sp", o[:, h * 128:(h + 1) * 128].rearrange("(t p) d -> p t d", p=128), ob[s][:, :, 0:128],
                  reads=[Bob[s]], owner=Bob[s])
        for hd in range(ndiff):
            vs_ = hd % 2
            _cast_dma(k, v1[vs_][:, :, 0:256], dv[hd].rearrange("(t p) d -> p t d", p=128), 256, writes=[Bv1[vs_]])
            k.op("pool", lambda e: e.memset(v1[vs_][:, :, 256:257], 1.0), pwrites=[Bv1[vs_]])
            for m in range(2):
                _rope(k, qb[m], Bqb[m], dqT[hd * 2 + m], dqTs[hd * 2 + m], C, S, Bcs, xa, Bxa, xb, Bxb, ctr)
                _rope(k, kb[m], Bkb[m], dkT[hd * 2 + m], dkTs[hd * 2 + m], C, S, Bcs, xa, Bxa, xb, Bxb, ctr)
                k.op("dve", lambda e: e.memset(ob[m][:, 0, 0:1], 0.0), writes=[Bob[m]])
                att.run(kT=kb[m], BkT=Bkb[m], qT=qb[m], BqT=Bqb[m], v1=v1[vs_], Bv1=Bv1[vs_], VW=257, plan=plan,
                        idb=idb, Bidb=Bidb, masks=mk, Bmasks=Bmk, out_cb=norm_cb(ob[m], Bob[m], 256))
            f0 = ob[0][:].rearrange("p t d -> p (t d)"); f1 = ob[1][:].rearrange("p t d -> p (t d)")
            k.op("dve", lambda e: e.scalar_tensor_tensor(out=f0, in0=f1, scalar=ls[:, 4:5], in1=f0,
                                                         op0=ALU.mult, op1=ALU.add),
                 reads=[Bob[1], Bob[0], Bls], writes=[Bob[0]])
            for qi in range(NTT):
                k.op("act", lambda e: e.activation(out=junk[:], in_=ob[0][:, qi, :], func=AF.Square,
                                                   accum_out=nst[:, 0, qi:qi + 1]),
                     reads=[Bob[0]], writes=[Bjunk], pwrites=[Bnst])
            k.op("dve", lambda e: e.tensor_scalar(out=nst[:, 1, :], in0=nst[:, 0, :], scalar1=1.0 / 256, scalar2=RMS_EPS,
                                                  op0=ALU.mult, op1=ALU.add), reads=[Bnst], pwrites=[Bnst])
            k.op("act", lambda e: e.activation(out=nst[:, 2, :], in_=nst[:, 1, :], func=AF.Sqrt), reads=[Bnst], pwrites=[Bnst])
            k.op("dve", lambda e: e.reciprocal(out=nst[:, 3, :], in_=nst[:, 2, :]), reads=[Bnst], pwrites=[Bnst])
            k.op("dve", lambda e: e.tensor_scalar(out=nst[:, 3, :], in0=nst[:, 3, :], scalar1=float(1.0 - lam_init),
                                                  scalar2=None, op0=ALU.mult), reads=[Bnst], pwrites=[Bnst])
            for qi in range(NTT):
                k.op("dve", lambda e: e.scalar_tensor_tensor(out=ob[1][:, qi, :], in0=ob[0][:, qi, :],
                                                             scalar=nst[:, 3, qi:qi + 1], in1=sg_[:],
                                                             op0=ALU.mult, op1=ALU.mult),
                     reads=[Bob[0], Bnst, Bsg_], writes=[Bob[1]] if qi == 0 else [], pwrites=[] if qi == 0 else [Bob[1]])
            k.dma("sp", o[:, 1024 + hd * 256:1024 + (hd + 1) * 256].rearrange("(t p) d -> p t d", p=128), ob[1][:],
                  reads=[Bob[1]], owner=Bob[1])
        k.finish(Bob)
        print(f"[att_even] ins={k.nins} waits={k.nwait} sems={k.nsem}")
    return nc


def _consts():
    c = {}
    c["ident"] = np.eye(128, dtype=np.float32)
    half = 64
    inv = (np.float32(10000.0) ** (-np.arange(half, dtype=np.float32) / np.float32(half))).astype(np.float32)
    ang = np.arange(T, dtype=np.float32)[:, None] * inv[None, :]
    cos = np.cos(ang).astype(np.float32).T; sin = np.sin(ang).astype(np.float32).T
    c["ropeC"] = np.ascontiguousarray(np.concatenate([cos, cos], 0))
    c["ropeS"] = np.ascontiguousarray(np.concatenate([-sin, sin], 0))
    ps = np.arange(128)[:, None]; tq = np.arange(512)[None, :]
    mc = np.stack([np.where(tq >= 128 * r + ps, 0.0, MNEG) for r in range(4)], 1)
    mw = np.stack([np.where(tq < 128 * r + ps, 0.0, MNEG) for r in range(4)], 1)
    c["maskc"] = np.ascontiguousarray(mc.reshape(128, 4 * 512).astype(np.float32))
    c["maskcw"] = np.ascontiguousarray(np.concatenate([mc, mw], 1).reshape(128, 8 * 512).astype(np.float32))
    c["triu"] = np.triu(np.ones((128, 128), np.float32))
    sh = np.zeros((8, 8, 128), np.float32)
    for h in range(8):
        sh[h, h, :] = 1.0
    c["selh"] = sh.reshape(8, 8 * 128)
    return c


def _swap_halves(a, axis):
    return np.concatenate(np.split(a, 2, axis=axis)[::-1], axis=axis)


def _even_inputs(proj_b, hh, p, c):
    fq, fk, fv, fgate, dq, dk, dv = np.split(proj_b, np.cumsum([2048, 2048, 2048, 16, 2048, 2048])[:].tolist(), axis=1)
    hs = slice(hh * 8, hh * 8 + 8); ds_ = slice(hh * 4, hh * 4 + 4)
    tr = lambda a: np.ascontiguousarray(a.reshape(T, 16, 128)[:, hs].transpose(1, 2, 0))
    dqT = dq.reshape(T, 8, 2, 128)[:, ds_].transpose(1, 2, 3, 0).reshape(8, 128, T)
    dkT = dk.reshape(T, 8, 2, 128)[:, ds_].transpose(1, 2, 3, 0).reshape(8, 128, T)
    ca = np.ascontiguousarray
    return {
        "fqT": tr(fq), "fkT": tr(fk), "fv": ca(fv.reshape(T, 16, 128)[:, hs].transpose(1, 0, 2)),
        "fg": ca(fgate[:, hs]), "bfr": ca(np.broadcast_to(np.tile(p["even_b_forget"][0][hs], NTT), (128, 128))),
        "dqT": ca(dqT), "dqTs": ca(_swap_halves(dqT, 1)), "dkT": ca(dkT), "dkTs": ca(_swap_halves(dkT, 1)),
        "dv": ca(dv.reshape(T, 8, 256)[:, ds_].transpose(1, 0, 2)),
        "ropeC": c["ropeC"], "ropeS": c["ropeS"], "maskc": c["maskc"], "ident": c["ident"],
        "triu": c["triu"], "selh": c["selh"],
        "lamv": ca(np.broadcast_to(np.concatenate([p["even_lambda_q1"][0], p["even_lambda_k1"][0],
                                                   p["even_lambda_q2"][0], p["even_lambda_k2"][0]]), (128, 512))),
        "subg": ca(np.broadcast_to(p["even_subln"][0], (128, 256))),
    }


NCMP = 127
NSA_NG = 2


def build_att_nsa(ngroups=2, nheads=8):
    nc = bass.Bass("TRN2", target_bir_lowering=False)
    dt_ = lambda n, sh: nc.dram_tensor(n, sh, F32, kind="ExternalInput").ap()
    NG = ngroups
    qT = dt_("qT", [8 * NG, 128, T]); qTs = dt_("qTs", [8 * NG, 128, T])
    kcb = dt_("kcb", [NG, 128, 32 * NCMP]); vcb = dt_("vcb", [NG, 128, 32 * NCMP])
    ksT = dt_("ksT", [NG, 128, T]); ksTs = dt_("ksTs", [NG, 128, T])
    kwT = dt_("kwT", [NG, 128, T]); kwTs = dt_("kwTs", [NG, 128, T])
    vs = dt_("vs", [NG, T, 128]); vw = dt_("vw", [NG, T, 128])
    gpre = dt_("gpre", [T, 24 * NG])
    w1k = dt_("w1k", [128, 32 * 256]); w1v = dt_("w1v", [128, 32 * 256])
    b1 = dt_("b1", [128, 4]); w2k = dt_("w2k", [128, 256]); w2v = dt_("w2v", [128, 256])
    pek = dt_("pek", [128, 32]); pev = dt_("pev", [128, 32])
    ropeC = dt_("ropeC", [128, T]); ropeS = dt_("ropeS", [128, T])
    maskcw = dt_("maskcw", [128, 8 * 512]); ident = dt_("ident", [128, 128])
    cmask = dt_("cmask", [128, T]); ovl = dt_("ovl", [128, 33])
    fconst = dt_("fconst", [128, NTT * 32]); esel = dt_("esel", [33, NTT * 128])
    o = nc.dram_tensor("o", [T, 1024 * NG], F32, kind="ExternalOutput").ap()
    with ExitStack() as es:
        k = K(nc, es)
        att = Att(k)
        pm = [k.ps(f"pm{i}", [128, 512]) for i in range(2)]
        Bpm = [Buf(f"pm{i}") for i in range(2)]
        idf = k.sb("idf", [128, 128], F32); Bidf = Buf("idf")
        idb = k.sb("idb", [128, 128], BF16); Bidb = Buf("idb")
        mk = k.sb("mk", [128, 8, 512], BF16); Bmk = Buf("mk")
        cmb = k.sb("cmb", [128, 1, T], BF16); Bcmb = Buf("cmb")
        C = k.sb("C", [128, T], F32); S = k.sb("S", [128, T], F32); Bcs = Buf("cs")
        fc = k.sb("fc", [128, NTT * 32], F32); Bfc = Buf("fc")
        eselb = k.sb("eselb", [33, NTT, 128], BF16); selT1 = k.sb("selT1", [33, T], BF16); Bsel = Buf("sel")
        vco = k.sb("vco", [128, 1, 161], BF16); Bvco = Buf("vco")
        gp = k.sb("gp", [128, NTT, 24 * NG], F32); Bgp = Buf("gp")
        b1s = k.sb("b1s", [128, 4], F32); Bb1 = Buf("b1s")
        pes = k.sb("pes", [128, 2, 32], F32); Bpe = Buf("pes")
        w2b = k.sb("w2b", [128, 2, 2, 128], BF16); Bw2 = Buf("w2b")
        k.dma("sp", idf[:], ident, writes=[Bidf])
        _cast_dma(k, idb[:], ident, 128, writes=[Bidb])
        _cast_dma(k, mk[:], maskcw.rearrange("p (m n) -> p m n", m=8), 512, writes=[Bmk])
        _cast_dma(k, cmb[:, 0, :], cmask, T, writes=[Bcmb])
        k.dma("sp", C[:], ropeC, writes=[Bcs])
        k.dma("sp", S[:], ropeS, pwrites=[Bcs])
        k.dma("sp", fc[:], fconst, writes=[Bfc])
        _cast_dma(k, eselb[:], esel.rearrange("j (t s) -> j t s", t=NTT), 128, writes=[Bsel])
        k.op("pool", lambda e: e.memset(selT1[32:33, :], 1.0), pwrites=[Bsel])
        _cast_dma(k, vco[:, 0, 128:161], ovl, 33, writes=[Bvco])
        k.dma("sp", gp[:], gpre.rearrange("(t p) c -> p t c", p=128), writes=[Bgp])
        k.op("act", lambda e: e.activation(out=gp[:], in_=gp[:], func=AF.Sigmoid), reads=[Bgp], pwrites=[Bgp])
        k.dma("sp", b1s[:], b1, writes=[Bb1])
        k.dma("sp", pes[:, 0, :], pek, writes=[Bpe])
        k.dma("sp", pes[:, 1, :], pev, pwrites=[Bpe])
        _cast_dma(k, w2b[:, 0, :, :], w2k.rearrange("p (c d) -> p c d", c=2), 128, writes=[Bw2])
        _cast_dma(k, w2b[:, 1, :, :], w2v.rearrange("p (c d) -> p c d", c=2), 128, pwrites=[Bw2])
        w1b = k.sb("w1b", [128, 32, 256], BF16); Bw1 = Buf("w1b")
        blk32 = [k.sb(f"blk32{i}", [128, 8, NCMP], F32) for i in range(2)]; Bblk32 = [Buf(f"blk32{i}") for i in range(2)]
        blkb = k.sb("blkb", [128, 32, NCMP], BF16); Bblkb = Buf("blkb")
        H1 = k.sb("H1", [128, 2, NCMP], BF16); BH1 = Buf("H1")
        kcm = k.sb("kcm", [128, 128], BF16); Bkcm = Buf("kcm")
        ksr = k.sb("ksr", [128, T], BF16); Bksr = Buf("ksr")
        kwr = k.sb("kwr", [128, T], BF16); Bkwr = Buf("kwr")
        v1s = k.sb("v1s", [128, NTT, 129], BF16); Bv1s = Buf("v1s")
        v1w = k.sb("v1w", [128, NTT, 129], BF16); Bv1w = Buf("v1w")
        qb = k.sb("qb", [128, T], BF16); Bqb = Buf("qb")
        qr = k.sb("qr", [128, T], BF16); Bqr = Buf("qr")
        xa = [k.sb(f"xa{i}", [128, 1024], F32) for i in range(2)]; Bxa = [Buf(f"xa{i}") for i in range(2)]
        xb = [k.sb(f"xb{i}", [128, 1024], F32) for i in range(2)]; Bxb = [Buf(f"xb{i}") for i in range(2)]
        acc = [k.sb(f"acc{i}", [128, NTT, 128], F32) for i in range(2)]; Bacc = [Buf(f"acc{i}") for i in range(2)]
        imp = k.sb("imp", [128, NTT, 32], F32); Bimp = Buf("imp")
        sc = k.sb("sc", [128, NTT, 32], F32); Bsc = Buf("sc")
        selm = k.sb("selm", [128, NTT, 32], F32); Bselm = Buf("selm")
        okm = k.sb("okm", [128, NTT * 32], F32); Bokm = Buf("okm")
        m8 = k.sb("m8", [128, 16], F32); Bm8 = Buf("m8")
        wk = k.sb("wk", [128, 32], F32); Bwk = Buf("wk")
        rsb = k.sb("rsb", [128, 128], F32); Brs = Buf("rsb")
        ctr = [0]; rc = [0]; bc = [0]
        cplan = [[(0, 0, 0, 3)] for _ in range(4)]
        caus = causal_plan(); wplan = window_plan()

        def compress(which, src, gi):
            _cast_dma(k, w1b[:], (w1k if which == 0 else w1v).rearrange("p (l h) -> p l h", l=32), 256, writes=[Bw1])
            for l8 in range(4):
                s = bc[0] % 2; bc[0] += 1
                k.dma("sp", blk32[s][:], src[gi][:, l8 * 8 * NCMP:(l8 + 1) * 8 * NCMP].rearrange("p (l n) -> p l n", l=8),
                      writes=[Bblk32[s]])
                for j in range(8):
                    l = l8 * 8 + j
                    k.op("dve", lambda e: e.tensor_scalar(out=blkb[:, l, :], in0=blk32[s][:, j, :],
                                                          scalar1=pes[:, which, l:l + 1], scalar2=None, op0=ALU.add),
                         reads=[Bblk32[s], Bpe], writes=[Bblkb] if l == 0 else [], pwrites=[] if l == 0 else [Bblkb])
            for hc in range(2):
                p = pm[hc]; Bp_ = Bpm[hc]
                for l in range(32):
                    k.op("pe", lambda e: e.matmul(p[:, 0:NCMP], lhsT=w1b[:, l, hc * 128:(hc + 1) * 128], rhs=blkb[:, l, :],
                                                  start=(l == 0), stop=(l == 31)),
                         reads=[Bw1, Bblkb], writes=[Bp_] if l == 0 else [], pwrites=[] if l == 0 else [Bp_])
                k.op("act", lambda e: e.activation(out=H1[:, hc, :], in_=p[:, 0:NCMP], func=AF.Silu,
                                                   bias=b1s[:, which * 2 + hc:which * 2 + hc + 1], scale=1.0),
                     reads=[Bp_, Bb1], writes=[BH1] if hc == 0 else [], pwrites=[] if hc == 0 else [BH1])
            p = pm[0]; Bp_ = Bpm[0]
            if which == 0:
                for hc in range(2):
                    k.op("pe", lambda e: e.matmul(p[:, 0:NCMP], lhsT=w2b[:, 0, hc, :], rhs=H1[:, hc, :],
                                                  start=(hc == 0), stop=(hc == 1)),
                         reads=[Bw2, BH1], writes=[Bp_] if hc == 0 else [], pwrites=[] if hc == 0 else [Bp_])
                k.op("act", lambda e: e.copy(out=kcm[:, 0:NCMP], in_=p[:, 0:NCMP]), reads=[Bp_], writes=[Bkcm])
            else:
                for hc in range(2):
                    k.op("pe", lambda e: e.matmul(p[0:NCMP, 0:128], lhsT=H1[:, hc, :], rhs=w2b[:, 1, hc, :],
                                                  start=(hc == 0), stop=(hc == 1)),
                         reads=[Bw2, BH1], writes=[Bp_] if hc == 0 else [], pwrites=[] if hc == 0 else [Bp_])
                k.op("act", lambda e: e.copy(out=vco[0:NCMP, 0, 0:128], in_=p[0:NCMP, 0:128]), reads=[Bp_], pwrites=[Bvco])

        def rs_of(po, Bpo, col, eps):
            c = rc[0] % 64; rc[0] += 1
            if eps:
                k.op("dve", lambda e: e.tensor_scalar(out=rsb[:, 64 + c:65 + c], in0=po[:, col:col + 1], scalar1=1e-30,
                                                      scalar2=None, op0=ALU.add), reads=[Bpo], pwrites=[Brs])
                k.op("dve", lambda e: e.reciprocal(out=rsb[:, c:c + 1], in_=rsb[:, 64 + c:65 + c]), reads=[Brs], pwrites=[Brs])
            else:
                k.op("dve", lambda e: e.reciprocal(out=rsb[:, c:c + 1], in_=po[:, col:col + 1]), reads=[Bpo], pwrites=[Brs])
            return c

        for gi in range(ngroups):
            if gi > 0:
                k.rotate()
            compress(0, kcb, gi)
            compress(1, vcb, gi)
            _rope(k, ksr, Bksr, ksT[gi], ksTs[gi], C, S, Bcs, xa, Bxa, xb, Bxb, ctr)
            _rope(k, kwr, Bkwr, kwT[gi], kwTs[gi], C, S, Bcs, xa, Bxa, xb, Bxb, ctr)
            _cast_dma(k, v1s[:, :, 0:128], vs[gi].rearrange("(t p) d -> p t d", p=128), 128, writes=[Bv1s])
            k.op("pool", lambda e: e.memset(v1s[:, :, 128:129], 1.0), pwrites=[Bv1s])
            _cast_dma(k, v1w[:, :, 0:128], vw[gi].rearrange("(t p) d -> p t d", p=128), 128, writes=[Bv1w])
            k.op("pool", lambda e: e.memset(v1w[:, :, 128:129], 1.0), pwrites=[Bv1w])
            for r in range(nheads):
                h = gi * 8 + r
                _cast_dma(k, qb[:], qT[h], T, writes=[Bqb])

                def cb1(g, qt, po, Bpo, r=r):
                    qi = 4 * g + qt
                    c = rs_of(po, Bpo, 0, True)
                    if r == 0:
                        k.op("dve", lambda e: e.tensor_scalar(out=imp[:, qi, :], in0=po[:, 1:33], scalar1=rsb[:, c:c + 1],
                                                              scalar2=None, op0=ALU.mult),
                             reads=[Bpo, Brs], writes=[Bimp] if qi == 0 else [], pwrites=[] if qi == 0 else [Bimp])
                    else:
                        k.op("dve", lambda e: e.scalar_tensor_tensor(out=imp[:, qi, :], in0=po[:, 1:33], scalar=rsb[:, c:c + 1],
                                                                     in1=imp[:, qi, :], op0=ALU.mult, op1=ALU.add),
                             reads=[Bpo, Brs, Bimp], pwrites=[Bimp])
                att.run(kT=kcm, BkT=Bkcm, qT=qb, BqT=Bqb, v1=vco[:, :, 128:161], Bv1=Bvco, VW=33, plan=cplan,
                        idb=idb, Bidb=Bidb, masks=cmb, Bmasks=Bcmb, out_cb=cb1, KP=NCMP, rhs_all=True)
            k.op("dve", lambda e: e.tensor_tensor(out=sc[:].rearrange("p t j -> p (t j)"),
                                                  in0=imp[:].rearrange("p t j -> p (t j)"), in1=fc[:], op=ALU.add),
                 reads=[Bimp, Bfc], writes=[Bsc])
            for i in range(NTT):
                k.op("dve", lambda e: e.max(out=m8[:, 0:8], in_=sc[:, i, :]), reads=[Bsc], writes=[Bm8])
                k.op("dve", lambda e: e.match_replace(out=wk[:], in_to_replace=m8[:, 0:8], in_values=sc[:, i, :],
                                                      imm_value=-3e9), reads=[Bsc, Bm8], writes=[Bwk])
                k.op("dve", lambda e: e.max(out=m8[:, 8:16], in_=wk[:]), reads=[Bwk], pwrites=[Bm8])
                k.op("dve", lambda e: e.tensor_scalar(out=selm[:, i, :], in0=sc[:, i, :], scalar1=m8[:, 15:16], scalar2=None,
                                                      op0=ALU.is_ge), reads=[Bsc, Bm8],
                     writes=[Bselm] if i == 0 else [], pwrites=[] if i == 0 else [Bselm])
            k.op("dve", lambda e: e.tensor_scalar(out=okm[:], in0=sc[:].rearrange("p t j -> p (t j)"), scalar1=-5e8,
                                                  scalar2=None, op0=ALU.is_gt), reads=[Bsc], writes=[Bokm])
            k.op("dve", lambda e: e.tensor_tensor(out=selm[:].rearrange("p t j -> p (t j)"),
                                                  in0=selm[:].rearrange("p t j -> p (t j)"), in1=okm[:], op=ALU.mult),
                 reads=[Bselm, Bokm], pwrites=[Bselm])
            for i4 in range(4):
                p = pm[i4 % 2]; Bp_ = Bpm[i4 % 2]
                for j in range(4):
                    i = i4 * 4 + j
                    k.op("pe", lambda e: e.transpose(out=p[0:32, j * 128:(j + 1) * 128], in_=selm[:, i, :], identity=idf[:]),
                         reads=[Bselm, Bidf], writes=[Bp_] if j == 0 else [], pwrites=[] if j == 0 else [Bp_])
                k.op("act", lambda e: e.copy(out=selT1[0:32, i4 * 512:(i4 + 1) * 512], in_=p[0:32, :]),
                     reads=[Bp_], pwrites=[Bsel])
            for r in range(nheads):
                h = gi * 8 + r
                a = acc[h % 2]; Ba = Bacc[h % 2]
                _cast_dma(k, qb[:], qT[h], T, writes=[Bqb])
                _rope(k, qr, Bqr, qT[h], qTs[h], C, S, Bcs, xa, Bxa, xb, Bxb, ctr)

                def mk_cb(br, first, eps, a=a, Ba=Ba, h=h):
                    def cb(g, qt, po, Bpo):
                        qi = 4 * g + qt
                        c = rs_of(po, Bpo, 128, eps)
                        k.op("dve", lambda e: e.tensor_tensor(out=rsb[:, 64 + c:65 + c], in0=rsb[:, c:c + 1],
                                                              in1=gp[:, qi, h * 3 + br:h * 3 + br + 1], op=ALU.mult),
                             reads=[Brs, Bgp], pwrites=[Brs])
                        if first:
                            k.op("act", lambda e: e.activation(out=a[:, qi, :], in_=po[:, 0:128], func=AF.Copy,
                                                               scale=rsb[:, 64 + c:65 + c]),
                                 reads=[Bpo, Brs], writes=[Ba] if qi == 0 else [], pwrites=[] if qi == 0 else [Ba])
                        else:
                            k.op("dve", lambda e: e.scalar_tensor_tensor(out=a[:, qi, :], in0=po[:, 0:128],
                                                                         scalar=rsb[:, 64 + c:65 + c], in1=a[:, qi, :],
                                                                         op0=ALU.mult, op1=ALU.add),
                                 reads=[Bpo, Brs, Ba], pwrites=[Ba])
                    return cb
                att.run(kT=kcm, BkT=Bkcm, qT=qb, BqT=Bqb, v1=vco[:, :, 0:129], Bv1=Bvco, VW=129, plan=cplan,
                        idb=idb, Bidb=Bidb, masks=cmb, Bmasks=Bcmb, out_cb=mk_cb(0, True, True), KP=NCMP, rhs_all=True)
                att.run(kT=ksr, BkT=Bksr, qT=qr, BqT=Bqr, v1=v1s, Bv1=Bv1s, VW=129, plan=caus,
                        idb=idb, Bidb=Bidb, masks=mk, Bmasks=Bmk, out_cb=mk_cb(1, False, False),
                        sel=(eselb, selT1, Bsel))
                att.run(kT=kwr, BkT=Bkwr, qT=qr, BqT=Bqr, v1=v1w, Bv1=Bv1w, VW=129, plan=wplan,
                        idb=idb, Bidb=Bidb, masks=mk, Bmasks=Bmk, out_cb=mk_cb(2, False, False))
                k.dma("sp", o[:, h * 128:(h + 1) * 128].rearrange("(t p) d -> p t d", p=128), a[:],
                      reads=[Ba], owner=Ba)
        k.finish(Bacc)
        print(f"[att_nsa] ins={k.nins} waits={k.nwait} sems={k.nsem}")
    return nc


def _nsa_consts():
    c = {}
    n = np.arange(128)[:, None]; t = np.arange(T)[None, :]
    cm = np.where((t >= 16 * n + 31) & (n < NCMP), 0.0, MNEG).astype(np.float32)
    c["cmask"] = np.ascontiguousarray(cm)
    cmp_start = np.arange(NCMP) * 16; sel_start = np.arange(32) * 64
    ov = ((cmp_start[:, None] < sel_start[None, :] + 64) & (cmp_start[:, None] + 32 > sel_start[None, :])).astype(np.float32)
    ovl = np.zeros((128, 33), np.float32); ovl[:, 0] = 1.0; ovl[:NCMP, 1:] = ov
    c["ovl"] = ovl
    tt = np.arange(T); blk = tt // 64; j = np.arange(32)[None, :]
    valid = j <= blk[:, None]
    forced = (j == 0) | (j == blk[:, None]) | (j == blk[:, None] - 1)
    fcst = np.where(valid, np.where(forced, 1e9, 0.0), -1e9).astype(np.float32)
    c["fconst"] = np.ascontiguousarray(fcst.reshape(NTT, 128, 32).transpose(1, 0, 2).reshape(128, NTT * 32))
    es_ = np.zeros((33, NTT, 128), np.float32)
    for kt in range(NTT):
        es_[2 * kt, kt, 0:64] = -MNEG
        es_[2 * kt + 1, kt, 64:128] = -MNEG
    es_[32, :, :] = MNEG
    c["esel"] = es_.reshape(33, NTT * 128)
    return c


def _nsa_inputs(proj_b, g0, ng, p, c, cn):
    ca = np.ascontiguousarray
    q, kc, vc, ks, vs, kw, vw, gates = np.split(proj_b, np.cumsum([4096] + [512] * 6).tolist(), axis=1)
    hs = slice(g0 * 8, (g0 + ng) * 8); gs_ = slice(g0, g0 + ng)
    qT = q.reshape(T, 32, 128)[:, hs].transpose(1, 2, 0)
    grp = lambda a: a.reshape(T, 4, 128)[:, gs_].transpose(1, 0, 2)
    idx = 16 * np.arange(NCMP)[None, :] + np.arange(32)[:, None]
    blk = lambda a: ca(grp(a)[:, idx].transpose(0, 3, 1, 2).reshape(ng, 128, 32 * NCMP))
    trT = lambda a: grp(a).transpose(0, 2, 1)
    w1l = lambda w: ca(w.reshape(32, 128, 256).transpose(1, 0, 2).reshape(128, 32 * 256))
    w2l = lambda w: ca(w.reshape(2, 128, 128).transpose(1, 0, 2).reshape(128, 256))
    b1 = np.concatenate([p["odd_cmp_k_b1"][0].reshape(2, 128).T, p["odd_cmp_v_b1"][0].reshape(2, 128).T], 1)
    return {
        "qT": ca(qT), "qTs": ca(_swap_halves(qT, 1)),
        "kcb": blk(kc), "vcb": blk(vc),
        "ksT": ca(trT(ks)), "ksTs": ca(_swap_halves(trT(ks), 1)),
        "kwT": ca(trT(kw)), "kwTs": ca(_swap_halves(trT(kw), 1)),
        "vs": ca(grp(vs)), "vw": ca(grp(vw)),
        "gpre": ca(gates[:, g0 * 24:(g0 + ng) * 24]),
        "w1k": w1l(p["odd_cmp_k_w1"][0]), "w1v": w1l(p["odd_cmp_v_w1"][0]), "b1": ca(b1),
        "w2k": w2l(p["odd_cmp_k_w2"][0]), "w2v": w2l(p["odd_cmp_v_w2"][0]),
        "pek": ca(p["odd_cmp_pe_k"][0].T), "pev": ca(p["odd_cmp_pe_v"][0].T),
        "ropeC": c["ropeC"], "ropeS": c["ropeS"], "maskcw": c["maskcw"], "ident": c["ident"],
        "cmask": cn["cmask"], "ovl": cn["ovl"], "fconst": cn["fconst"], "esel": cn["esel"],
    }


_PROGS = {}


def _prog(key, fn):
    if key not in _PROGS:
        _PROGS[key] = fn()
    return _PROGS[key]


def _launch(nc, in_maps, out_name):
    res = run_bass_kernel_spmd(nc, in_maps, core_ids=list(range(NCORES)))
    return [r[out_name] for r in res.results]


def _run_chain(key, stages, x0, feeds, c):
    nc = _prog(key, lambda: build_chain(stages))
    ims = []
    for i in range(NCORES):
        m = {"ident": c["ident"], "x0": np.ascontiguousarray(x0[i * 1024:(i + 1) * 1024])}
        for name, (arr, sharded) in feeds.items():
            m[name] = np.ascontiguousarray(arr[i * 1024:(i + 1) * 1024]) if sharded else arr
        ims.append(m)
    res = run_bass_kernel_spmd(nc, ims, core_ids=list(range(NCORES)))
    return lambda name: np.concatenate([r[name] for r in res.results], 0)


def _ffn_feeds(i, p, which, layer):
    return {f"s{i}_gT": (_gT(p[f"{which}_norm"][layer]), False),
            f"s{i}_wg": (_ffn_weight_layout(p[f"{which}_w_gate"][layer]), False),
            f"s{i}_wu": (_ffn_weight_layout(p[f"{which}_w_up"][layer]), False),
            f"s{i}_wd": (np.ascontiguousarray(p[f"{which}_w_down"][layer]), False)}


def kernel(**inp):
    p = {k_: np.asarray(v, dtype=np.float32) for k_, v in inp.items()}
    c = _consts(); cn = _nsa_consts()
    B = 4
    ones_gT = np.ones((128, DC), np.float32)
    x = p["x"].reshape(B * T, D)
    feeds = _ffn_feeds(0, p, "ffn1", 0)
    feeds.update({"s1_gT": (_gT(p["mix_norm"][0]), False), "s1_w": (np.ascontiguousarray(p["even_w_in"][0]), False)})
    out = _run_chain("A", [("ffn",), ("lin", 12304, True, False)], x, feeds, c)
    h = out("s0_y"); proj = out("s1_y")
    lam_init = 0.8 - 0.6 * math.exp(-0.3 * 0)
    nc = _prog("even", lambda: build_att_even(lam_init))
    ims = [_even_inputs(proj[(i // 2) * T:(i // 2 + 1) * T], i % 2, p, c) for i in range(NCORES)]
    outs = _launch(nc, ims, "o")
    attn = np.empty((B * T, D), np.float32)
    for i in range(NCORES):
        b, hh = i // 2, i % 2
        attn[b * T:(b + 1) * T, hh * 1024:(hh + 1) * 1024] = outs[i][:, 0:1024]
        attn[b * T:(b + 1) * T, 2048 + hh * 1024:2048 + (hh + 1) * 1024] = outs[i][:, 1024:2048]
    del proj, ims, outs, out, feeds
    feeds = {"s0_gT": (ones_gT, False), "s0_w": (np.ascontiguousarray(p["even_w_out"][0]), False), "s0_res": (h, True)}
    feeds.update(_ffn_feeds(1, p, "ffn2", 0))
    out = _run_chain("C1", [("lin", 4096, False, True), ("ffn",)], attn, feeds, c)
    h = out("s1_y")
    del feeds, out
    feeds = _ffn_feeds(0, p, "ffn1", 1)
    feeds.update({"s1_gT": (_gT(p["mix_norm"][1]), False), "s1_w": (np.ascontiguousarray(p["odd_w_in"][0]), False)})
    out = _run_chain("C2", [("ffn",), ("lin", 7264, True, False)], h, feeds, c)
    h = out("s0_y"); proj = out("s1_y")
    del feeds, out
    nc = _prog("nsa", lambda: build_att_nsa(ngroups=NSA_NG))
    for L in range(2 // NSA_NG):
        ims = [_nsa_inputs(proj[(i // 2) * T:(i // 2 + 1) * T], (L * 2 + i % 2) if NSA_NG == 1 else 2 * (i % 2),
                           NSA_NG, p, c, cn) for i in range(NCORES)]
        outs = _launch(nc, ims, "o")
        for i in range(NCORES):
            b = i // 2
            g0 = (L * 2 + i % 2) if NSA_NG == 1 else 2 * (i % 2)
            attn[b * T:(b + 1) * T, g0 * 1024:(g0 + NSA_NG) * 1024] = outs[i]
    del proj, ims, outs
    feeds = {"s0_gT": (ones_gT, False), "s0_w": (np.ascontiguousarray(p["odd_w_out"][0]), False), "s0_res": (h, True)}
    feeds.update(_ffn_feeds(1, p, "ffn2", 1))
    feeds.update({"s2_gR": (np.ascontiguousarray(np.broadcast_to(p["final_norm"], (128, D))), False)})
    out = _run_chain("E", [("lin", 4096, False, True), ("ffn",), ("fnorm",)], attn, feeds, c)
    return out("s2_y").reshape(B, T, D).astype(np.float32)
```
